# Optimizing a Trainium2 kernel written in Bass

```python
import math
import jax
import jax.numpy as jnp
from jax import lax
import numpy as np

D_MODEL = 1024
BATCH = 4
SEQ = 4096
DEPTH = 2

MEM_LEN = 256
HEAD_DIM = 64
MIX_WIDTH = 2 * D_MODEL
SSD_WIDTH = MIX_WIDTH // 2
RWKV_WIDTH = MIX_WIDTH - SSD_WIDTH
SSD_HEADS = SSD_WIDTH // HEAD_DIM
SSD_GROUPS = 2
SSD_STATE = 128
SSD_CONV = 4
SSD_CHUNK = 128
SSD_XBC = SSD_WIDTH + 2 * SSD_GROUPS * SSD_STATE
SSD_IN = SSD_WIDTH + SSD_XBC + SSD_HEADS
RWKV_HEADS = RWKV_WIDTH // HEAD_DIM
RWKV_DECAY_LORA = 64
RWKV_ICLR_LORA = 64
RWKV_GATE_LORA = 128
RWKV_IN = 3 * RWKV_WIDTH + RWKV_DECAY_LORA + RWKV_ICLR_LORA + RWKV_GATE_LORA
EVEN_IN = SSD_IN + RWKV_IN
MOBA_HEADS = D_MODEL // HEAD_DIM
MOBA_BLOCK = 256
MOBA_TOPK = 3
MOBA_QBLOCK = 128
XATTN_HEADS = 4
XATTN_HEAD_DIM = D_MODEL // XATTN_HEADS
FFN_RAW = -(-8 * D_MODEL // 3)
FFN_HIDDEN = -(-FFN_RAW // 256) * 256
N_EVEN = (DEPTH + 1) // 2
N_ODD = DEPTH // 2
DEEPNORM_ALPHA = (2 * DEPTH) ** 0.25
DEEPNORM_BETA = (8 * DEPTH) ** -0.25
LN_EPS = 1e-5
RMS_EPS = 1e-5
RWKV_LNX_EPS = 64e-5

kernel_name = "hybrid_ssd_rwkv7_moba_deepnorm"

F32 = jnp.float32


def _layer_norm(x, g, b):
    xf = x.astype(F32)
    mu = jnp.mean(xf, axis=-1, keepdims=True)
    var = jnp.mean(jnp.square(xf - mu), axis=-1, keepdims=True)
    return ((xf - mu) * lax.rsqrt(var + LN_EPS) * g + b).astype(x.dtype)


def _token_shift(s):
    return jnp.pad(s, ((0, 0), (1, 0), (0, 0)))[:, :-1]


def _causal_dwconv(u, w, b):
    k = w.shape[0]
    out = lax.conv_general_dilated(
        u, w.astype(u.dtype)[:, None, :], window_strides=(1,), padding=[(k - 1, 0)],
        dimension_numbers=("NWC", "WIO", "NWC"), feature_group_count=u.shape[-1])
    return out + b


def _ssd_chunked(xs, dt, a_neg, bm, cm):
    bsz, l, h, p = xs.shape
    g, n = bm.shape[2], bm.shape[3]
    rep = h // g
    nc = l // SSD_CHUNK
    bh = jnp.repeat(bm, rep, axis=2).reshape(bsz, nc, SSD_CHUNK, h, n)
    ch = jnp.repeat(cm, rep, axis=2).reshape(bsz, nc, SSD_CHUNK, h, n)
    xdt = (xs.astype(F32) * dt[..., None]).reshape(bsz, nc, SSD_CHUNK, h, p)
    a = (dt * a_neg).reshape(bsz, nc, SSD_CHUNK, h).transpose(0, 3, 1, 2)
    a_cum = jnp.cumsum(a, axis=-1)
    causal = jnp.tril(jnp.ones((SSD_CHUNK, SSD_CHUNK), dtype=bool))
    seg = a_cum[..., :, None] - a_cum[..., None, :]
    lmat = jnp.exp(jnp.where(causal, seg, -jnp.inf))
    cb = jnp.einsum("bclhn,bcshn->bhcls", ch, bh) * lmat
    y_diag = jnp.einsum("bhcls,bcshp->bclhp", cb, xdt)
    decay_states = jnp.exp(a_cum[..., -1:] - a_cum)
    states = jnp.einsum("bcqhn,bhcq,bcqhp->bchpn", bh, decay_states, xdt)
    chunk_decay = jnp.exp(a_cum[..., -1])

    def step(carry, inp):
        st, dec = inp
        return carry * dec[..., None, None] + st, carry

    init = jnp.zeros((bsz, h, p, n), F32)
    _, prev = lax.scan(step, init, (states.transpose(1, 0, 2, 3, 4), chunk_decay.transpose(2, 0, 1)))
    prev = prev.transpose(1, 0, 2, 3, 4)
    y_off = jnp.einsum("bclhn,bchpn,bhcl->bclhp", ch, prev, jnp.exp(a_cum))
    return (y_diag + y_off).reshape(bsz, l, h, p)


def _ssd_mixer(z, xbc, dt_raw, conv_w, conv_b, dt_bias, a_log, d_skip, norm_g):
    bsz, l, _ = z.shape
    xbc = jax.nn.silu(_causal_dwconv(xbc, conv_w, conv_b))
    xs, bm, cm = jnp.split(xbc, [SSD_WIDTH, SSD_WIDTH + SSD_GROUPS * SSD_STATE], axis=-1)
    xs = xs.reshape(bsz, l, SSD_HEADS, HEAD_DIM)
    bm = bm.reshape(bsz, l, SSD_GROUPS, SSD_STATE)
    cm = cm.reshape(bsz, l, SSD_GROUPS, SSD_STATE)
    dt = jax.nn.softplus((dt_raw + dt_bias).astype(F32))
    a_neg = -jnp.exp(a_log.astype(F32))
    y = _ssd_chunked(xs, dt, a_neg, bm, cm) + d_skip[:, None] * xs
    y = y.reshape(bsz, l, SSD_WIDTH) * jax.nn.silu(z)
    yg = y.astype(F32).reshape(bsz, l, SSD_GROUPS, SSD_WIDTH // SSD_GROUPS)
    yg = yg * lax.rsqrt(jnp.mean(jnp.square(yg), axis=-1, keepdims=True) + RMS_EPS)
    return (yg.reshape(bsz, l, SSD_WIDTH) * norm_g).astype(z.dtype)


def _rwkv7_scan(r, w, k, v, a, b):
    bsz, l, h, n = r.shape
    seq = tuple(t.transpose(1, 0, 2, 3) for t in (r, w, k, v, a, b))

    def step(s, inp):
        rt, wt, kt, vt, at, bt = inp
        sa = jnp.einsum("bhij,bhj->bhi", s, at)
        s = s * wt[:, :, None, :] + sa[..., None] * bt[:, :, None, :] + vt[..., None] * kt[:, :, None, :]
        return s, jnp.einsum("bhij,bhj->bhi", s, rt)

    _, y = lax.scan(step, jnp.zeros((bsz, h, n, n), F32), seq)
    return y.transpose(1, 0, 2, 3)


def _rwkv7_mixer(s, mu, w0, w2, a0, a2, g2, k_k, k_a, r_k, lnx_g, lnx_b):
    bsz, l, _ = s.shape
    s = s + (_token_shift(s) - s) * mu
    o1, o2, o3 = RWKV_WIDTH, 2 * RWKV_WIDTH, 3 * RWKV_WIDTH
    r, k, v, w_lo, a_lo, g_lo = jnp.split(
        s, [o1, o2, o3, o3 + RWKV_DECAY_LORA, o3 + RWKV_DECAY_LORA + RWKV_ICLR_LORA], axis=-1)
    w = -jax.nn.softplus(-(w0 + jnp.tanh(w_lo) @ w2)) - 0.5
    decay = jnp.exp(-jnp.exp(w.astype(F32)))
    a = jax.nn.sigmoid(a0 + a_lo @ a2)
    g = jax.nn.sigmoid(g_lo) @ g2
    hs = (bsz, l, RWKV_HEADS, HEAD_DIM)
    kk = (k * k_k).astype(F32).reshape(hs)
    kk = kk * lax.rsqrt(jnp.maximum(jnp.sum(kk * kk, axis=-1, keepdims=True), 1e-24))
    k = k * (1 + (a - 1) * k_a)
    rh, kh, vh, ah = [t.astype(F32).reshape(hs) for t in (r, k, v, a)]
    y = _rwkv7_scan(rh, decay.reshape(hs), kh, vh, -kk, kk * ah)
    ym = jnp.mean(y, axis=-1, keepdims=True)
    yv = jnp.mean(jnp.square(y - ym), axis=-1, keepdims=True)
    y = ((y - ym) * lax.rsqrt(yv + RWKV_LNX_EPS)).reshape(bsz, l, RWKV_WIDTH) * lnx_g + lnx_b
    bonus = jnp.sum(rh * kh * r_k, axis=-1, keepdims=True) * vh
    y = (y + bonus.reshape(bsz, l, RWKV_WIDTH)) * g
    return y.astype(s.dtype)


def _moba_attention(q, k, v):
    bsz, s, h, d = q.shape
    nb = -(-s // MOBA_BLOCK)
    pad = nb * MOBA_BLOCK - s
    q = q.transpose(0, 2, 1, 3)
    k = jnp.pad(k.transpose(0, 2, 1, 3), ((0, 0), (0, 0), (0, pad), (0, 0)))
    v = jnp.pad(v.transpose(0, 2, 1, 3), ((0, 0), (0, 0), (0, pad), (0, 0)))
    kblk = k.reshape(bsz, h, nb, MOBA_BLOCK, d)
    vblk = v.reshape(bsz, h, nb, MOBA_BLOCK, d)
    kmean = jnp.mean(kblk.astype(F32), axis=3)
    topk = min(MOBA_TOPK, nb)
    nqb = s // MOBA_QBLOCK
    qb_all = q.reshape(bsz, h, nqb, MOBA_QBLOCK, d).transpose(2, 0, 1, 3, 4)
    scale = d ** -0.5
    bi = jnp.arange(bsz)[:, None, None, None]
    hi = jnp.arange(h)[None, :, None, None]
    blk_ids = jnp.arange(nb)

    def attend(inp):
        c, qb = inp
        q_pos = c * MOBA_QBLOCK + jnp.arange(MOBA_QBLOCK)
        own = (c * MOBA_QBLOCK) // MOBA_BLOCK
        gate = jnp.einsum("bhqd,bhnd->bhqn", qb.astype(F32), kmean)
        gate = jnp.where(blk_ids < own, gate, -jnp.inf)
        _, idx = lax.top_k(gate, topk)
        sel_ok = idx < own
        kg = kblk[bi, hi, idx]
        vg = vblk[bi, hi, idx]
        s_sel = jnp.einsum("bhqd,bhqjtd->bhqjt", qb, kg).astype(F32) * scale
        s_sel = jnp.where(sel_ok[..., None], s_sel, -jnp.inf).reshape(bsz, h, MOBA_QBLOCK, topk * MOBA_BLOCK)
        k_own = lax.dynamic_slice_in_dim(k, own * MOBA_BLOCK, MOBA_BLOCK, axis=2)
        v_own = lax.dynamic_slice_in_dim(v, own * MOBA_BLOCK, MOBA_BLOCK, axis=2)
        k_pos = own * MOBA_BLOCK + jnp.arange(MOBA_BLOCK)
        s_own = jnp.einsum("bhqd,bhtd->bhqt", qb, k_own).astype(F32) * scale
        s_own = jnp.where(k_pos[None, :] <= q_pos[:, None], s_own, -jnp.inf)
        p = jax.nn.softmax(jnp.concatenate([s_sel, s_own], axis=-1), axis=-1)
        p_sel = p[..., :topk * MOBA_BLOCK].reshape(bsz, h, MOBA_QBLOCK, topk, MOBA_BLOCK).astype(v.dtype)
        p_own = p[..., topk * MOBA_BLOCK:].astype(v.dtype)
        return jnp.einsum("bhqjt,bhqjtd->bhqd", p_sel, vg) + jnp.einsum("bhqt,bhtd->bhqd", p_own, v_own)

    out = lax.map(attend, (jnp.arange(nqb), qb_all))
    return out.transpose(1, 0, 3, 2, 4).reshape(bsz, s, h * d)


def _memory_cross_attention(x, mem, wq, wkv, wo):
    bsz, s, _ = x.shape
    m = mem.shape[1]
    q = (x @ wq).reshape(bsz, s, XATTN_HEADS, XATTN_HEAD_DIM)
    k, v = jnp.split(mem @ wkv, 2, axis=-1)
    k = k.reshape(bsz, m, XATTN_HEADS, XATTN_HEAD_DIM)
    v = v.reshape(bsz, m, XATTN_HEADS, XATTN_HEAD_DIM)
    sc = jnp.einsum("bshd,bmhd->bhsm", q, k).astype(F32) * XATTN_HEAD_DIM ** -0.5
    p = jax.nn.softmax(sc, axis=-1).astype(v.dtype)
    o = jnp.einsum("bhsm,bmhd->bshd", p, v).reshape(bsz, s, D_MODEL)
    return o @ wo


def _swiglu(x, w13, w2):
    gate, up = jnp.split(x @ w13, 2, axis=-1)
    return (jax.nn.silu(gate) * up) @ w2


def setup_inputs(seed: int = 0) -> dict:
    key = jax.random.key(seed)
    ks = iter(jax.random.split(key, 40))

    def nrm(shape, scale):
        return scale * jax.random.normal(next(ks), shape, F32)

    def gain(shape):
        return 1.0 + nrm(shape, 0.02)

    x = nrm((BATCH, SEQ, D_MODEL), 1.0)
    mem = nrm((BATCH, MEM_LEN, D_MODEL), 1.0)
    even_w_in = nrm((N_EVEN, D_MODEL, EVEN_IN), D_MODEL ** -0.5)
    ssd_conv_w = nrm((N_EVEN, SSD_CONV, SSD_XBC), SSD_CONV ** -0.5)
    ssd_conv_b = nrm((N_EVEN, SSD_XBC), 0.01)
    dt0 = jnp.exp(jax.random.uniform(next(ks), (N_EVEN, SSD_HEADS), F32, math.log(1e-3), math.log(1e-1)))
    ssd_dt_bias = dt0 + jnp.log(-jnp.expm1(-dt0))
    ssd_a_log = jnp.log(jax.random.uniform(next(ks), (N_EVEN, SSD_HEADS), F32, 1.0, 16.0))
    ssd_d = 1.0 + nrm((N_EVEN, SSD_HEADS), 0.1)
    ssd_norm_g = gain((N_EVEN, SSD_WIDTH))
    rwkv_mu = jax.random.uniform(next(ks), (N_EVEN, RWKV_IN), F32, 0.0, 1.0)
    rwkv_w0 = jnp.linspace(-6.5, -1.5, RWKV_WIDTH, dtype=F32)[None, :] + nrm((N_EVEN, RWKV_WIDTH), 0.1)
    rwkv_w2 = nrm((N_EVEN, RWKV_DECAY_LORA, RWKV_WIDTH), 0.1 * RWKV_DECAY_LORA ** -0.5)
    rwkv_a0 = nrm((N_EVEN, RWKV_WIDTH), 0.1)
    rwkv_a2 = nrm((N_EVEN, RWKV_ICLR_LORA, RWKV_WIDTH), 0.1 * RWKV_ICLR_LORA ** -0.5)
    rwkv_g2 = nrm((N_EVEN, RWKV_GATE_LORA, RWKV_WIDTH), RWKV_GATE_LORA ** -0.5)
    rwkv_k_k = 0.85 + nrm((N_EVEN, RWKV_WIDTH), 0.05)
    rwkv_k_a = 1.0 + nrm((N_EVEN, RWKV_WIDTH), 0.05)
    rwkv_r_k = nrm((N_EVEN, RWKV_HEADS, HEAD_DIM), 0.1)
    rwkv_lnx_g = gain((N_EVEN, RWKV_WIDTH))
    rwkv_lnx_b = nrm((N_EVEN, RWKV_WIDTH), 0.02)
    even_w_out = nrm((N_EVEN, SSD_WIDTH + RWKV_WIDTH, D_MODEL), DEEPNORM_BETA * (SSD_WIDTH + RWKV_WIDTH) ** -0.5)
    odd_w_qkv = nrm((N_ODD, D_MODEL, 3 * D_MODEL), D_MODEL ** -0.5)
    odd_w_out = nrm((N_ODD, D_MODEL, D_MODEL), DEEPNORM_BETA * D_MODEL ** -0.5)
    ln_mix_g = gain((DEPTH, D_MODEL))
    ln_mix_b = nrm((DEPTH, D_MODEL), 0.02)
    xa_wq = nrm((DEPTH, D_MODEL, D_MODEL), D_MODEL ** -0.5)
    xa_wkv = nrm((DEPTH, D_MODEL, 2 * D_MODEL), D_MODEL ** -0.5)
    xa_wo = nrm((DEPTH, D_MODEL, D_MODEL), DEEPNORM_BETA * D_MODEL ** -0.5)
    ln_xa_g = gain((DEPTH, D_MODEL))
    ln_xa_b = nrm((DEPTH, D_MODEL), 0.02)
    ffn_w13 = nrm((DEPTH, D_MODEL, 2 * FFN_HIDDEN), D_MODEL ** -0.5)
    ffn_w2 = nrm((DEPTH, FFN_HIDDEN, D_MODEL), DEEPNORM_BETA * FFN_HIDDEN ** -0.5)
    ln_ffn_g = gain((DEPTH, D_MODEL))
    ln_ffn_b = nrm((DEPTH, D_MODEL), 0.02)
    return {
        "x": x, "mem": mem, "even_w_in": even_w_in, "ssd_conv_w": ssd_conv_w,
        "ssd_conv_b": ssd_conv_b, "ssd_dt_bias": ssd_dt_bias, "ssd_a_log": ssd_a_log,
        "ssd_d": ssd_d, "ssd_norm_g": ssd_norm_g, "rwkv_mu": rwkv_mu, "rwkv_w0": rwkv_w0,
        "rwkv_w2": rwkv_w2, "rwkv_a0": rwkv_a0, "rwkv_a2": rwkv_a2, "rwkv_g2": rwkv_g2,
        "rwkv_k_k": rwkv_k_k, "rwkv_k_a": rwkv_k_a, "rwkv_r_k": rwkv_r_k,
        "rwkv_lnx_g": rwkv_lnx_g, "rwkv_lnx_b": rwkv_lnx_b, "even_w_out": even_w_out,
        "odd_w_qkv": odd_w_qkv, "odd_w_out": odd_w_out, "ln_mix_g": ln_mix_g,
        "ln_mix_b": ln_mix_b, "xa_wq": xa_wq, "xa_wkv": xa_wkv, "xa_wo": xa_wo,
        "ln_xa_g": ln_xa_g, "ln_xa_b": ln_xa_b, "ffn_w13": ffn_w13, "ffn_w2": ffn_w2,
        "ln_ffn_g": ln_ffn_g, "ln_ffn_b": ln_ffn_b,
    }


def reference(x, mem, even_w_in, ssd_conv_w, ssd_conv_b, ssd_dt_bias, ssd_a_log, ssd_d,
              ssd_norm_g, rwkv_mu, rwkv_w0, rwkv_w2, rwkv_a0, rwkv_a2, rwkv_g2, rwkv_k_k,
              rwkv_k_a, rwkv_r_k, rwkv_lnx_g, rwkv_lnx_b, even_w_out, odd_w_qkv, odd_w_out,
              ln_mix_g, ln_mix_b, xa_wq, xa_wkv, xa_wo, ln_xa_g, ln_xa_b, ffn_w13, ffn_w2,
              ln_ffn_g, ln_ffn_b):
    bsz, s, _ = x.shape
    for layer in range(DEPTH):
        j = layer // 2
        if layer % 2 == 0:
            proj = x @ even_w_in[j]
            z, xbc, dt_raw, rw = jnp.split(proj, [SSD_WIDTH, SSD_WIDTH + SSD_XBC, SSD_IN], axis=-1)
            y_ssd = _ssd_mixer(z, xbc, dt_raw, ssd_conv_w[j], ssd_conv_b[j], ssd_dt_bias[j],
                               ssd_a_log[j], ssd_d[j], ssd_norm_g[j])
            y_rwkv = _rwkv7_mixer(rw, rwkv_mu[j], rwkv_w0[j], rwkv_w2[j], rwkv_a0[j], rwkv_a2[j],
                                  rwkv_g2[j], rwkv_k_k[j], rwkv_k_a[j], rwkv_r_k[j],
                                  rwkv_lnx_g[j], rwkv_lnx_b[j])
            mix = jnp.concatenate([y_ssd, y_rwkv], axis=-1) @ even_w_out[j]
        else:
            q, k, v = jnp.split(x @ odd_w_qkv[j], 3, axis=-1)
            hs = (bsz, s, MOBA_HEADS, HEAD_DIM)
            mix = _moba_attention(q.reshape(hs), k.reshape(hs), v.reshape(hs)) @ odd_w_out[j]
        x = _layer_norm(DEEPNORM_ALPHA * x + mix, ln_mix_g[layer], ln_mix_b[layer])
        xa = _memory_cross_attention(x, mem, xa_wq[layer], xa_wkv[layer], xa_wo[layer])
        x = _layer_norm(DEEPNORM_ALPHA * x + xa, ln_xa_g[layer], ln_xa_b[layer])
        ff = _swiglu(x, ffn_w13[layer], ffn_w2[layer])
        x = _layer_norm(DEEPNORM_ALPHA * x + ff, ln_ffn_g[layer], ln_ffn_b[layer])
    return x
```

```python
import contextlib
import numpy as np
import concourse.bass as bass
import concourse.mybir as mybir
from concourse.bass_utils import run_bass_kernel_spmd

F32 = mybir.dt.float32
BF16 = mybir.dt.bfloat16
AF = mybir.ActivationFunctionType
ALU = mybir.AluOpType
AX = mybir.AxisListType

ENGS = ["pe", "act", "dve", "pool", "sp"]
N_DMA_SEMS = 24


class Prog:
    def __init__(self, nc, stack):
        self.nc = nc
        self.stack = stack
        self.q = {e: [] for e in ENGS}
        self.cnt = {e: 0 for e in ENGS}
        self.known = {e: {} for e in ENGS}
        self.res = {}
        self.sems = {}
        for e in ENGS:
            self.sems[e] = stack.enter_context(nc.semaphore("s_" + e))
        self.dma_val = [0] * N_DMA_SEMS
        self.dma_pool = {"sp": list(range(0, 14)), "pool": list(range(14, 20)), "act": list(range(20, 24))}
        self.dma_rr = {"sp": 0, "pool": 0, "act": 0}
        for i in range(N_DMA_SEMS):
            self.sems[("d", i)] = stack.enter_context(nc.semaphore("s_d%d" % i))

    def _deps(self, reads, writes):
        need = {}

        def add(sv):
            if sv is None:
                return
            k, v = sv
            if need.get(k, 0) < v:
                need[k] = v
        for r in reads:
            st = self.res.get(r)
            if st is not None:
                add(st[0])
        for w in writes:
            st = self.res.get(w)
            if st is not None:
                add(st[0])
                for k, v in st[1].items():
                    add((k, v))
        return need

    def _mark(self, reads, writes, done):
        for r in reads:
            st = self.res.setdefault(r, [None, {}])
            if st[1].get(done[0], 0) < done[1]:
                st[1][done[0]] = done[1]
        for w in writes:
            self.res[w] = [done, {}]

    def _waits(self, eng, need, pe_chain=False):
        out = []
        kn = self.known[eng]
        for k, v in need.items():
            if k == eng and pe_chain:
                continue
            if kn.get(k, 0) >= v:
                continue
            kn[k] = v
            out.append((k, v))
        return out

    @staticmethod
    def _is_psum(k):
        return k == "pst" or (isinstance(k, tuple) and k[0] in ("ps", "ptm", "pfm"))

    def op(self, eng, fn, reads=(), writes=(), pe_chain=False):
        extra = [k for k in reads if self._is_psum(k) and k not in writes]
        if extra:
            writes = list(writes) + extra
        need = self._deps(reads, writes)
        waits = self._waits(eng, need, pe_chain)
        self.cnt[eng] += 1
        done = (eng, self.cnt[eng])
        self.q[eng].append((waits, fn, (eng, 1)))
        self._mark(reads, writes, done)
        return done

    def dma(self, fn, reads=(), writes=(), eng="sp"):
        need = self._deps(reads, writes)
        pool = self.dma_pool[eng]
        i = pool[self.dma_rr[eng]]
        self.dma_rr[eng] = (self.dma_rr[eng] + 1) % len(pool)
        key = ("d", i)
        if self.dma_val[i] > 0:
            if need.get(key, 0) < self.dma_val[i]:
                need[key] = self.dma_val[i]
        waits = self._waits(eng, need)
        self.dma_val[i] += 16
        done = (key, self.dma_val[i])
        self.q[eng].append((waits, fn, (key, 16)))
        self._mark(reads, writes, done)
        return done

    def barrier(self):
        state = {e: self.cnt[e] for e in ENGS if self.cnt[e] > 0}
        for i in range(N_DMA_SEMS):
            if self.dma_val[i] > 0:
                state[("d", i)] = self.dma_val[i]
        for e in ENGS:
            waits = self._waits(e, {k: v for k, v in state.items() if k != e})
            if waits:
                self.q[e].append((waits, None, None))
        self.res = {}

    def emit(self):
        nc = self.nc
        sems = self.sems
        q = self.q
        with nc.Block() as block:
            def run(eng_name):
                def body(eng):
                    for waits, fn, inc in q[eng_name]:
                        for k, v in waits:
                            eng.wait_ge(sems[k], v)
                        if fn is not None:
                            ins = fn(eng)
                            ins.then_inc(sems[inc[0]], inc[1])
                return body
            block.tensor(run("pe"))
            block.scalar(run("act"))
            block.vector(run("dve"))
            block.gpsimd(run("pool"))
            block.sync(run("sp"))


D = 1024
S = 4096
NT = 32
TG = 512
NG = 8
SSD_IN = 2576
RW_IN = 3328
EVEN_IN = 5904
FFN_H = 2816
ALPHA = 4.0 ** 0.25
SC_W = -0.6065306597126334

IN_SPECS = [
    ("x", [S, D]), ("mem", [256, D]), ("w_in", [D, EVEN_IN]), ("conv_w", [4, 1536]), ("conv_b", [1, 1536]),
    ("dt_bias", [1, 16]), ("a_log", [1, 16]), ("ssd_d", [1, 16]), ("ssd_ng", [1, D]), ("mu", [1, RW_IN]),
    ("w0", [1, D]), ("w2l", [64, D]), ("a0", [1, D]), ("a2l", [64, D]), ("g2l", [128, D]), ("k_k", [1, D]),
    ("k_a", [1, D]), ("r_k", [1, D]), ("lnx_g", [1, D]), ("lnx_b", [1, D]), ("w_eo", [2048, D]),
    ("w_qkv", [D, 3072]), ("w_oo", [D, D]), ("ln_mix_g", [2, D]), ("ln_mix_b", [2, D]),
    ("xa_wq", [2, D, D]), ("xa_wkv", [2, D, 2048]), ("xa_wo", [2, D, D]), ("ln_xa_g", [2, D]), ("ln_xa_b", [2, D]),
    ("w13", [2, D, 2 * FFN_H]), ("w2", [2, FFN_H, D]), ("ln_ffn_g", [2, D]), ("ln_ffn_b", [2, D]),
]


import os
CUT = int(os.environ.get('MK_CUT', '99'))


def build_program(upto=99, dumps=(), scan_tiles=NT, scan_parts=('ssd', 'rwkv'), n_groups=NG, skip_w=False):
    nc = bass.Bass("TRN2", target_bir_lowering=False)
    Dm = {}
    for name, shape in IN_SPECS:
        Dm[name] = nc.dram_tensor(name, shape, F32, kind="ExternalInput").ap()
    Dm["out"] = nc.dram_tensor("out", [S, D], F32, kind="ExternalOutput").ap()

    def internal(name, shape, dt):
        Dm[name] = nc.dram_tensor(name, shape, dt, kind="Internal").ap()

    internal("wb_in_a", [D, EVEN_IN], BF16)
    internal("wb_in_b", [D, RW_IN], BF16)
    internal("wb_eo", [2048, D], BF16)
    internal("wb_qkv", [D, 3072], BF16)
    internal("wb_oo", [D, D], BF16)
    internal("wb_xq", [2, D, D], BF16)
    internal("wb_xkv", [2, D, 2048], BF16)
    internal("wb_xo", [2, D, D], BF16)
    internal("wb_13", [2, D, 2 * FFN_H], BF16)
    internal("wb_2", [2, FFN_H, D], BF16)
    internal("wb_w2l", [64, D], BF16)
    internal("wb_a2l", [64, D], BF16)
    internal("wb_g2l", [128, D], BF16)
    internal("zs", [S, D], F32)
    internal("xbc", [1536, 3 + S], F32)
    internal("dts", [S, 16], F32)
    internal("rkv", [S, 3072], F32)
    internal("lw", [S, D], F32)
    internal("aa", [S, D], F32)
    internal("gg", [S, D], F32)
    internal("yT", [2048, S], BF16)
    internal("xcur", [S, D], F32)
    internal("qT", [D, S], BF16)
    internal("kT", [D, S], BF16)
    internal("vv", [S, D], BF16)
    internal("kmT", [128, 8, 16], F32)
    for nm in dumps:
        shp, dt = {"yT": ([2048, S], BF16), "xcur": ([S, D], F32), "lw": ([S, D], F32), "rkv": ([S, 3072], F32),
                   "zs": ([S, D], F32), "xbc": ([1536, 3 + S], F32), "aa": ([S, D], F32), "gg": ([S, D], F32),
                   "dts": ([S, 16], F32), "qT": ([D, S], BF16), "vv": ([S, D], BF16)}[nm]
        Dm["dump_" + nm] = nc.dram_tensor("dump_" + nm, shp, dt, kind="ExternalOutput").ap()

    with contextlib.ExitStack() as top:
        P = Prog(nc, top)

        uid = [0]

        def sb(st, name, shape, dt=F32):
            uid[0] += 1
            return st.enter_context(nc.sbuf_tensor("%s_%d" % (name, uid[0]), shape, dt))

        def psum(st, name, shape, dt=F32):
            uid[0] += 1
            return st.enter_context(nc.psum_tensor("%s_%d" % (name, uid[0]), shape, dt))

        def op(eng, fn, r=(), w=()):
            return P.op(eng, fn, reads=r, writes=w, pe_chain=(eng == "pe"))

        def dma(out, in_, r=(), w=(), eng="sp"):
            return P.dma(lambda e: e.dma_start(out=out, in_=in_), reads=r, writes=w, eng=eng)

        def dma_slow(out, in_, r=(), w=(), eng="sp"):
            return P.dma(lambda e: e.dma_start(out=out, in_=in_, allow_slow_non_contiguous=True), reads=r, writes=w, eng=eng)

        def mm(out, lhsT, rhs, start, stop, r, w):
            return op("pe", lambda e: e.matmul(out, lhsT=lhsT, rhs=rhs, start=start, stop=stop), r, w)

        def tp(out, in_, ident, r, w):
            return op("pe", lambda e: e.transpose(out=out, in_=in_, identity=ident), r, w)

        def act(out, in_, func, r, w, bias=None, scale=None, accum_out=None):
            kw = {}
            if bias is not None:
                kw["bias"] = bias
            if scale is not None:
                kw["scale"] = scale
            if accum_out is not None:
                kw["accum_out"] = accum_out
            return op("act", lambda e: e.activation(out=out, in_=in_, func=func, **kw), r, w)

        def tt(eng, out, in0, in1, alu, r, w):
            return op(eng, lambda e: e.tensor_tensor(out=out, in0=in0, in1=in1, op=alu), r, w)

        def ts(eng, out, in0, s1, s2, op0, op1, r, w):
            if op1 is None:
                return op(eng, lambda e: e.tensor_scalar(out=out, in0=in0, scalar1=s1, scalar2=None, op0=op0), r, w)
            return op(eng, lambda e: e.tensor_scalar(out=out, in0=in0, scalar1=s1, scalar2=s2, op0=op0, op1=op1), r, w)

        def stt(out, in0, scalar, in1, op0, op1, r, w):
            return op("dve", lambda e: e.scalar_tensor_tensor(out=out, in0=in0, scalar=scalar, in1=in1, op0=op0, op1=op1), r, w)

        def cp(eng, out, in_, r, w):
            if eng == "act":
                return act(out, in_, AF.Copy, r, w)
            return op(eng, lambda e: e.tensor_copy(out=out, in_=in_), r, w)

        def red(out, in_, r, w, alu=ALU.add):
            return op("dve", lambda e: e.tensor_reduce(out=out, in_=in_, axis=AX.X, op=alu), r, w)

        def bc_load(tile, vec_row_ap, w):
            return dma(tile, vec_row_ap.partition_broadcast(128), r=(), w=w)

        ident_b = sb(top, "ident_b", [128, 128], BF16)
        ident_f = sb(top, "ident_f", [128, 128], F32)
        ones_b = sb(top, "ones_b", [128, 128], BF16)
        ones_f = sb(top, "ones_f", [128, 128], F32)
        le_f = sb(top, "le_f", [128, 128], F32)
        lt_f = sb(top, "lt_f", [128, 128], F32)
        gt_f = sb(top, "gt_f", [128, 128], F32)
        op("pool", lambda e: e.memset(ones_f[:], 1.0), w=["ones_f"])
        op("pool", lambda e: e.memset(ones_b[:], 1.0), w=["ones_b"])
        op("pool", lambda e: e.memset(ident_f[:], 0.0), w=["ident_f"])
        op("pool", lambda e: e.affine_select(out=ident_f[:], in_=ident_f[:], pattern=[[-1, 128]], compare_op=ALU.not_equal,
                                             fill=1.0, base=0, channel_multiplier=1), r=["ident_f"], w=["ident_f"])
        op("pool", lambda e: e.tensor_copy(out=ident_b[:], in_=ident_f[:]), r=["ident_f"], w=["ident_b"])
        op("pool", lambda e: e.affine_select(out=le_f[:], in_=ones_f[:], pattern=[[1, 128]], compare_op=ALU.is_ge,
                                             fill=0.0, base=0, channel_multiplier=-1), r=["ones_f"], w=["le_f"])
        op("pool", lambda e: e.affine_select(out=lt_f[:], in_=ones_f[:], pattern=[[1, 128]], compare_op=ALU.is_gt,
                                             fill=0.0, base=0, channel_multiplier=-1), r=["ones_f"], w=["lt_f"])
        op("pool", lambda e: e.affine_select(out=gt_f[:], in_=ones_f[:], pattern=[[-1, 128]], compare_op=ALU.is_gt,
                                             fill=0.0, base=0, channel_multiplier=1), r=["ones_f"], w=["gt_f"])
        CONST_R = ["ident_b", "ident_f", "ones_b", "ones_f", "le_f", "lt_f", "gt_f"]

        def cast_w(dst, src, rows, name):
            for r0 in range(0, rows, 128):
                r1 = min(rows, r0 + 128)
                dma(dst[r0:r1], src[r0:r1], w=[], eng="pool")

        def stage_w():
            with contextlib.ExitStack() as st:
                cast_w(Dm["wb_in_a"][:, 0:SSD_IN], Dm["w_in"][:, 0:SSD_IN], D, "wb_in_a")
                cast_w(Dm["wb_eo"], Dm["w_eo"], 2048, "wb_eo")
                cast_w(Dm["wb_qkv"], Dm["w_qkv"], D, "wb_qkv")
                cast_w(Dm["wb_oo"], Dm["w_oo"], D, "wb_oo")
                for l in range(2):
                    cast_w(Dm["wb_xq"][l], Dm["xa_wq"][l], D, "wb_xq")
                    cast_w(Dm["wb_xkv"][l], Dm["xa_wkv"][l], D, "wb_xkv")
                    cast_w(Dm["wb_xo"][l], Dm["xa_wo"][l], D, "wb_xo")
                    cast_w(Dm["wb_13"][l], Dm["w13"][l], D, "wb_13")
                    cast_w(Dm["wb_2"][l], Dm["w2"][l], FFN_H, "wb_2")
                cast_w(Dm["wb_w2l"], Dm["w2l"], 64, "wb_w2l")
                cast_w(Dm["wb_a2l"], Dm["a2l"], 64, "wb_a2l")
                cast_w(Dm["wb_g2l"], Dm["g2l"], 128, "wb_g2l")
                mu_bc = sb(st, "mu_bc", [128, RW_IN])
                omu_bc = sb(st, "omu_bc", [128, RW_IN])
                bc_load(mu_bc[:], Dm["mu"][0], ["mu_bc"])
                ts("dve", omu_bc[:], mu_bc[:], -1.0, 1.0, ALU.mult, ALU.add, ["mu_bc"], ["omu_bc"])
                wf = [sb(st, "wf%d" % i, [128, RW_IN]) for i in range(2)]
                wa = [sb(st, "wa%d" % i, [128, RW_IN], BF16) for i in range(2)]
                wbt = [sb(st, "wbt%d" % i, [128, RW_IN], BF16) for i in range(2)]
                for kc in range(8):
                    i = kc % 2
                    dma(wf[i][:], Dm["w_in"][kc * 128:(kc + 1) * 128, SSD_IN:EVEN_IN], w=["wf%d" % i])
                    tt("dve", wbt[i][:], wf[i][:], mu_bc[:], ALU.mult, ["wf%d" % i, "mu_bc"], ["wbt%d" % i])
                    tt("pool", wa[i][:], wf[i][:], omu_bc[:], ALU.mult, ["wf%d" % i, "omu_bc"], ["wa%d" % i])
                    dma(Dm["wb_in_b"][kc * 128:(kc + 1) * 128, :], wbt[i][:], r=["wbt%d" % i], w=[])
                    dma(Dm["wb_in_a"][kc * 128:(kc + 1) * 128, SSD_IN:EVEN_IN], wa[i][:], r=["wa%d" % i], w=[])
                zt = sb(st, "zt", [128, 12, 3])
                op("dve", lambda e: e.memset(zt[:], 0.0), w=["zt"])
                dma(Dm["xbc"][:, 0:3].rearrange("(c p) t -> p c t", p=128), zt[:], r=["zt"], w=[])
                P.barrier()

        class WB:
            def __init__(self, st, n=3):
                self.t = [sb(st, "wbuf%d" % i, [128, 8, 512], BF16) for i in range(n)]
                self.i = 0

            def load(self, Wd, wname, r0, nkc, c0, ncols):
                i = self.i
                self.i = (self.i + 1) % len(self.t)
                t = self.t[i]
                dma(t[:, 0:nkc, 0:ncols],
                    Wd[r0 * 128:(r0 + nkc) * 128, c0:c0 + ncols].rearrange("(c p) f -> p c f", p=128),
                    r=[wname], w=["wbuf%d" % i])
                return t, "wbuf%d" % i

        def transpose_tile(src_bf, nblk, pst, pst_key, dst_view, dst_key, src_key, evac_eng):
            for c in range(nblk):
                tp(pst[:, c, :], src_bf[:, c * 128:(c + 1) * 128], ident_b[:], [src_key, "ident_b"], [pst_key])
            cp(evac_eng, dst_view, pst[:, 0:nblk, :], [pst_key], [dst_key])

        def layer_norm(s_ap, skey, out_ap, okey, g_bc, b_bc, gkeys, sm, smkey):
            op("dve", lambda e: e.bn_stats(out=sm[:, 0:6], in_=s_ap[:, 0:512]), [skey], [smkey])
            op("dve", lambda e: e.bn_stats(out=sm[:, 6:12], in_=s_ap[:, 512:1024]), [skey], [smkey])
            op("dve", lambda e: e.bn_aggr(out=sm[:, 12:14], in_=sm[:, 0:12]), [smkey], [smkey])
            ts("dve", sm[:, 14:15], sm[:, 13:14], 1e-5, None, ALU.add, None, [smkey], [smkey])
            act(sm[:, 14:15], sm[:, 14:15], AF.Sqrt, [smkey], [smkey])
            op("dve", lambda e: e.reciprocal(out=sm[:, 14:15], in_=sm[:, 14:15]), [smkey], [smkey])
            ts("dve", out_ap, s_ap, sm[:, 12:13], sm[:, 14:15], ALU.subtract, ALU.mult, [skey, smkey], [okey])
            tt("pool", out_ap, out_ap, g_bc, ALU.mult, [okey] + gkeys, [okey])
            tt("pool", out_ap, out_ap, b_bc, ALU.add, [okey] + gkeys, [okey])

        def stage_l0_proj():
            with contextlib.ExitStack() as st:
                wb = WB(st, 3)
                xt = [sb(st, "xt%d" % i, [128, D]) for i in range(2)]
                xb = [sb(st, "xb%d" % i, [128, D], BF16) for i in range(2)]
                xT = [sb(st, "xT%d" % i, [128, 8, 513], BF16) for i in range(2)]
                ev = [sb(st, "ev%d" % i, [128, 512]) for i in range(4)]
                lo1 = sb(st, "lo1", [128, 512], BF16)
                lo2 = sb(st, "lo2", [128, 512], BF16)
                wa2 = sb(st, "wa2", [128, D], BF16)
                g2s = sb(st, "g2s", [128, D], BF16)
                w0_bc = sb(st, "w0_bc", [128, D])
                a0_bc = sb(st, "a0_bc", [128, D])
                pst = psum(st, "pst", [128, 8, 128], BF16)
                ptm = psum(st, "ptm", [128, 4, 512], F32)
                pfm = psum(st, "pfm", [128, 2, 512], F32)
                dma(wa2[0:64, :], Dm["wb_w2l"], r=["wb_w2l"], w=["wa2"])
                dma(wa2[64:128, :], Dm["wb_a2l"], r=["wb_a2l"], w=["wa2"])
                dma(g2s[:], Dm["wb_g2l"], r=["wb_g2l"], w=["g2s"])
                bc_load(w0_bc[:], Dm["w0"][0], ["w0_bc"])
                bc_load(a0_bc[:], Dm["a0"][0], ["a0_bc"])
                evi = [0]

                def next_ev():
                    i = evi[0]
                    evi[0] = (i + 1) % 4
                    return ev[i], "ev%d" % i

                for g in range(n_groups):
                    xTg = xT[g % 2]
                    xk = "xT%d" % (g % 2)
                    if g == 0:
                        op("dve", lambda e, xTg=xTg: e.memset(xTg[:, :, 0:1], 0.0), w=[xk])
                    else:
                        cp("dve", xTg[:, :, 0:1], xT[(g - 1) % 2][:, :, 512:513], ["xT%d" % ((g - 1) % 2)], [xk])
                    for t in range(4):
                        i = t % 2
                        row0 = g * TG + t * 128
                        dma(xt[i][:], Dm["x"][row0:row0 + 128, :], r=["x"], w=["xt%d" % i])
                        cp("act", xb[i][:], xt[i][:], ["xt%d" % i], ["xb%d" % i])
                        transpose_tile(xb[i], 8, pst, "pst", xTg[:, :, 1 + t * 128:1 + (t + 1) * 128], xk, "xb%d" % i, "dve")

                    def lhs_cur(kc, t):
                        return xTg[:, kc, 1 + t * 128:1 + (t + 1) * 128]

                    def lhs_prev(kc, t):
                        return xTg[:, kc, t * 128:(t + 1) * 128]

                    for n in range(2):
                        wt, wk = wb.load(Dm["wb_in_a"], "wb_in_a", 0, 8, n * 512, 512)
                        for t in range(4):
                            for kc in range(8):
                                mm(ptm[:, t, :], lhs_cur(kc, t), wt[:, kc, :], kc == 0, kc == 7, [xk, wk], [("ptm", t)])
                            e_t, e_k = next_ev()
                            act(e_t[:], ptm[:, t, :], AF.Silu, [("ptm", t)], [e_k])
                            row0 = g * TG + t * 128
                            dma(Dm["zs"][row0:row0 + 128, n * 512:(n + 1) * 512], e_t[:], r=[e_k], w=[])
                    for fc in range(12):
                        if fc % 4 == 0:
                            wt, wk = wb.load(Dm["wb_in_a"], "wb_in_a", 0, 8, 1024 + fc * 128, 512)
                        pb = fc % 2
                        for kc in range(8):
                            mm(pfm[:, pb, :], wt[:, kc, (fc % 4) * 128:(fc % 4 + 1) * 128], xTg[:, kc, 1:513], kc == 0, kc == 7,
                               [xk, wk], [("pfm", pb)])
                        e_t, e_k = next_ev()
                        cp("dve", e_t[:], pfm[:, pb, :], [("pfm", pb)], [e_k])
                        dma(Dm["xbc"][fc * 128:(fc + 1) * 128, 3 + g * TG:3 + (g + 1) * TG], e_t[:], r=[e_k], w=[])
                    wt, wk = wb.load(Dm["wb_in_a"], "wb_in_a", 0, 8, 2560, 16)
                    for t in range(4):
                        for kc in range(8):
                            mm(ptm[:, t, 0:16], lhs_cur(kc, t), wt[:, kc, 0:16], kc == 0, kc == 7, [xk, wk], [("ptm", t)])
                        e_t, e_k = next_ev()
                        cp("dve", e_t[:, 0:16], ptm[:, t, 0:16], [("ptm", t)], [e_k])
                        row0 = g * TG + t * 128
                        dma(Dm["dts"][row0:row0 + 128, :], e_t[:, 0:16], r=[e_k], w=[])
                    for n in range(6):
                        wt_a, wk_a = wb.load(Dm["wb_in_a"], "wb_in_a", 0, 8, SSD_IN + n * 512, 512)
                        wt_b, wk_b = wb.load(Dm["wb_in_b"], "wb_in_b", 0, 8, n * 512, 512)
                        for t in range(4):
                            for kc in range(8):
                                mm(ptm[:, t, :], lhs_cur(kc, t), wt_a[:, kc, :], kc == 0, False, [xk, wk_a], [("ptm", t)])
                            for kc in range(8):
                                mm(ptm[:, t, :], lhs_prev(kc, t), wt_b[:, kc, :], False, kc == 7, [xk, wk_b], [("ptm", t)])
                            e_t, e_k = next_ev()
                            cp("act" if t % 2 else "dve", e_t[:], ptm[:, t, :], [("ptm", t)], [e_k])
                            row0 = g * TG + t * 128
                            dma(Dm["rkv"][row0:row0 + 128, n * 512:(n + 1) * 512], e_t[:], r=[e_k], w=[])
                    wt_a, wk_a = wb.load(Dm["wb_in_a"], "wb_in_a", 0, 8, SSD_IN + 3072, 256)
                    wt_b, wk_b = wb.load(Dm["wb_in_b"], "wb_in_b", 0, 8, 3072, 256)
                    for c in range(2):
                        for kc in range(8):
                            mm(pfm[:, c, :], wt_a[:, kc, c * 128:(c + 1) * 128], xTg[:, kc, 1:513], kc == 0, False, [xk, wk_a], [("pfm", c)])
                        for kc in range(8):
                            mm(pfm[:, c, :], wt_b[:, kc, c * 128:(c + 1) * 128], xTg[:, kc, 0:512], False, kc == 7, [xk, wk_b], [("pfm", c)])
                    act(lo1[0:64, :], pfm[0:64, 0, :], AF.Tanh, [("pfm", 0)], ["lo1"])
                    act(lo1[64:128, :], pfm[64:128, 0, :], AF.Copy, [("pfm", 0)], ["lo1"])
                    act(lo2[:], pfm[:, 1, :], AF.Sigmoid, [("pfm", 1)], ["lo2"])
                    for kind in range(3):
                        for n in range(2):
                            for t in range(4):
                                cs = slice(t * 128, (t + 1) * 128)
                                ns = slice(n * 512, (n + 1) * 512)
                                row0 = g * TG + t * 128
                                e_t, e_k = next_ev()
                                if kind == 0:
                                    mm(ptm[:, t, :], lo1[0:64, cs], wa2[0:64, ns], True, True, ["lo1", "wa2"], [("ptm", t)])
                                    tt("dve", e_t[:], ptm[:, t, :], w0_bc[:, ns], ALU.add, [("ptm", t), "w0_bc"], [e_k])
                                    act(e_t[:], e_t[:], AF.Sigmoid, [e_k], [e_k])
                                    dma(Dm["lw"][row0:row0 + 128, ns], e_t[:], r=[e_k], w=[])
                                elif kind == 1:
                                    mm(ptm[:, t, :], lo1[64:128, cs], wa2[64:128, ns], True, True, ["lo1", "wa2"], [("ptm", t)])
                                    tt("dve", e_t[:], ptm[:, t, :], a0_bc[:, ns], ALU.add, [("ptm", t), "a0_bc"], [e_k])
                                    act(e_t[:], e_t[:], AF.Sigmoid, [e_k], [e_k])
                                    dma(Dm["aa"][row0:row0 + 128, ns], e_t[:], r=[e_k], w=[])
                                else:
                                    mm(ptm[:, t, :], lo2[:, cs], g2s[:, ns], True, True, ["lo2", "g2s"], [("ptm", t)])
                                    cp("dve", e_t[:], ptm[:, t, :], [("ptm", t)], [e_k])
                                    dma(Dm["gg"][row0:row0 + 128, ns], e_t[:], r=[e_k], w=[])
                P.barrier()

        def stage_l0_scan():
            with contextlib.ExitStack() as st:
                Ft = [sb(st, "F%d" % i, [128, D]) for i in range(11)]
                Fk = ["F%d" % i for i in range(11)] + [("arhs", 0), ("arhs", 1)]
                Hn = ["rH", "kH", "bH", "aH", "kC", "bC", "vH", "Xb", "Ub", "Yo"]
                Ht = {n: sb(st, n, [128, D], BF16) for n in Hn}
                Ht["XDT"] = Ht["kC"]; Ht["XD"] = Ht["bC"]
                fmA = sb(st, "fmA", [64, 16, 2, 128], BF16)
                fmB = sb(st, "fmB", [64, 16, 128], BF16)
                fmK = sb(st, "fmK", [64, 16, 128], BF16)
                MrbT = sb(st, "MrbT", [128, 16, 128], BF16)
                MakT = sb(st, "MakT", [128, 16, 128], BF16)
                MrkT = sb(st, "MrkT", [128, 16, 128], BF16)
                TT = sb(st, "TT", [128, 16, 128], BF16)
                NQ = [[sb(st, "NQ%d_%d" % (a, b), [128, 4, 128], BF16) for b in range(2)] for a in range(2)]
                NQT = [[sb(st, "NQT%d_%d" % (a, b), [128, 4, 128], BF16) for b in range(2)] for a in range(2)]
                NP = [[sb(st, "NP%d_%d" % (a, b), [128, 4, 128], BF16) for b in range(2)] for a in range(2)]
                S0f = sb(st, "S0f", [64, D])
                S0b = sb(st, "S0b", [64, D], BF16)
                SPf = sb(st, "SPf", [128, D])
                SPb = sb(st, "SPb", [128, D], BF16)
                xin = sb(st, "xin", [128, 12, 131])
                xinb = sb(st, "xinb", [128, 12, 131], BF16)
                diagW = sb(st, "diagW", [128, 4, 12, 128], BF16)
                cwt = sb(st, "cwt", [128, 4, 12])
                cbrow = sb(st, "cbrow", [1, 1536], BF16)
                cbrow_f = sb(st, "cbrow_f", [1, 1536])
                BCf = sb(st, "BCf", [128, 4, 128], BF16)
                arhs = sb(st, "arhs", [128, 16, 128])
                arf = arhs[:].rearrange("p h l -> p (h l)")
                Ft.append(arf[:, 0:1024]); Ft.append(arf[:, 1024:2048])
                xs_fm = Ft[9][:].rearrange("p (c t) -> p c t", t=128)
                LTe = sb(st, "LTe", [128, 16, 128], BF16)
                MT = sb(st, "MT", [128, 16, 128], BF16)
                CBm = sb(st, "CBm", [128, 2, 128], BF16)
                Btm = sb(st, "Btm", [128, 256], BF16)
                yTo = [sb(st, "yTo%d" % i, [128, 8, 128], BF16) for i in range(2)]
                sm = sb(st, "sm", [128, 256])
                dtt = sb(st, "dtt", [128, 16])
                kk_bc = sb(st, "kk_bc", [128, D]); ka_bc = sb(st, "ka_bc", [128, D]); rk_bc = sb(st, "rk_bc", [128, D])
                lg_bc = sb(st, "lg_bc", [128, D]); lb_bc = sb(st, "lb_bc", [128, D]); ng_bc = sb(st, "ng_bc", [128, D])
                vec16 = sb(st, "vec16", [128, 64])
                PS = psum(st, "PS", [128, 8, 512], F32)

                def pk(b0, nb=1):
                    return [("ps", b) for b in range(b0, b0 + nb)]

                def psv(b0, nb, inner):
                    return PS[:, b0:b0 + nb, :].rearrange("p b (c t) -> p (b c) t", t=inner)

                def psflat(b0, nb):
                    return PS[:, b0:b0 + nb, :].rearrange("p b t -> p (b t)")

                def psbf(b0, nb):
                    return PS[:, b0:b0 + nb, :].rearrange("p b t -> p (b t)").bitcast(BF16)

                for t_, nm in ((kk_bc, "k_k"), (ka_bc, "k_a"), (rk_bc, "r_k"), (lg_bc, "lnx_g"), (lb_bc, "lnx_b"), (ng_bc, "ssd_ng")):
                    bc_load(t_[:], Dm[nm][0], ["bcs"])
                bc_load(vec16[:, 0:16], Dm["dt_bias"][0], ["vec16"])
                bc_load(vec16[:, 16:32], Dm["a_log"][0], ["vec16"])
                bc_load(vec16[:, 32:48], Dm["ssd_d"][0], ["vec16"])
                act(vec16[:, 16:32], vec16[:, 16:32], AF.Exp, ["vec16"], ["vec16"])
                ts("dve", vec16[:, 16:32], vec16[:, 16:32], -1.0, None, ALU.mult, None, ["vec16"], ["vec16"])
                dma_slow(cwt[:], Dm["conv_w"].rearrange("k (c p) -> p k c", p=128), w=["cwt"])
                dma(cbrow_f[:], Dm["conv_b"], w=["cbrow_f"])
                cp("dve", cbrow[:], cbrow_f[:], ["cbrow_f"], ["cbrow"])
                for k in range(4):
                    for c in range(12):
                        ts("dve" if (k * 12 + c) % 2 else "pool", diagW[:, k, c, :], ident_f[:], cwt[:, k, c:c + 1], None, ALU.mult, None,
                           ["cwt", "ident_f"], ["diagW"])
                op("dve", lambda e: e.memset(S0f[:], 0.0), w=["S0f"])
                op("dve", lambda e: e.memset(S0b[:], 0.0), w=["S0b"])
                op("pool", lambda e: e.memset(SPf[:], 0.0), w=["SPf"])
                op("pool", lambda e: e.memset(SPb[:], 0.0), w=["SPb"])
                R_, K_, V_, LW_, A_, G_, ZS_ = range(7)

                def v3(ap):
                    return ap.rearrange("p (h d) -> p h d", d=64)

                def bc16(ap16):
                    return ap16.unsqueeze(2).to_broadcast([128, 16, 64])

                def mat_bc(m, n):
                    return m.unsqueeze(1).to_broadcast([128, n, 128])

                for it in range(scan_tiles):
                    r0 = it * 128
                    dma(Ft[R_][:], Dm["rkv"][r0:r0 + 128, 0:1024], r=["rkv"], w=[Fk[R_]])
                    dma(Ft[K_][:], Dm["rkv"][r0:r0 + 128, 1024:2048], r=["rkv"], w=[Fk[K_]])
                    dma(Ft[V_][:], Dm["rkv"][r0:r0 + 128, 2048:3072], r=["rkv"], w=[Fk[V_]])
                    dma(Ft[LW_][:], Dm["lw"][r0:r0 + 128, :], r=["lw"], w=[Fk[LW_]])
                    dma(Ft[A_][:], Dm["aa"][r0:r0 + 128, :], r=["aa"], w=[Fk[A_]])
                    dma(Ft[G_][:], Dm["gg"][r0:r0 + 128, :], r=["gg"], w=[Fk[G_]])
                    dma(Ft[ZS_][:], Dm["zs"][r0:r0 + 128, :], r=["zs"], w=[Fk[ZS_]])
                    dma(xin[:], Dm["xbc"][:, r0:r0 + 131].rearrange("(c p) t -> p c t", p=128), r=["xbc"], w=["xin"])
                    dma(dtt[:], Dm["dts"][r0:r0 + 128, :], r=["dts"], w=["dtt"])

                    if 'ssd' not in scan_parts:
                        continue
                    XS, TY, T3 = Ft[7], Ft[8], Ft[9]
                    kXS, kTY, kT3 = Fk[7], Fk[8], Fk[9]
                    cp("pool", xinb[:], xin[:], ["xin"], ["xinb"])
                    pconv = psv(0, 3, 128)
                    for c in range(12):
                        for k in range(4):
                            mm(pconv[:, c, :], diagW[:, k, c, :], xinb[:, c, k:k + 128], k == 0, False, ["diagW", "xinb"], pk(c // 4))
                        mm(pconv[:, c, :], cbrow[0:1, c * 128:(c + 1) * 128], ones_b[0:1, :], False, True, ["cbrow", "ones_b"], pk(c // 4))
                    act(xs_fm[:], pconv[:, 0:8, :], AF.Silu, pk(0, 2), [Fk[9]])
                    act(BCf[:], pconv[:, 8:12, :], AF.Silu, pk(2), ["BCf"])
                    if CUT < 1:
                        continue
                    dtv = sm[:, 0:16]; av = sm[:, 16:32]; eac = sm[:, 32:48]; cdec = sm[:, 48:64]
                    tt("dve", dtv, dtt[:], vec16[:, 0:16], ALU.add, ["dtt", "vec16"], ["sm_dt"])
                    act(dtv, dtv, AF.Exp, ["sm_dt"], ["sm_dt"])
                    act(dtv, dtv, AF.Ln, ["sm_dt"], ["sm_dt"], bias=1.0)
                    tt("dve", av, dtv, vec16[:, 16:32], ALU.mult, ["sm_dt", "vec16"], ["sm_av"])
                    if CUT < 2:
                        continue
                    pxs = psv(3, 2, 128)
                    for c in range(8):
                        if os.environ.get("MK_V") == "A":
                            break
                        mm(pxs[:, c, :], xs_fm[:, c, :], ident_f[:], True, True, [Fk[9], "ident_f"], pk(3 + c // 4))
                    pxs_f = psflat(3, 2)
                    if os.environ.get("MK_V") != "B":
                        cp("act", XS[:], pxs_f, pk(3, 2), [kXS])
                    if os.environ.get("MK_V") != "C":
                        tt("dve", v3(Ht["XDT"][:]), v3(pxs_f), bc16(dtv), ALU.mult, pk(3, 2) + ["sm_dt"], ["kC"])
                    if CUT < 3:
                        continue
                    pbt = psbf(5, 1)[:, 0:256].rearrange("p (c t) -> p c t", t=128)
                    for c in range(2):
                        tp(pbt[:, c, :], BCf[:, c, :], ident_b[:], ["BCf", "ident_b"], pk(5))
                    cp("dve", Btm[:].rearrange("p (c t) -> p c t", t=128), pbt, pk(5), ["Btm"])
                    if CUT < 4:
                        continue
                    mm(PS[:, 6, 0:16], le_f[:], av, True, True, ["le_f", "sm_av"], pk(6))
                    mm(PS[:, 6, 16:32], ones_f[:], av, True, True, ["ones_f", "sm_av"], pk(6))
                    act(sm[:, 32:64], PS[:, 6, 0:32], AF.Exp, pk(6), ["sm_e"])
                    if CUT < 5:
                        continue
                    tt("dve", arhs[:], mat_bc(le_f[:], 16), av.unsqueeze(2).to_broadcast([128, 16, 128]), ALU.mult, ["le_f", "sm_av"], [("arhs", 0), ("arhs", 1)])
                    pseg = psv(0, 4, 128)
                    for b in range(4):
                        mm(PS[:, b, :], gt_f[:], arhs[:, 4 * b:4 * b + 4, :].rearrange("p h l -> p (h l)"), True, True, ["gt_f", ("arhs", 0), ("arhs", 1)], pk(b))
                    act(LTe[:], pseg, AF.Exp, pk(0, 4), ["LTe"])
                    if CUT < 6:
                        continue
                    pcb = PS[:, 7, 0:256].rearrange("p (g l) -> p g l", l=128)
                    for g2 in range(2):
                        mm(pcb[:, g2, :], BCf[:, g2, :], BCf[:, 2 + g2, :], True, True, ["BCf"], pk(7))
                    tt("dve", CBm[:], pcb, mat_bc(le_f[:], 2), ALU.mult, pk(7) + ["le_f"], ["CBm"])
                    tt("pool", MT[:].rearrange("p (g e) l -> p g e l", g=2), LTe[:].rearrange("p (g e) l -> p g e l", g=2),
                       CBm[:].unsqueeze(2).to_broadcast([128, 2, 8, 128]), ALU.mult, ["LTe", "CBm"], ["MT"])
                    if CUT < 7:
                        continue
                    pyd = psflat(4, 2)
                    for h in range(16):
                        mm(pyd[:, h * 64:(h + 1) * 64], MT[:, h, :], Ht["XDT"][:, h * 64:(h + 1) * 64], True, True, ["MT", "kC"], pk(4 + h // 8))
                    pyo = psflat(6, 2)
                    for g2 in range(2):
                        mm(pyo[:, g2 * 512:(g2 + 1) * 512], BCf[:, 2 + g2, :], SPb[:, g2 * 512:(g2 + 1) * 512], True, True, ["BCf", "SPb"], pk(6 + g2))
                    tt("dve", v3(TY[:]), v3(pyo), bc16(eac), ALU.mult, pk(6, 2) + ["sm_e"], [kTY])
                    tt("dve", TY[:], TY[:], pyd, ALU.add, [kTY] + pk(4, 2), [kTY])
                    tt("pool", v3(T3[:]), v3(XS[:]), bc16(vec16[:, 32:48]), ALU.mult, [kXS, "vec16"], [kT3])
                    tt("pool", TY[:], TY[:], T3[:], ALU.add, [kTY, kT3], [kTY])
                    if CUT < 8:
                        continue
                    tt("dve", v3(Ht["XD"][:]), v3(Ht["XDT"][:]), LTe[:, :, 127:128].to_broadcast([128, 16, 64]), ALU.mult, ["kC", "LTe"], ["bC"])
                    pst_ = psflat(0, 2)
                    for g2 in range(2):
                        mm(pst_[:, g2 * 512:(g2 + 1) * 512], Btm[:, g2 * 128:(g2 + 1) * 128], Ht["XD"][:, g2 * 512:(g2 + 1) * 512], True, True,
                           ["Btm", "bC"], pk(g2))
                    tt("dve", v3(SPf[:]), v3(SPf[:]), bc16(cdec), ALU.mult, ["SPf", "sm_e"], ["SPf"])
                    tt("dve", SPf[:], SPf[:], pst_, ALU.add, ["SPf"] + pk(0, 2), ["SPf"])
                    cp("act", SPb[:], SPf[:], ["SPf"], ["SPb"])
                    if CUT < 9:
                        continue
                    tt("dve", TY[:], TY[:], Ft[ZS_][:], ALU.mult, [kTY, Fk[ZS_]], [kTY])
                    for g2 in range(2):
                        act(T3[:, g2 * 512:(g2 + 1) * 512], TY[:, g2 * 512:(g2 + 1) * 512], AF.Square, [kTY], [kT3, "sm_q"], accum_out=sm[:, 64 + g2:65 + g2])
                    ts("dve", sm[:, 64:66], sm[:, 64:66], 1.0 / 512.0, 1e-5, ALU.mult, ALU.add, ["sm_q"], ["sm_q"])
                    act(sm[:, 64:66], sm[:, 64:66], AF.Sqrt, ["sm_q"], ["sm_q"])
                    op("dve", lambda e: e.reciprocal(out=sm[:, 64:66], in_=sm[:, 64:66]), ["sm_q"], ["sm_q"])
                    tt("dve", TY[:].rearrange("p (g e) -> p g e", g=2), TY[:].rearrange("p (g e) -> p g e", g=2),
                       sm[:, 64:66].unsqueeze(2).to_broadcast([128, 2, 512]), ALU.mult, [kTY, "sm_q"], [kTY])
                    tt("pool", Ht["Yo"][:], TY[:], ng_bc[:], ALU.mult, [kTY, "bcs"], ["Yo"])
                    pty = psbf(6, 1).rearrange("p (c t) -> p c t", t=128)
                    yt_ = yTo[0]
                    transpose_tile(Ht["Yo"], 8, pty, ("ps", 6), yt_[:], "yTo0", "Yo", "act")
                    dma(Dm["yT"][0:1024, r0:r0 + 128].rearrange("(c p) t -> p c t", p=128), yt_[:], r=["yTo0"], w=[])

                    if 'rwkv' not in scan_parts:
                        continue
                    KK, T1, E1, E2, E3, E4 = Ft[7], Ft[8], Ft[9], Ft[10], Ft[11], Ft[12]
                    kKK, kT1, kE1, kE2, kE3, kE4 = Fk[7], Fk[8], Fk[9], Fk[10], Fk[11], Fk[12]
                    R, Kt, V, LW, A, G = Ft[R_], Ft[K_], Ft[V_], Ft[LW_], Ft[A_], Ft[G_]
                    kR, kK, kV, kLW, kA, kG = Fk[R_], Fk[K_], Fk[V_], Fk[LW_], Fk[A_], Fk[G_]
                    tt("dve", KK[:], Kt[:], kk_bc[:], ALU.mult, [kK, "bcs"], [kKK])
                    tt("pool", T1[:], KK[:], KK[:], ALU.mult, [kKK], [kT1])
                    red(sm[:, 80:96], v3(T1[:]), [kT1], ["sm_r"])
                    ts("dve", sm[:, 80:96], sm[:, 80:96], 1e-24, None, ALU.max, None, ["sm_r"], ["sm_r"])
                    act(sm[:, 80:96], sm[:, 80:96], AF.Sqrt, ["sm_r"], ["sm_r"])
                    op("dve", lambda e: e.reciprocal(out=sm[:, 80:96], in_=sm[:, 80:96]), ["sm_r"], ["sm_r"])
                    tt("dve", v3(KK[:]), v3(KK[:]), bc16(sm[:, 80:96]), ALU.mult, [kKK, "sm_r"], [kKK])
                    stt(T1[:], A[:], -1.0, ka_bc[:], ALU.add, ALU.mult, [kA, "bcs"], [kT1])
                    stt(Kt[:], T1[:], 1.0, Kt[:], ALU.add, ALU.mult, [kT1, kK], [kK])
                    tt("pool", T1[:], KK[:], A[:], ALU.mult, [kKK, kA], [kT1])
                    for (b0, msk, mk) in ((0, le_f, "le_f"), (2, lt_f, "lt_f"), (4, gt_f, "gt_f")):
                        for n in range(2):
                            mm(PS[:, b0 + n, :], msk[:], LW[:, n * 512:(n + 1) * 512], True, True, [mk, kLW], pk(b0 + n))
                    act(E1[:], psflat(0, 2), AF.Exp, pk(0, 2), [kE1], scale=SC_W)
                    act(E2[:], psflat(0, 2), AF.Exp, pk(0, 2), [kE2], scale=-SC_W)
                    act(E3[:], psflat(2, 2), AF.Exp, pk(2, 2), [kE3], scale=SC_W)
                    act(E4[:], psflat(4, 2), AF.Exp, pk(4, 2), [kE4], scale=SC_W)
                    for h in range(16):
                        mm(PS[0:64, 6, h:h + 1], LW[:, h * 64:(h + 1) * 64], ones_f[:, 0:1], True, True, [kLW, "ones_f"], pk(6))
                    pC = sm[0:64, 96:112]
                    act(pC, PS[0:64, 6, 0:16], AF.Exp, pk(6), ["sm_pc"], scale=SC_W)
                    tt("dve", Ht["rH"][:], R[:], E1[:], ALU.mult, [kR, kE1], ["rH"])
                    tt("pool", Ht["kH"][:], Kt[:], E2[:], ALU.mult, [kK, kE2], ["kH"])
                    tt("dve", Ht["bH"][:], T1[:], E2[:], ALU.mult, [kT1, kE2], ["bH"])
                    stt(Ht["aH"][:], KK[:], -1.0, E3[:], ALU.mult, ALU.mult, [kKK, kE3], ["aH"])
                    tt("pool", Ht["kC"][:], Kt[:], E4[:], ALU.mult, [kK, kE4], ["kC"])
                    tt("pool", Ht["bC"][:], T1[:], E4[:], ALU.mult, [kT1, kE4], ["bC"])
                    cp("act", Ht["vH"][:], V[:], [kV], ["vH"])
                    for j, (src, dstv, dk, b0) in enumerate((("aH", fmA[:, :, 0, :], "fmA", 0), ("rH", fmA[:, :, 1, :], "fmA", 2),
                                                             ("bH", fmB[:], "fmB", 4), ("kH", fmK[:], "fmK", 6))):
                        pt_ = psbf(b0, 2)[0:64, :].rearrange("p (h t) -> p h t", t=128)
                        for h in range(16):
                            tp(pt_[:, h, :], Ht[src][:, h * 64:(h + 1) * 64], ident_b[:], [src, "ident_b"], pk(b0 + h // 8))
                        cp("act" if j % 2 else "dve", dstv, pt_, pk(b0, 2), [dk])
                    for g4 in range(4):
                        bb = 0 if g4 % 2 == 0 else 4
                        pB = psv(bb, 2, 256)
                        pK = psv(bb + 2, 2, 256)
                        for hl in range(4):
                            h = g4 * 4 + hl
                            rhsA = fmA[:, h, :, :].rearrange("p a t -> p (a t)")
                            mm(pB[:, hl, :], fmB[:, h, :], rhsA, True, True, ["fmA", "fmB"], pk(bb + hl // 2))
                            mm(pK[:, hl, :], fmK[:, h, :], rhsA, True, True, ["fmA", "fmK"], pk(bb + 2 + hl // 2))
                        hs = slice(g4 * 4, g4 * 4 + 4)
                        q0 = NQT[g4 % 2][0]
                        tt("dve", q0[:], pB[:, :, 0:128], mat_bc(lt_f[:], 4), ALU.mult, pk(bb, 2) + ["lt_f"], ["NQT%d_0" % (g4 % 2), ("LT", g4)])
                        tt("dve", MrbT[:, hs, :], pB[:, :, 128:256], mat_bc(le_f[:], 4), ALU.mult, pk(bb, 2) + ["le_f"], [("MrbT", g4)])
                        tt("dve", MakT[:, hs, :], pK[:, :, 0:128], mat_bc(lt_f[:], 4), ALU.mult, pk(bb + 2, 2) + ["lt_f"], [("MakT", g4)])
                        tt("dve", MrkT[:, hs, :], pK[:, :, 128:256], mat_bc(le_f[:], 4), ALU.mult, pk(bb + 2, 2) + ["le_f"], [("MrkT", g4)])
                        cp("pool", TT[:, hs, :], q0[:], ["NQT%d_0" % (g4 % 2)], [("TT", g4)])
                    for pair in range(2):
                        cur = [0, 0]
                        for gi in range(2):
                            g4 = pair * 2 + gi
                            hs = slice(g4 * 4, g4 * 4 + 4)
                            bq = gi * 3
                            pL = psv(bq, 1, 128)
                            for hl in range(4):
                                h = g4 * 4 + hl
                                mm(pL[:, hl, :], fmA[:, h, 0, :], fmB[:, h, :], True, True, ["fmA", "fmB"], pk(bq))
                            tt("dve", NQ[gi][0][:], pL, mat_bc(gt_f[:], 4), ALU.mult, pk(bq) + ["gt_f"], ["NQ%d_0" % gi])
                            cp("pool", NQT[gi][0][:], TT[:, hs, :], [("TT", g4)], ["NQT%d_0" % gi])
                            tt("pool", NP[gi][0][:], NQT[gi][0][:], mat_bc(ident_f[:], 4), ALU.add, ["NQT%d_0" % gi, "ident_f"], ["NP%d_0" % gi])
                        for lvl in range(1, 7):
                            for gi in range(2):
                                g4 = pair * 2 + gi
                                bq = gi * 3
                                c = cur[gi]
                                n_ = 1 - c
                                Qc, QTc, Pc = NQ[gi][c], NQT[gi][c], NP[gi][c]
                                Qn, QTn, Pn = NQ[gi][n_], NQT[gi][n_], NP[gi][n_]
                                kQc, kQTc, kPc = "NQ%d_%d" % (gi, c), "NQT%d_%d" % (gi, c), "NP%d_%d" % (gi, c)
                                kQn, kQTn, kPn = "NQ%d_%d" % (gi, n_), "NQT%d_%d" % (gi, n_), "NP%d_%d" % (gi, n_)
                                pQ = psv(bq, 1, 128); pQT = psv(bq + 1, 1, 128); pP = psv(bq + 2, 1, 128)
                                for hl in range(4):
                                    mm(pQ[:, hl, :], QTc[:, hl, :], Qc[:, hl, :], True, True, [kQc, kQTc], pk(bq))
                                cp("act", Qn[:], pQ, pk(bq), [kQn])
                                if lvl < 6:
                                    for hl in range(4):
                                        mm(pQT[:, hl, :], Qc[:, hl, :], QTc[:, hl, :], True, True, [kQc, kQTc], pk(bq + 1))
                                    cp("dve", QTn[:], pQT, pk(bq + 1), [kQTn])
                                for hl in range(4):
                                    mm(pP[:, hl, :], ident_b[:], Pc[:, hl, :], True, False, ["ident_b", kPc], pk(bq + 2))
                                    mm(pP[:, hl, :], Qn[:, hl, :], Pc[:, hl, :], False, True, [kQn, kPc], pk(bq + 2))
                                if lvl < 6:
                                    cp("dve", Pn[:], pP, pk(bq + 2), [kPn])
                                else:
                                    cp("dve", TT[:, g4 * 4:g4 * 4 + 4, :], pP, pk(bq + 2), [("TT", g4)])
                                cur[gi] = n_
                    TTk = [("TT", g4) for g4 in range(4)]
                    MK = lambda nm: [(nm, g4) for g4 in range(4)]
                    pX = psflat(6, 2)
                    for h in range(16):
                        hc = slice(h * 64, (h + 1) * 64)
                        mm(pX[:, hc], fmA[:, h, 0, :], S0b[:, hc], True, False, ["fmA", "S0b"], pk(6 + h // 8))
                        mm(pX[:, hc], MakT[:, h, :], Ht["vH"][:, hc], False, True, MK("MakT") + ["vH"], pk(6 + h // 8))
                    cp("act", Ht["Xb"][:], pX, pk(6, 2), ["Xb"])
                    pU = psflat(0, 2)
                    for h in range(16):
                        hc = slice(h * 64, (h + 1) * 64)
                        mm(pU[:, hc], TT[:, h, :], Ht["Xb"][:, hc], True, True, TTk + ["Xb"], pk(h // 8))
                    cp("act", Ht["Ub"][:], pU, pk(0, 2), ["Ub"])
                    pY = psflat(2, 2)
                    for h in range(16):
                        hc = slice(h * 64, (h + 1) * 64)
                        mm(pY[:, hc], fmA[:, h, 1, :], S0b[:, hc], True, False, ["fmA", "S0b"], pk(2 + h // 8))
                        mm(pY[:, hc], MrbT[:, h, :], Ht["Ub"][:, hc], False, False, MK("MrbT") + ["Ub"], pk(2 + h // 8))
                        mm(pY[:, hc], MrkT[:, h, :], Ht["vH"][:, hc], False, True, MK("MrkT") + ["vH"], pk(2 + h // 8))
                    pS = psflat(4, 2)
                    for h in range(16):
                        hc = slice(h * 64, (h + 1) * 64)
                        mm(pS[0:64, hc], Ht["bC"][:, hc], Ht["Ub"][:, hc], True, False, ["bC", "Ub"], pk(4 + h // 8))
                        mm(pS[0:64, hc], Ht["kC"][:, hc], Ht["vH"][:, hc], False, True, ["kC", "vH"], pk(4 + h // 8))
                    S3 = S0f[:].rearrange("p (h d) -> p h d", d=64)
                    tt("dve", S3, S3, pC.unsqueeze(2).to_broadcast([64, 16, 64]), ALU.mult, ["S0f", "sm_pc"], ["S0f"])
                    tt("dve", S0f[:], S0f[:], pS[0:64, :], ALU.add, ["S0f"] + pk(4, 2), ["S0f"])
                    cp("act", S0b[:], S0f[:], ["S0f"], ["S0b"])
                    Yc, T2 = E1, E2
                    kYc, kT2 = kE1, kE2
                    red(sm[:, 112:128], v3(pY), pk(2, 2), ["sm_m"])
                    ts("dve", sm[:, 112:128], sm[:, 112:128], 1.0 / 64.0, None, ALU.mult, None, ["sm_m"], ["sm_m"])
                    tt("dve", v3(Yc[:]), v3(pY), bc16(sm[:, 112:128]), ALU.subtract, pk(2, 2) + ["sm_m"], [kYc])
                    act(T2[:], Yc[:], AF.Square, [kYc], [kT2])
                    red(sm[:, 128:144], v3(T2[:]), [kT2], ["sm_v"])
                    ts("dve", sm[:, 128:144], sm[:, 128:144], 1.0 / 64.0, 64e-5, ALU.mult, ALU.add, ["sm_v"], ["sm_v"])
                    act(sm[:, 128:144], sm[:, 128:144], AF.Sqrt, ["sm_v"], ["sm_v"])
                    op("dve", lambda e: e.reciprocal(out=sm[:, 128:144], in_=sm[:, 128:144]), ["sm_v"], ["sm_v"])
                    tt("dve", v3(Yc[:]), v3(Yc[:]), bc16(sm[:, 128:144]), ALU.mult, [kYc, "sm_v"], [kYc])
                    tt("pool", Yc[:], Yc[:], lg_bc[:], ALU.mult, [kYc, "bcs"], [kYc])
                    tt("pool", Yc[:], Yc[:], lb_bc[:], ALU.add, [kYc, "bcs"], [kYc])
                    tt("pool", T2[:], R[:], Kt[:], ALU.mult, [kR, kK], [kT2])
                    tt("pool", T2[:], T2[:], rk_bc[:], ALU.mult, [kT2, "bcs"], [kT2])
                    red(sm[:, 144:160], v3(T2[:]), [kT2], ["sm_b"])
                    tt("dve", v3(T2[:]), v3(V[:]), bc16(sm[:, 144:160]), ALU.mult, [kV, "sm_b", kT2], [kT2])
                    tt("pool", Yc[:], Yc[:], T2[:], ALU.add, [kYc, kT2], [kYc])
                    tt("dve", Ht["Yo"][:], Yc[:], G[:], ALU.mult, [kYc, kG], ["Yo"])
                    pty = psbf(6, 1).rearrange("p (c t) -> p c t", t=128)
                    yt_ = yTo[1]
                    transpose_tile(Ht["Yo"], 8, pty, ("ps", 6), yt_[:], "yTo1", "Yo", "act")
                    dma(Dm["yT"][1024:2048, r0:r0 + 128].rearrange("(c p) t -> p c t", p=128), yt_[:], r=["yTo1"], w=[])
                P.barrier()


        def stage_tail(layer):
            KC = 16 if layer == 0 else 8
            Wo = Dm["wb_eo"] if layer == 0 else Dm["wb_oo"]
            Wo_name = "wb_eo" if layer == 0 else "wb_oo"
            xsrc = Dm["x"] if layer == 0 else Dm["xcur"]
            xdst = Dm["xcur"] if layer == 0 else Dm["out"]
            with contextlib.ExitStack() as st:
                wb = WB(st, 3)
                XR = [sb(st, "XR%d" % i, [128, D]) for i in range(4)]
                xb = [sb(st, "xb%d" % i, [128, D], BF16) for i in range(2)]
                yTg = sb(st, "yTg", [128, KC, 512], BF16)
                x1T = sb(st, "x1T", [128, 8, 512], BF16)
                qT = sb(st, "qTx", [128, 8, 512], BF16)
                oT = sb(st, "oTx", [128, 8, 512], BF16)
                hT = sb(st, "hT", [128, 22, 512], BF16)
                PT = [sb(st, "PTx%d" % i, [128, 512], BF16) for i in range(2)]
                rden = sb(st, "rden", [128, 512])
                sg = [sb(st, "sg%d" % i, [128, 512]) for i in range(2)]
                memT = sb(st, "memT", [128, 8, 256], BF16)
                KmT = sb(st, "KmT", [128, 8, 256], BF16)
                Vm = sb(st, "Vm", [128, 2, D], BF16)
                sm = sb(st, "smx", [128, 16])
                lnp = {}
                for nm in ("ln_mix_g", "ln_mix_b", "ln_xa_g", "ln_xa_b", "ln_ffn_g", "ln_ffn_b"):
                    lnp[nm] = sb(st, "bc_" + nm, [128, D])
                    bc_load(lnp[nm][:], Dm[nm][layer], ["lnp"])
                PS = psum(st, "PS3", [128, 8, 512], F32)

                def pk(b0, nb=1):
                    return [("ps", b) for b in range(b0, b0 + nb)]
                pst = PS[:, 6, :].bitcast(BF16).rearrange("p (c t) -> p c t", t=128)

                for mt in range(2):
                    dma(XR[mt][:], Dm["mem"][mt * 128:(mt + 1) * 128, :], r=["mem"], w=["XR%d" % mt])
                    cp("act", xb[mt][:], XR[mt][:], ["XR%d" % mt], ["xb%d" % mt])
                    transpose_tile(xb[mt], 8, pst, ("ps", 6), memT[:, :, mt * 128:(mt + 1) * 128], "memT", "xb%d" % mt, "dve")
                for fc in range(8):
                    if fc % 4 == 0:
                        wt, wk = wb.load(Dm["wb_xkv"][layer], "wb_xkv", 0, 8, fc * 128, 512)
                    for kc in range(8):
                        mm(PS[:, 4 + fc % 2, 0:256], wt[:, kc, (fc % 4) * 128:(fc % 4 + 1) * 128], memT[:, kc, :], kc == 0, kc == 7, ["memT", wk], pk(4 + fc % 2))
                    cp("dve", KmT[:, fc, :], PS[:, 4 + fc % 2, 0:256], pk(4 + fc % 2), ["KmT"])
                for n in range(2):
                    wt, wk = wb.load(Dm["wb_xkv"][layer], "wb_xkv", 0, 8, 1024 + n * 512, 512)
                    for mt in range(2):
                        for kc in range(8):
                            mm(PS[:, mt, :], memT[:, kc, mt * 128:(mt + 1) * 128], wt[:, kc, :], kc == 0, kc == 7, ["memT", wk], pk(mt))
                        cp("act", Vm[:, mt, n * 512:(n + 1) * 512], PS[:, mt, :], pk(mt), ["Vm"])

                def gemm_res_ln(lhs_fn, lhs_keys, nk, Wd, wname, gname, bname):
                    kgs = [(k0, min(8, nk - k0)) for k0 in range(0, nk, 8)]
                    for n in range(2):
                        for gi, (k0, nkc) in enumerate(kgs):
                            wt, wk = wb.load(Wd, wname, k0, nkc, n * 512, 512)
                            for t in range(4):
                                for kc in range(nkc):
                                    mm(PS[:, t, :], lhs_fn(k0 + kc, t), wt[:, kc, :], gi == 0 and kc == 0, gi == len(kgs) - 1 and kc == nkc - 1,
                                       lhs_keys + [wk], pk(t))
                        for t in range(4):
                            ns = slice(n * 512, (n + 1) * 512)
                            stt(XR[t][:, ns], XR[t][:, ns], ALPHA, PS[:, t, :], ALU.mult, ALU.add, ["XR%d" % t] + pk(t), ["XR%d" % t])
                    for t in range(4):
                        layer_norm(XR[t][:], "XR%d" % t, XR[t][:], "XR%d" % t, lnp[gname][:], lnp[bname][:], ["lnp"], sm, "smx")

                def to_fm(dst, dkey):
                    for t in range(4):
                        i = t % 2
                        cp("act", xb[i][:], XR[t][:], ["XR%d" % t], ["xb%d" % i])
                        transpose_tile(xb[i], 8, pst, ("ps", 6), dst[:, :, t * 128:(t + 1) * 128], dkey, "xb%d" % i, "dve")

                for g in range(NG):
                    c0 = g * TG
                    dma(yTg[:], Dm["yT"][0:KC * 128, c0:c0 + TG].rearrange("(c p) t -> p c t", p=128), r=["yT"], w=["yTg"])
                    for t in range(4):
                        dma(XR[t][:], xsrc[c0 + t * 128:c0 + (t + 1) * 128, :], r=["xsrc"], w=["XR%d" % t])
                    gemm_res_ln(lambda kc, t: yTg[:, kc, t * 128:(t + 1) * 128], ["yTg"], KC, Wo, Wo_name, "ln_mix_g", "ln_mix_b")
                    to_fm(x1T, "x1T")
                    for fc in range(8):
                        if fc % 4 == 0:
                            wt, wk = wb.load(Dm["wb_xq"][layer], "wb_xq", 0, 8, fc * 128, 512)
                        for kc in range(8):
                            mm(PS[:, 4 + fc % 2, :], wt[:, kc, (fc % 4) * 128:(fc % 4 + 1) * 128], x1T[:, kc, :], kc == 0, kc == 7, ["x1T", wk], pk(4 + fc % 2))
                        cp("act" if fc % 2 else "dve", qT[:, fc, :], PS[:, 4 + fc % 2, :], pk(4 + fc % 2), ["qTx"])
                    for hh in range(4):
                        for mt in range(2):
                            for dc in range(2):
                                mm(PS[:, mt, :], KmT[:, 2 * hh + dc, mt * 128:(mt + 1) * 128], qT[:, 2 * hh + dc, :], dc == 0, dc == 1, ["KmT", "qTx"], pk(mt))
                            act(PT[mt][:], PS[:, mt, :], AF.Exp, pk(mt), ["PTx%d" % mt], scale=1.0 / 16.0)
                        for mt in range(2):
                            mm(PS[:, 2, :], ones_b[:], PT[mt][:], mt == 0, mt == 1, ["ones_b", "PTx%d" % mt], pk(2))
                        for dc in range(2):
                            bo = 3 if dc == 0 else 7
                            for mt in range(2):
                                mm(PS[:, bo, :], Vm[:, mt, (2 * hh + dc) * 128:(2 * hh + dc + 1) * 128], PT[mt][:], mt == 0, mt == 1, ["Vm", "PTx%d" % mt], pk(bo))
                        op("dve", lambda e: e.reciprocal(out=rden[:], in_=PS[:, 2, :]), pk(2), ["rden"])
                        for dc in range(2):
                            bo = 3 if dc == 0 else 7
                            tt("dve", oT[:, 2 * hh + dc, :], PS[:, bo, :], rden[:], ALU.mult, pk(bo) + ["rden"], ["oTx"])
                    gemm_res_ln(lambda kc, t: oT[:, kc, t * 128:(t + 1) * 128], ["oTx"], 8, Dm["wb_xo"][layer], "wb_xo", "ln_xa_g", "ln_xa_b")
                    to_fm(x1T, "x1T")
                    for j in range(22):
                        if j % 2 == 0:
                            i = wb.i
                            wb.i = (wb.i + 1) % len(wb.t)
                            wt = wb.t[i]
                            wk = "wbuf%d" % i
                            for half, cbase in ((0, 0), (1, FFN_H)):
                                dma(wt[:, :, half * 256:half * 256 + 256],
                                    Dm["wb_13"][layer][:, cbase + j * 128:cbase + j * 128 + 256].rearrange("(c p) f -> p c f", p=128),
                                    r=["wb_13"], w=[wk])
                        jl = j % 2
                        bg = 4 + 2 * jl
                        for kc in range(8):
                            mm(PS[:, bg, :], wt[:, kc, jl * 128:(jl + 1) * 128], x1T[:, kc, :], kc == 0, kc == 7, ["x1T", wk], pk(bg))
                        for kc in range(8):
                            mm(PS[:, bg + 1, :], wt[:, kc, 256 + jl * 128:256 + (jl + 1) * 128], x1T[:, kc, :], kc == 0, kc == 7, ["x1T", wk], pk(bg + 1))
                        act(sg[jl][:], PS[:, bg, :], AF.Silu, pk(bg), ["sg%d" % jl])
                        tt("dve", hT[:, j, :], sg[jl][:], PS[:, bg + 1, :], ALU.mult, ["sg%d" % jl] + pk(bg + 1), ["hT"])
                    gemm_res_ln(lambda kc, t: hT[:, kc, t * 128:(t + 1) * 128], ["hT"], 22, Dm["wb_2"][layer], "wb_2", "ln_ffn_g", "ln_ffn_b")
                    for t in range(4):
                        dma(xdst[c0 + t * 128:c0 + (t + 1) * 128, :], XR[t][:], r=["XR%d" % t], w=[])
                P.barrier()

        def stage_l1_proj():
            with contextlib.ExitStack() as st:
                wb = WB(st, 3)
                xt = [sb(st, "xt%d" % i, [128, D]) for i in range(2)]
                xb = [sb(st, "xb%d" % i, [128, D], BF16) for i in range(2)]
                xT = [sb(st, "xT%d" % i, [128, 8, 512], BF16) for i in range(2)]
                evb = [sb(st, "evb%d" % i, [128, 512], BF16) for i in range(4)]
                kms = sb(st, "kms", [128, 8, 16])
                pst = psum(st, "pst1", [128, 8, 128], BF16)
                ptm = psum(st, "ptm1", [128, 4, 512], F32)
                pfm = psum(st, "pfm1", [128, 2, 512], F32)
                evi = [0]

                def next_ev():
                    i = evi[0]
                    evi[0] = (i + 1) % 4
                    return evb[i], "evb%d" % i
                for g in range(NG):
                    xTg = xT[g % 2]
                    xk = "xT%d" % (g % 2)
                    for t in range(4):
                        i = t % 2
                        row0 = g * TG + t * 128
                        dma(xt[i][:], Dm["xcur"][row0:row0 + 128, :], r=["xcur"], w=["xt%d" % i])
                        cp("act", xb[i][:], xt[i][:], ["xt%d" % i], ["xb%d" % i])
                        transpose_tile(xb[i], 8, pst, "pst", xTg[:, :, t * 128:(t + 1) * 128], xk, "xb%d" % i, "dve")
                    for fc in range(16):
                        if fc % 4 == 0:
                            wt, wk = wb.load(Dm["wb_qkv"], "wb_qkv", 0, 8, fc * 128, 512)
                        pb = fc % 2
                        for kc in range(8):
                            mm(pfm[:, pb, :], wt[:, kc, (fc % 4) * 128:(fc % 4 + 1) * 128], xTg[:, kc, :], kc == 0, kc == 7, [xk, wk], [("pfm", pb)])
                        e_t, e_k = next_ev()
                        cp("act" if fc % 2 else "dve", e_t[:], pfm[:, pb, :], [("pfm", pb)], [e_k])
                        if fc < 8:
                            dma(Dm["qT"][fc * 128:(fc + 1) * 128, g * TG:(g + 1) * TG], e_t[:], r=[e_k], w=[])
                        else:
                            dma(Dm["kT"][(fc - 8) * 128:(fc - 7) * 128, g * TG:(g + 1) * TG], e_t[:], r=[e_k], w=[])
                            red(kms[:, fc - 8, 2 * g:2 * g + 2], pfm[:, pb, :].rearrange("p (b t) -> p b t", t=256), [("pfm", pb)], ["kms"])
                    for n in range(2):
                        wt, wk = wb.load(Dm["wb_qkv"], "wb_qkv", 0, 8, 2048 + n * 512, 512)
                        for t in range(4):
                            for kc in range(8):
                                mm(ptm[:, t, :], xTg[:, kc, t * 128:(t + 1) * 128], wt[:, kc, :], kc == 0, kc == 7, [xk, wk], [("ptm", t)])
                            e_t, e_k = next_ev()
                            cp("act" if t % 2 else "dve", e_t[:], ptm[:, t, :], [("ptm", t)], [e_k])
                            row0 = g * TG + t * 128
                            dma(Dm["vv"][row0:row0 + 128, n * 512:(n + 1) * 512], e_t[:], r=[e_k], w=[])
                ts("dve", kms[:], kms[:], 1.0 / 256.0, None, ALU.mult, None, ["kms"], ["kms"])
                dma(Dm["kmT"], kms[:], r=["kms"], w=[])
                P.barrier()

        def stage_l1_moba():
            with contextlib.ExitStack() as st:
                KT = sb(st, "KT", [128, 8, S], BF16)
                VV = sb(st, "VV", [128, NT, D], BF16)
                qraw = sb(st, "qraw", [128, 8, 512], BF16)
                qm = sb(st, "qm", [128, 16, 512], BF16)
                kmf = sb(st, "kmf", [128, 8, 16])
                kmb = sb(st, "kmb", [128, 8, 16], BF16)
                Esel = sb(st, "Esel", [128, 16, 128], BF16)
                gsb = sb(st, "gsb", [128, 16, 16])
                t8 = sb(st, "t8", [128, 16, 8])
                thr = sb(st, "thr", [128, 16])
                sel = sb(st, "sel", [128, 16, 16])
                bpad = sb(st, "bpad", [128, 16, 128], BF16)
                biasT = sb(st, "biasT", [128, 16, 512], BF16)
                PTm = [sb(st, "PTm%d" % i, [128, 512], BF16) for i in range(3)]
                rden = sb(st, "rdenm", [64, 512])
                oTs = [sb(st, "oTs%d" % i, [64, 512], BF16) for i in range(2)]
                PS = psum(st, "PS5", [128, 8, 512], F32)

                def pk(b0, nb=1):
                    return [("ps", b) for b in range(b0, b0 + nb)]
                dma(kmf[:], Dm["kmT"], r=["kmT"], w=["kmf"])
                cp("dve", kmb[:], kmf[:], ["kmf"], ["kmb"])
                op("pool", lambda e: e.memset(Esel[:], 0.0), [], ["Esel"])
                op("pool", lambda e: e.memset(bpad[:], 0.0), [], ["bpad"])
                op("pool", lambda e: e.memset(qm[:], 0.0), [], ["qm"])
                for n in range(16):
                    ts("dve", Esel[0:16, n, :], ones_f[0:16, :], ident_f[0:16, n:n + 1], None, ALU.mult, None, ["ones_f", "ident_f", "Esel"], ["Esel"])
                MG = int(os.environ.get("MK_MG", NG))
                for G in range(MG):
                    c0 = G * TG
                    dma(qraw[:], Dm["qT"][:, c0:c0 + TG].rearrange("(c p) t -> p c t", p=128), r=["qT"], w=["qraw"])
                    dma(KT[:, :, c0:c0 + TG], Dm["kT"][:, c0:c0 + TG].rearrange("(c p) t -> p c t", p=128), r=["kT"], w=[("KT", G)])
                    dma(VV[:, 4 * G:4 * G + 4, :], Dm["vv"][c0:c0 + TG, :].rearrange("(t p) f -> p t f", p=128), r=["vv"], w=[("VV", G)])
                    KTk = [("KT", gg_) for gg_ in range(G + 1)]
                    VVk = [("VV", gg_) for gg_ in range(G + 1)]
                    qm4 = qm[:].rearrange("p (a b) t -> p a b t", b=2)
                    cp("dve", qm4[0:64, :, 0, :], qraw[0:64, :, :], ["qraw"], ["qm"])
                    cp("pool", qm4[64:128, :, 1, :], qraw[64:128, :, :], ["qraw"], ["qm"])
                    for c in range(4):
                        own = 2 * G + c // 2
                        pg = PS[:, 7, 0:256].rearrange("p (h n) -> p h n", n=16)
                        for h in range(16):
                            mm(pg[:, h, :], qm[:, h, c * 128:(c + 1) * 128], kmb[:, h // 2, :], True, True, ["qm", "kmb"], pk(7))
                        op("pool", lambda e: e.memset(gsb[:], -1e30), [], ["gsb"])
                        if own > 0:
                            cp("dve", gsb[:, :, 0:own], pg[:, :, 0:own], pk(7), ["gsb"])
                        for h in range(16):
                            op("dve", lambda e, h=h: e.max(out=t8[:, h, :], in_=gsb[:, h, :]), ["gsb"], ["t8"])
                        ts("dve", thr[:], t8[:, :, 2], -1e29, None, ALU.max, None, ["t8"], ["thr"])
                        tt("dve", sel[:], gsb[:], thr[:].unsqueeze(2).to_broadcast([128, 16, 16]), ALU.is_ge, ["gsb", "thr"], ["sel"])
                        op("dve", lambda e, own=own: e.memset(sel[:, :, own:own + 1], 1.0), [], ["sel"])
                        ts("dve", bpad[:, :, 0:16], sel[:], 30000.0, -30000.0, ALU.mult, ALU.add, ["sel"], ["bpad"])
                        for half in range(2):
                            pbt = PS[:, 5 + half, :].bitcast(BF16).rearrange("p (h t) -> p h t", t=128)
                            for hl in range(8):
                                h = half * 8 + hl
                                tp(pbt[:, hl, :], bpad[:, h, :], ident_b[:], ["bpad", "ident_b"], pk(5 + half))
                            cp("act", biasT[:, half * 8:half * 8 + 8, c * 128:(c + 1) * 128], pbt, pk(5 + half), ["biasT"])
                    nkt = 4 * G + 4
                    for h in range(16):
                        pr = h // 2
                        for kt in range(nkt):
                            j = kt - 4 * G
                            col0 = max(0, j) * 128
                            bs = kt % 2
                            pt = PTm[kt % 3]
                            ptk = "PTm%d" % (kt % 3)
                            mm(PS[:, bs, col0:512], KT[:, pr, kt * 128:(kt + 1) * 128], qm[:, h, col0:512], True, False, KTk + ["qm"], pk(bs))
                            mm(PS[:, bs, col0:512], Esel[:, kt // 2, :], biasT[:, h, col0:512], False, True, ["Esel", "biasT"], pk(bs))
                            act(pt[:, col0:512], PS[:, bs, col0:512], AF.Exp, pk(bs), [ptk], scale=0.125)
                            if j >= 0:
                                tt("pool", pt[:, col0:col0 + 128], pt[:, col0:col0 + 128], le_f[:], ALU.mult, [ptk, "le_f"], [ptk])
                            mm(PS[0:64, 2, col0:512], VV[:, kt, h * 64:(h + 1) * 64], pt[:, col0:512], kt == 0, kt == nkt - 1, VVk + [ptk], pk(2))
                            mm(PS[0:64, 3, col0:512], ones_b[:, 0:64], pt[:, col0:512], kt == 0, kt == nkt - 1, ["ones_b", ptk], pk(3))
                        op("dve", lambda e: e.reciprocal(out=rden[:], in_=PS[0:64, 3, :]), pk(3), ["rdenm"])
                        ot = oTs[h % 2]
                        otk = "oTs%d" % (h % 2)
                        tt("dve", ot[:], PS[0:64, 2, :], rden[:], ALU.mult, pk(2) + ["rdenm"], [otk])
                        dma(Dm["yT"][h * 64:(h + 1) * 64, c0:c0 + TG], ot[:], r=[otk], w=[])
                P.barrier()

        ONLY5 = os.environ.get("MK_ONLY5") == "1"
        if not ONLY5:
            stage_w()
        if upto >= 1 and not ONLY5:
            stage_l0_proj()
        if upto >= 2 and not ONLY5:
            stage_l0_scan()
        if upto >= 3 and not ONLY5:
            stage_tail(0)
        if upto >= 4 and not ONLY5:
            stage_l1_proj()
        if upto >= 5:
            stage_l1_moba()
        if upto >= 6:
            stage_tail(1)
        for nm in dumps:
            dma(Dm["dump_" + nm], Dm[nm], r=[], w=[])
        P.barrier()
        P.emit()
    return nc


def make_in_maps(inputs, n_cores=8):
    def a(x):
        return np.ascontiguousarray(np.asarray(x, dtype=np.float32))
    i = inputs
    shared = {
        "w_in": a(i["even_w_in"][0]), "conv_w": a(i["ssd_conv_w"][0]), "conv_b": a(i["ssd_conv_b"]).reshape(1, 1536),
        "dt_bias": a(i["ssd_dt_bias"]).reshape(1, 16), "a_log": a(i["ssd_a_log"]).reshape(1, 16), "ssd_d": a(i["ssd_d"]).reshape(1, 16),
        "ssd_ng": a(i["ssd_norm_g"]).reshape(1, D), "mu": a(i["rwkv_mu"]).reshape(1, RW_IN), "w0": a(i["rwkv_w0"]).reshape(1, D),
        "w2l": a(i["rwkv_w2"][0]), "a0": a(i["rwkv_a0"]).reshape(1, D), "a2l": a(i["rwkv_a2"][0]), "g2l": a(i["rwkv_g2"][0]),
        "k_k": a(i["rwkv_k_k"]).reshape(1, D), "k_a": a(i["rwkv_k_a"]).reshape(1, D), "r_k": a(i["rwkv_r_k"]).reshape(1, D),
        "lnx_g": a(i["rwkv_lnx_g"]).reshape(1, D), "lnx_b": a(i["rwkv_lnx_b"]).reshape(1, D), "w_eo": a(i["even_w_out"][0]),
        "w_qkv": a(i["odd_w_qkv"][0]), "w_oo": a(i["odd_w_out"][0]), "ln_mix_g": a(i["ln_mix_g"]), "ln_mix_b": a(i["ln_mix_b"]),
        "xa_wq": a(i["xa_wq"]), "xa_wkv": a(i["xa_wkv"]), "xa_wo": a(i["xa_wo"]), "ln_xa_g": a(i["ln_xa_g"]), "ln_xa_b": a(i["ln_xa_b"]),
        "w13": a(i["ffn_w13"]), "w2": a(i["ffn_w2"]), "ln_ffn_g": a(i["ln_ffn_g"]), "ln_ffn_b": a(i["ln_ffn_b"]),
    }
    maps = []
    for c in range(n_cores):
        b = c % 4
        m = dict(shared)
        m["x"] = a(i["x"][b])
        m["mem"] = a(i["mem"][b])
        maps.append(m)
    return maps


def kernel(**inputs):
    nc = build_program()
    in_maps = make_in_maps(inputs, 8)
    res = run_bass_kernel_spmd(nc, in_maps, core_ids=list(range(8)))
    out = np.stack([np.asarray(res.results[b]["out"], dtype=np.float32) for b in range(4)], axis=0)
    return out
```

```python
import contextlib
import numpy as np
import concourse.bass as bass
import concourse.mybir as mybir
from concourse.bass_utils import run_bass_kernel_spmd

F32 = mybir.dt.float32
BF16 = mybir.dt.bfloat16
AF = mybir.ActivationFunctionType
ALU = mybir.AluOpType
AX = mybir.AxisListType

ENGS = ["pe", "act", "dve", "pool", "sp"]
N_DMA_SEMS = 24


class Prog:
    def __init__(self, nc, stack):
        self.nc = nc
        self.stack = stack
        self.q = {e: [] for e in ENGS}
        self.cnt = {e: 0 for e in ENGS}
        self.known = {e: {} for e in ENGS}
        self.res = {}
        self.sems = {}
        for e in ENGS:
            self.sems[e] = stack.enter_context(nc.semaphore("s_" + e))
        self.dma_val = [0] * N_DMA_SEMS
        self.dma_pool = {"sp": list(range(0, 14)), "pool": list(range(14, 20)), "act": list(range(20, 24))}
        self.dma_rr = {"sp": 0, "pool": 0, "act": 0}
        for i in range(N_DMA_SEMS):
            self.sems[("d", i)] = stack.enter_context(nc.semaphore("s_d%d" % i))

    def _deps(self, reads, writes):
        need = {}

        def add(sv):
            if sv is None:
                return
            k, v = sv
            if need.get(k, 0) < v:
                need[k] = v
        for r in reads:
            st = self.res.get(r)
            if st is not None:
                add(st[0])
        for w in writes:
            st = self.res.get(w)
            if st is not None:
                add(st[0])
                for k, v in st[1].items():
                    add((k, v))
        return need

    def _mark(self, reads, writes, done):
        for r in reads:
            st = self.res.setdefault(r, [None, {}])
            if st[1].get(done[0], 0) < done[1]:
                st[1][done[0]] = done[1]
        for w in writes:
            self.res[w] = [done, {}]

    def _waits(self, eng, need, pe_chain=False):
        out = []
        kn = self.known[eng]
        for k, v in need.items():
            if k == eng and pe_chain:
                continue
            if kn.get(k, 0) >= v:
                continue
            kn[k] = v
            out.append((k, v))
        return out

    @staticmethod
    def _is_psum(k):
        return k == "pst" or (isinstance(k, tuple) and k[0] in ("ps", "ptm", "pfm"))

    def op(self, eng, fn, reads=(), writes=(), pe_chain=False):
        extra = [k for k in reads if self._is_psum(k) and k not in writes]
        if extra:
            writes = list(writes) + extra
        need = self._deps(reads, writes)
        waits = self._waits(eng, need, pe_chain)
        self.cnt[eng] += 1
        done = (eng, self.cnt[eng])
        self.q[eng].append((waits, fn, (eng, 1)))
        self._mark(reads, writes, done)
        return done

    def dma(self, fn, reads=(), writes=(), eng="sp"):
        need = self._deps(reads, writes)
        pool = self.dma_pool[eng]
        i = pool[self.dma_rr[eng]]
        self.dma_rr[eng] = (self.dma_rr[eng] + 1) % len(pool)
        key = ("d", i)
        if self.dma_val[i] > 0:
            if need.get(key, 0) < self.dma_val[i]:
                need[key] = self.dma_val[i]
        waits = self._waits(eng, need)
        self.dma_val[i] += 16
        done = (key, self.dma_val[i])
        self.q[eng].append((waits, fn, (key, 16)))
        self._mark(reads, writes, done)
        return done

    def barrier(self):
        state = {e: self.cnt[e] for e in ENGS if self.cnt[e] > 0}
        for i in range(N_DMA_SEMS):
            if self.dma_val[i] > 0:
                state[("d", i)] = self.dma_val[i]
        for e in ENGS:
            waits = self._waits(e, {k: v for k, v in state.items() if k != e})
            if waits:
                self.q[e].append((waits, None, None))
        self.res = {}

    def emit(self):
        nc = self.nc
        sems = self.sems
        q = self.q
        with nc.Block() as block:
            def run(eng_name):
                def body(eng):
                    for waits, fn, inc in q[eng_name]:
                        for k, v in waits:
                            eng.wait_ge(sems[k], v)
                        if fn is not None:
                            ins = fn(eng)
                            ins.then_inc(sems[inc[0]], inc[1])
                return body
            block.tensor(run("pe"))
            block.scalar(run("act"))
            block.vector(run("dve"))
            block.gpsimd(run("pool"))
            block.sync(run("sp"))


D = 1024
S = 4096
NT = 32
TG = 512
NG = 8
SSD_IN = 2576
RW_IN = 3328
EVEN_IN = 5904
FFN_H = 2816
ALPHA = 4.0 ** 0.25
SC_W = -0.6065306597126334

IN_SPECS = [
    ("x", [S, D]), ("mem", [256, D]), ("w_in", [D, EVEN_IN]), ("conv_w", [4, 1536]), ("conv_b", [1, 1536]),
    ("dt_bias", [1, 16]), ("a_log", [1, 16]), ("ssd_d", [1, 16]), ("ssd_ng", [1, D]), ("mu", [1, RW_IN]),
    ("w0", [1, D]), ("w2l", [64, D]), ("a0", [1, D]), ("a2l", [64, D]), ("g2l", [128, D]), ("k_k", [1, D]),
    ("k_a", [1, D]), ("r_k", [1, D]), ("lnx_g", [1, D]), ("lnx_b", [1, D]), ("w_eo", [2048, D]),
    ("w_qkv", [D, 3072]), ("w_oo", [D, D]), ("ln_mix_g", [2, D]), ("ln_mix_b", [2, D]),
    ("xa_wq", [2, D, D]), ("xa_wkv", [2, D, 2048]), ("xa_wo", [2, D, D]), ("ln_xa_g", [2, D]), ("ln_xa_b", [2, D]),
    ("w13", [2, D, 2 * FFN_H]), ("w2", [2, FFN_H, D]), ("ln_ffn_g", [2, D]), ("ln_ffn_b", [2, D]),
]


import os
CUT = int(os.environ.get('MK_CUT', '99'))


def build_program(upto=99, dumps=(), scan_tiles=NT, scan_parts=('ssd', 'rwkv'), n_groups=NG, skip_w=False):
    nc = bass.Bass("TRN2", target_bir_lowering=False)
    Dm = {}
    for name, shape in IN_SPECS:
        Dm[name] = nc.dram_tensor(name, shape, F32, kind="ExternalInput").ap()
    Dm["out"] = nc.dram_tensor("out", [S, D], F32, kind="ExternalOutput").ap()

    def internal(name, shape, dt):
        Dm[name] = nc.dram_tensor(name, shape, dt, kind="Internal").ap()

    internal("wb_in_a", [D, EVEN_IN], BF16)
    internal("wb_in_b", [D, RW_IN], BF16)
    internal("wb_eo", [2048, D], BF16)
    internal("wb_qkv", [D, 3072], BF16)
    internal("wb_oo", [D, D], BF16)
    internal("wb_xq", [2, D, D], BF16)
    internal("wb_xkv", [2, D, 2048], BF16)
    internal("wb_xo", [2, D, D], BF16)
    internal("wb_13", [2, D, 2 * FFN_H], BF16)
    internal("wb_2", [2, FFN_H, D], BF16)
    internal("wb_w2l", [64, D], BF16)
    internal("wb_a2l", [64, D], BF16)
    internal("wb_g2l", [128, D], BF16)
    internal("zs", [S, D], F32)
    internal("xbc", [1536, 3 + S], F32)
    internal("dts", [S, 16], F32)
    internal("rkv", [S, 3072], F32)
    internal("lw", [S, D], F32)
    internal("aa", [S, D], F32)
    internal("gg", [S, D], F32)
    internal("yT", [2048, S], BF16)
    internal("xcur", [S, D], F32)
    internal("qT", [D, S], BF16)
    internal("kT", [D, S], BF16)
    internal("vv", [S, D], BF16)
    internal("kmT", [128, 8, 16], F32)
    for nm in dumps:
        shp, dt = {"yT": ([2048, S], BF16), "xcur": ([S, D], F32), "lw": ([S, D], F32), "rkv": ([S, 3072], F32),
                   "zs": ([S, D], F32), "xbc": ([1536, 3 + S], F32), "aa": ([S, D], F32), "gg": ([S, D], F32),
                   "dts": ([S, 16], F32), "qT": ([D, S], BF16), "vv": ([S, D], BF16)}[nm]
        Dm["dump_" + nm] = nc.dram_tensor("dump_" + nm, shp, dt, kind="ExternalOutput").ap()

    with contextlib.ExitStack() as top:
        P = Prog(nc, top)

        uid = [0]

        def sb(st, name, shape, dt=F32):
            uid[0] += 1
            return st.enter_context(nc.sbuf_tensor("%s_%d" % (name, uid[0]), shape, dt))

        def psum(st, name, shape, dt=F32):
            uid[0] += 1
            return st.enter_context(nc.psum_tensor("%s_%d" % (name, uid[0]), shape, dt))

        def op(eng, fn, r=(), w=()):
            return P.op(eng, fn, reads=r, writes=w, pe_chain=(eng == "pe"))

        def dma(out, in_, r=(), w=(), eng="sp"):
            return P.dma(lambda e: e.dma_start(out=out, in_=in_), reads=r, writes=w, eng=eng)

        def dma_slow(out, in_, r=(), w=(), eng="sp"):
            return P.dma(lambda e: e.dma_start(out=out, in_=in_, allow_slow_non_contiguous=True), reads=r, writes=w, eng=eng)

        def mm(out, lhsT, rhs, start, stop, r, w):
            return op("pe", lambda e: e.matmul(out, lhsT=lhsT, rhs=rhs, start=start, stop=stop), r, w)

        def tp(out, in_, ident, r, w):
            return op("pe", lambda e: e.transpose(out=out, in_=in_, identity=ident), r, w)

        def act(out, in_, func, r, w, bias=None, scale=None, accum_out=None):
            kw = {}
            if bias is not None:
                kw["bias"] = bias
            if scale is not None:
                kw["scale"] = scale
            if accum_out is not None:
                kw["accum_out"] = accum_out
            return op("act", lambda e: e.activation(out=out, in_=in_, func=func, **kw), r, w)

        def tt(eng, out, in0, in1, alu, r, w):
            return op(eng, lambda e: e.tensor_tensor(out=out, in0=in0, in1=in1, op=alu), r, w)

        def ts(eng, out, in0, s1, s2, op0, op1, r, w):
            if op1 is None:
                return op(eng, lambda e: e.tensor_scalar(out=out, in0=in0, scalar1=s1, scalar2=None, op0=op0), r, w)
            return op(eng, lambda e: e.tensor_scalar(out=out, in0=in0, scalar1=s1, scalar2=s2, op0=op0, op1=op1), r, w)

        def stt(out, in0, scalar, in1, op0, op1, r, w):
            return op("dve", lambda e: e.scalar_tensor_tensor(out=out, in0=in0, scalar=scalar, in1=in1, op0=op0, op1=op1), r, w)

        def cp(eng, out, in_, r, w):
            if eng == "act":
                return act(out, in_, AF.Copy, r, w)
            return op(eng, lambda e: e.tensor_copy(out=out, in_=in_), r, w)

        def red(out, in_, r, w, alu=ALU.add):
            return op("dve", lambda e: e.tensor_reduce(out=out, in_=in_, axis=AX.X, op=alu), r, w)

        def bc_load(tile, vec_row_ap, w):
            return dma(tile, vec_row_ap.partition_broadcast(128), r=(), w=w)

        ident_b = sb(top, "ident_b", [128, 128], BF16)
        ident_f = sb(top, "ident_f", [128, 128], F32)
        ones_b = sb(top, "ones_b", [128, 128], BF16)
        ones_f = sb(top, "ones_f", [128, 128], F32)
        le_f = sb(top, "le_f", [128, 128], F32)
        lt_f = sb(top, "lt_f", [128, 128], F32)
        gt_f = sb(top, "gt_f", [128, 128], F32)
        op("pool", lambda e: e.memset(ones_f[:], 1.0), w=["ones_f"])
        op("pool", lambda e: e.memset(ones_b[:], 1.0), w=["ones_b"])
        op("pool", lambda e: e.memset(ident_f[:], 0.0), w=["ident_f"])
        op("pool", lambda e: e.affine_select(out=ident_f[:], in_=ident_f[:], pattern=[[-1, 128]], compare_op=ALU.not_equal,
                                             fill=1.0, base=0, channel_multiplier=1), r=["ident_f"], w=["ident_f"])
        op("pool", lambda e: e.tensor_copy(out=ident_b[:], in_=ident_f[:]), r=["ident_f"], w=["ident_b"])
        op("pool", lambda e: e.affine_select(out=le_f[:], in_=ones_f[:], pattern=[[1, 128]], compare_op=ALU.is_ge,
                                             fill=0.0, base=0, channel_multiplier=-1), r=["ones_f"], w=["le_f"])
        op("pool", lambda e: e.affine_select(out=lt_f[:], in_=ones_f[:], pattern=[[1, 128]], compare_op=ALU.is_gt,
                                             fill=0.0, base=0, channel_multiplier=-1), r=["ones_f"], w=["lt_f"])
        op("pool", lambda e: e.affine_select(out=gt_f[:], in_=ones_f[:], pattern=[[-1, 128]], compare_op=ALU.is_gt,
                                             fill=0.0, base=0, channel_multiplier=1), r=["ones_f"], w=["gt_f"])
        CONST_R = ["ident_b", "ident_f", "ones_b", "ones_f", "le_f", "lt_f", "gt_f"]

        def cast_w(dst, src, rows, name):
            for r0 in range(0, rows, 128):
                r1 = min(rows, r0 + 128)
                dma(dst[r0:r1], src[r0:r1], w=[], eng="pool")

        def stage_w():
            with contextlib.ExitStack() as st:
                cast_w(Dm["wb_in_a"][:, 0:SSD_IN], Dm["w_in"][:, 0:SSD_IN], D, "wb_in_a")
                cast_w(Dm["wb_eo"], Dm["w_eo"], 2048, "wb_eo")
                cast_w(Dm["wb_qkv"], Dm["w_qkv"], D, "wb_qkv")
                cast_w(Dm["wb_oo"], Dm["w_oo"], D, "wb_oo")
                for l in range(2):
                    cast_w(Dm["wb_xq"][l], Dm["xa_wq"][l], D, "wb_xq")
                    cast_w(Dm["wb_xkv"][l], Dm["xa_wkv"][l], D, "wb_xkv")
                    cast_w(Dm["wb_xo"][l], Dm["xa_wo"][l], D, "wb_xo")
                    cast_w(Dm["wb_13"][l], Dm["w13"][l], D, "wb_13")
                    cast_w(Dm["wb_2"][l], Dm["w2"][l], FFN_H, "wb_2")
                cast_w(Dm["wb_w2l"], Dm["w2l"], 64, "wb_w2l")
                cast_w(Dm["wb_a2l"], Dm["a2l"], 64, "wb_a2l")
                cast_w(Dm["wb_g2l"], Dm["g2l"], 128, "wb_g2l")
                mu_bc = sb(st, "mu_bc", [128, RW_IN])
                omu_bc = sb(st, "omu_bc", [128, RW_IN])
                bc_load(mu_bc[:], Dm["mu"][0], ["mu_bc"])
                ts("dve", omu_bc[:], mu_bc[:], -1.0, 1.0, ALU.mult, ALU.add, ["mu_bc"], ["omu_bc"])
                wf = [sb(st, "wf%d" % i, [128, RW_IN]) for i in range(2)]
                wa = [sb(st, "wa%d" % i, [128, RW_IN], BF16) for i in range(2)]
                wbt = [sb(st, "wbt%d" % i, [128, RW_IN], BF16) for i in range(2)]
                for kc in range(8):
                    i = kc % 2
                    dma(wf[i][:], Dm["w_in"][kc * 128:(kc + 1) * 128, SSD_IN:EVEN_IN], w=["wf%d" % i])
                    tt("dve", wbt[i][:], wf[i][:], mu_bc[:], ALU.mult, ["wf%d" % i, "mu_bc"], ["wbt%d" % i])
                    tt("pool", wa[i][:], wf[i][:], omu_bc[:], ALU.mult, ["wf%d" % i, "omu_bc"], ["wa%d" % i])
                    dma(Dm["wb_in_b"][kc * 128:(kc + 1) * 128, :], wbt[i][:], r=["wbt%d" % i], w=[])
                    dma(Dm["wb_in_a"][kc * 128:(kc + 1) * 128, SSD_IN:EVEN_IN], wa[i][:], r=["wa%d" % i], w=[])
                zt = sb(st, "zt", [128, 12, 3])
                op("dve", lambda e: e.memset(zt[:], 0.0), w=["zt"])
                dma(Dm["xbc"][:, 0:3].rearrange("(c p) t -> p c t", p=128), zt[:], r=["zt"], w=[])
                P.barrier()

        class WB:
            def __init__(self, st, n=3):
                self.t = [sb(st, "wbuf%d" % i, [128, 8, 512], BF16) for i in range(n)]
                self.i = 0

            def load(self, Wd, wname, r0, nkc, c0, ncols):
                i = self.i
                self.i = (self.i + 1) % len(self.t)
                t = self.t[i]
                dma(t[:, 0:nkc, 0:ncols],
                    Wd[r0 * 128:(r0 + nkc) * 128, c0:c0 + ncols].rearrange("(c p) f -> p c f", p=128),
                    r=[wname], w=["wbuf%d" % i])
                return t, "wbuf%d" % i

        def transpose_tile(src_bf, nblk, pst, pst_key, dst_view, dst_key, src_key, evac_eng):
            for c in range(nblk):
                tp(pst[:, c, :], src_bf[:, c * 128:(c + 1) * 128], ident_b[:], [src_key, "ident_b"], [pst_key])
            cp(evac_eng, dst_view, pst[:, 0:nblk, :], [pst_key], [dst_key])

        def layer_norm(s_ap, skey, out_ap, okey, g_bc, b_bc, gkeys, sm, smkey):
            op("dve", lambda e: e.bn_stats(out=sm[:, 0:6], in_=s_ap[:, 0:512]), [skey], [smkey])
            op("dve", lambda e: e.bn_stats(out=sm[:, 6:12], in_=s_ap[:, 512:1024]), [skey], [smkey])
            op("dve", lambda e: e.bn_aggr(out=sm[:, 12:14], in_=sm[:, 0:12]), [smkey], [smkey])
            ts("dve", sm[:, 14:15], sm[:, 13:14], 1e-5, None, ALU.add, None, [smkey], [smkey])
            act(sm[:, 14:15], sm[:, 14:15], AF.Sqrt, [smkey], [smkey])
            op("dve", lambda e: e.reciprocal(out=sm[:, 14:15], in_=sm[:, 14:15]), [smkey], [smkey])
            ts("dve", out_ap, s_ap, sm[:, 12:13], sm[:, 14:15], ALU.subtract, ALU.mult, [skey, smkey], [okey])
            tt("pool", out_ap, out_ap, g_bc, ALU.mult, [okey] + gkeys, [okey])
            tt("pool", out_ap, out_ap, b_bc, ALU.add, [okey] + gkeys, [okey])

        def stage_l0_proj():
            with contextlib.ExitStack() as st:
                wb = WB(st, 3)
                xt = [sb(st, "xt%d" % i, [128, D]) for i in range(2)]
                xb = [sb(st, "xb%d" % i, [128, D], BF16) for i in range(2)]
                xT = [sb(st, "xT%d" % i, [128, 8, 513], BF16) for i in range(2)]
                ev = [sb(st, "ev%d" % i, [128, 512]) for i in range(4)]
                lo1 = sb(st, "lo1", [128, 512], BF16)
                lo2 = sb(st, "lo2", [128, 512], BF16)
                wa2 = sb(st, "wa2", [128, D], BF16)
                g2s = sb(st, "g2s", [128, D], BF16)
                w0_bc = sb(st, "w0_bc", [128, D])
                a0_bc = sb(st, "a0_bc", [128, D])
                pst = psum(st, "pst", [128, 8, 128], BF16)
                ptm = psum(st, "ptm", [128, 4, 512], F32)
                pfm = psum(st, "pfm", [128, 2, 512], F32)
                dma(wa2[0:64, :], Dm["wb_w2l"], r=["wb_w2l"], w=["wa2"])
                dma(wa2[64:128, :], Dm["wb_a2l"], r=["wb_a2l"], w=["wa2"])
                dma(g2s[:], Dm["wb_g2l"], r=["wb_g2l"], w=["g2s"])
                bc_load(w0_bc[:], Dm["w0"][0], ["w0_bc"])
                bc_load(a0_bc[:], Dm["a0"][0], ["a0_bc"])
                evi = [0]

                def next_ev():
                    i = evi[0]
                    evi[0] = (i + 1) % 4
                    return ev[i], "ev%d" % i

                for g in range(n_groups):
                    xTg = xT[g % 2]
                    xk = "xT%d" % (g % 2)
                    if g == 0:
                        op("dve", lambda e, xTg=xTg: e.memset(xTg[:, :, 0:1], 0.0), w=[xk])
                    else:
                        cp("dve", xTg[:, :, 0:1], xT[(g - 1) % 2][:, :, 512:513], ["xT%d" % ((g - 1) % 2)], [xk])
                    for t in range(4):
                        i = t % 2
                        row0 = g * TG + t * 128
                        dma(xt[i][:], Dm["x"][row0:row0 + 128, :], r=["x"], w=["xt%d" % i])
                        cp("act", xb[i][:], xt[i][:], ["xt%d" % i], ["xb%d" % i])
                        transpose_tile(xb[i], 8, pst, "pst", xTg[:, :, 1 + t * 128:1 + (t + 1) * 128], xk, "xb%d" % i, "dve")

                    def lhs_cur(kc, t):
                        return xTg[:, kc, 1 + t * 128:1 + (t + 1) * 128]

                    def lhs_prev(kc, t):
                        return xTg[:, kc, t * 128:(t + 1) * 128]

                    for n in range(2):
                        wt, wk = wb.load(Dm["wb_in_a"], "wb_in_a", 0, 8, n * 512, 512)
                        for t in range(4):
                            for kc in range(8):
                                mm(ptm[:, t, :], lhs_cur(kc, t), wt[:, kc, :], kc == 0, kc == 7, [xk, wk], [("ptm", t)])
                            e_t, e_k = next_ev()
                            act(e_t[:], ptm[:, t, :], AF.Silu, [("ptm", t)], [e_k])
                            row0 = g * TG + t * 128
                            dma(Dm["zs"][row0:row0 + 128, n * 512:(n + 1) * 512], e_t[:], r=[e_k], w=[])
                    for fc in range(12):
                        if fc % 4 == 0:
                            wt, wk = wb.load(Dm["wb_in_a"], "wb_in_a", 0, 8, 1024 + fc * 128, 512)
                        pb = fc % 2
                        for kc in range(8):
                            mm(pfm[:, pb, :], wt[:, kc, (fc % 4) * 128:(fc % 4 + 1) * 128], xTg[:, kc, 1:513], kc == 0, kc == 7,
                               [xk, wk], [("pfm", pb)])
                        e_t, e_k = next_ev()
                        cp("dve", e_t[:], pfm[:, pb, :], [("pfm", pb)], [e_k])
                        dma(Dm["xbc"][fc * 128:(fc + 1) * 128, 3 + g * TG:3 + (g + 1) * TG], e_t[:], r=[e_k], w=[])
                    wt, wk = wb.load(Dm["wb_in_a"], "wb_in_a", 0, 8, 2560, 16)
                    for t in range(4):
                        for kc in range(8):
                            mm(ptm[:, t, 0:16], lhs_cur(kc, t), wt[:, kc, 0:16], kc == 0, kc == 7, [xk, wk], [("ptm", t)])
                        e_t, e_k = next_ev()
                        cp("dve", e_t[:, 0:16], ptm[:, t, 0:16], [("ptm", t)], [e_k])
                        row0 = g * TG + t * 128
                        dma(Dm["dts"][row0:row0 + 128, :], e_t[:, 0:16], r=[e_k], w=[])
                    for n in range(6):
                        wt_a, wk_a = wb.load(Dm["wb_in_a"], "wb_in_a", 0, 8, SSD_IN + n * 512, 512)
                        wt_b, wk_b = wb.load(Dm["wb_in_b"], "wb_in_b", 0, 8, n * 512, 512)
                        for t in range(4):
                            for kc in range(8):
                                mm(ptm[:, t, :], lhs_cur(kc, t), wt_a[:, kc, :], kc == 0, False, [xk, wk_a], [("ptm", t)])
                            for kc in range(8):
                                mm(ptm[:, t, :], lhs_prev(kc, t), wt_b[:, kc, :], False, kc == 7, [xk, wk_b], [("ptm", t)])
                            e_t, e_k = next_ev()
                            cp("act" if t % 2 else "dve", e_t[:], ptm[:, t, :], [("ptm", t)], [e_k])
                            row0 = g * TG + t * 128
                            dma(Dm["rkv"][row0:row0 + 128, n * 512:(n + 1) * 512], e_t[:], r=[e_k], w=[])
                    wt_a, wk_a = wb.load(Dm["wb_in_a"], "wb_in_a", 0, 8, SSD_IN + 3072, 256)
                    wt_b, wk_b = wb.load(Dm["wb_in_b"], "wb_in_b", 0, 8, 3072, 256)
                    for c in range(2):
                        for kc in range(8):
                            mm(pfm[:, c, :], wt_a[:, kc, c * 128:(c + 1) * 128], xTg[:, kc, 1:513], kc == 0, False, [xk, wk_a], [("pfm", c)])
                        for kc in range(8):
                            mm(pfm[:, c, :], wt_b[:, kc, c * 128:(c + 1) * 128], xTg[:, kc, 0:512], False, kc == 7, [xk, wk_b], [("pfm", c)])
                    act(lo1[0:64, :], pfm[0:64, 0, :], AF.Tanh, [("pfm", 0)], ["lo1"])
                    act(lo1[64:128, :], pfm[64:128, 0, :], AF.Copy, [("pfm", 0)], ["lo1"])
                    act(lo2[:], pfm[:, 1, :], AF.Sigmoid, [("pfm", 1)], ["lo2"])
                    for kind in range(3):
                        for n in range(2):
                            for t in range(4):
                                cs = slice(t * 128, (t + 1) * 128)
                                ns = slice(n * 512, (n + 1) * 512)
                                row0 = g * TG + t * 128
                                e_t, e_k = next_ev()
                                if kind == 0:
                                    mm(ptm[:, t, :], lo1[0:64, cs], wa2[0:64, ns], True, True, ["lo1", "wa2"], [("ptm", t)])
                                    tt("dve", e_t[:], ptm[:, t, :], w0_bc[:, ns], ALU.add, [("ptm", t), "w0_bc"], [e_k])
                                    act(e_t[:], e_t[:], AF.Sigmoid, [e_k], [e_k])
                                    dma(Dm["lw"][row0:row0 + 128, ns], e_t[:], r=[e_k], w=[])
                                elif kind == 1:
                                    mm(ptm[:, t, :], lo1[64:128, cs], wa2[64:128, ns], True, True, ["lo1", "wa2"], [("ptm", t)])
                                    tt("dve", e_t[:], ptm[:, t, :], a0_bc[:, ns], ALU.add, [("ptm", t), "a0_bc"], [e_k])
                                    act(e_t[:], e_t[:], AF.Sigmoid, [e_k], [e_k])
                                    dma(Dm["aa"][row0:row0 + 128, ns], e_t[:], r=[e_k], w=[])
                                else:
                                    mm(ptm[:, t, :], lo2[:, cs], g2s[:, ns], True, True, ["lo2", "g2s"], [("ptm", t)])
                                    cp("dve", e_t[:], ptm[:, t, :], [("ptm", t)], [e_k])
                                    dma(Dm["gg"][row0:row0 + 128, ns], e_t[:], r=[e_k], w=[])
                P.barrier()

        def stage_l0_scan():
            with contextlib.ExitStack() as st:
                Ft = [sb(st, "F%d" % i, [128, D]) for i in range(11)]
                Fk = ["F%d" % i for i in range(11)] + [("arhs", 0), ("arhs", 1)]
                Hn = ["rH", "kH", "bH", "aH", "kC", "bC", "vH", "Xb", "Ub", "Yo"]
                Ht = {n: sb(st, n, [128, D], BF16) for n in Hn}
                Ht["XDT"] = Ht["kC"]; Ht["XD"] = Ht["bC"]
                fmA = sb(st, "fmA", [64, 16, 2, 128], BF16)
                fmB = sb(st, "fmB", [64, 16, 128], BF16)
                fmK = sb(st, "fmK", [64, 16, 128], BF16)
                MrbT = sb(st, "MrbT", [128, 16, 128], BF16)
                MakT = sb(st, "MakT", [128, 16, 128], BF16)
                MrkT = sb(st, "MrkT", [128, 16, 128], BF16)
                TT = sb(st, "TT", [128, 16, 128], BF16)
                NQ = [[sb(st, "NQ%d_%d" % (a, b), [128, 4, 128], BF16) for b in range(2)] for a in range(2)]
                NQT = [[sb(st, "NQT%d_%d" % (a, b), [128, 4, 128], BF16) for b in range(2)] for a in range(2)]
                NP = [[sb(st, "NP%d_%d" % (a, b), [128, 4, 128], BF16) for b in range(2)] for a in range(2)]
                S0f = sb(st, "S0f", [64, D])
                S0b = sb(st, "S0b", [64, D], BF16)
                SPf = sb(st, "SPf", [128, D])
                SPb = sb(st, "SPb", [128, D], BF16)
                xin = sb(st, "xin", [128, 12, 131])
                xinb = sb(st, "xinb", [128, 12, 131], BF16)
                diagW = sb(st, "diagW", [128, 4, 12, 128], BF16)
                cwt = sb(st, "cwt", [128, 4, 12])
                cbrow = sb(st, "cbrow", [1, 1536], BF16)
                cbrow_f = sb(st, "cbrow_f", [1, 1536])
                BCf = sb(st, "BCf", [128, 4, 128], BF16)
                arhs = sb(st, "arhs", [128, 16, 128])
                arf = arhs[:].rearrange("p h l -> p (h l)")
                Ft.append(arf[:, 0:1024]); Ft.append(arf[:, 1024:2048])
                xs_fm = Ft[9][:].rearrange("p (c t) -> p c t", t=128)
                LTe = sb(st, "LTe", [128, 16, 128], BF16)
                MT = sb(st, "MT", [128, 16, 128], BF16)
                CBm = sb(st, "CBm", [128, 2, 128], BF16)
                Btm = sb(st, "Btm", [128, 256], BF16)
                yTo = [sb(st, "yTo%d" % i, [128, 8, 128], BF16) for i in range(2)]
                sm = sb(st, "sm", [128, 256])
                dtt = sb(st, "dtt", [128, 16])
                kk_bc = sb(st, "kk_bc", [128, D]); ka_bc = sb(st, "ka_bc", [128, D]); rk_bc = sb(st, "rk_bc", [128, D])
                lg_bc = sb(st, "lg_bc", [128, D]); lb_bc = sb(st, "lb_bc", [128, D]); ng_bc = sb(st, "ng_bc", [128, D])
                vec16 = sb(st, "vec16", [128, 64])
                PS = psum(st, "PS", [128, 8, 512], F32)

                def pk(b0, nb=1):
                    return [("ps", b) for b in range(b0, b0 + nb)]

                def psv(b0, nb, inner):
                    return PS[:, b0:b0 + nb, :].rearrange("p b (c t) -> p (b c) t", t=inner)

                def psflat(b0, nb):
                    return PS[:, b0:b0 + nb, :].rearrange("p b t -> p (b t)")

                def psbf(b0, nb):
                    return PS[:, b0:b0 + nb, :].rearrange("p b t -> p (b t)").bitcast(BF16)

                for t_, nm in ((kk_bc, "k_k"), (ka_bc, "k_a"), (rk_bc, "r_k"), (lg_bc, "lnx_g"), (lb_bc, "lnx_b"), (ng_bc, "ssd_ng")):
                    bc_load(t_[:], Dm[nm][0], ["bcs"])
                bc_load(vec16[:, 0:16], Dm["dt_bias"][0], ["vec16"])
                bc_load(vec16[:, 16:32], Dm["a_log"][0], ["vec16"])
                bc_load(vec16[:, 32:48], Dm["ssd_d"][0], ["vec16"])
                act(vec16[:, 16:32], vec16[:, 16:32], AF.Exp, ["vec16"], ["vec16"])
                ts("dve", vec16[:, 16:32], vec16[:, 16:32], -1.0, None, ALU.mult, None, ["vec16"], ["vec16"])
                dma_slow(cwt[:], Dm["conv_w"].rearrange("k (c p) -> p k c", p=128), w=["cwt"])
                dma(cbrow_f[:], Dm["conv_b"], w=["cbrow_f"])
                cp("dve", cbrow[:], cbrow_f[:], ["cbrow_f"], ["cbrow"])
                for k in range(4):
                    for c in range(12):
                        ts("dve" if (k * 12 + c) % 2 else "pool", diagW[:, k, c, :], ident_f[:], cwt[:, k, c:c + 1], None, ALU.mult, None,
                           ["cwt", "ident_f"], ["diagW"])
                op("dve", lambda e: e.memset(S0f[:], 0.0), w=["S0f"])
                op("dve", lambda e: e.memset(S0b[:], 0.0), w=["S0b"])
                op("pool", lambda e: e.memset(SPf[:], 0.0), w=["SPf"])
                op("pool", lambda e: e.memset(SPb[:], 0.0), w=["SPb"])
                R_, K_, V_, LW_, A_, G_, ZS_ = range(7)

                def v3(ap):
                    return ap.rearrange("p (h d) -> p h d", d=64)

                def bc16(ap16):
                    return ap16.unsqueeze(2).to_broadcast([128, 16, 64])

                def mat_bc(m, n):
                    return m.unsqueeze(1).to_broadcast([128, n, 128])

                for it in range(scan_tiles):
                    r0 = it * 128
                    dma(Ft[R_][:], Dm["rkv"][r0:r0 + 128, 0:1024], r=["rkv"], w=[Fk[R_]])
                    dma(Ft[K_][:], Dm["rkv"][r0:r0 + 128, 1024:2048], r=["rkv"], w=[Fk[K_]])
                    dma(Ft[V_][:], Dm["rkv"][r0:r0 + 128, 2048:3072], r=["rkv"], w=[Fk[V_]])
                    dma(Ft[LW_][:], Dm["lw"][r0:r0 + 128, :], r=["lw"], w=[Fk[LW_]])
                    dma(Ft[A_][:], Dm["aa"][r0:r0 + 128, :], r=["aa"], w=[Fk[A_]])
                    dma(Ft[G_][:], Dm["gg"][r0:r0 + 128, :], r=["gg"], w=[Fk[G_]])
                    dma(Ft[ZS_][:], Dm["zs"][r0:r0 + 128, :], r=["zs"], w=[Fk[ZS_]])
                    dma(xin[:], Dm["xbc"][:, r0:r0 + 131].rearrange("(c p) t -> p c t", p=128), r=["xbc"], w=["xin"])
                    dma(dtt[:], Dm["dts"][r0:r0 + 128, :], r=["dts"], w=["dtt"])

                    if 'ssd' not in scan_parts:
                        continue
                    XS, TY, T3 = Ft[7], Ft[8], Ft[9]
                    kXS, kTY, kT3 = Fk[7], Fk[8], Fk[9]
                    cp("pool", xinb[:], xin[:], ["xin"], ["xinb"])
                    pconv = psv(0, 3, 128)
                    for c in range(12):
                        for k in range(4):
                            mm(pconv[:, c, :], diagW[:, k, c, :], xinb[:, c, k:k + 128], k == 0, False, ["diagW", "xinb"], pk(c // 4))
                        mm(pconv[:, c, :], cbrow[0:1, c * 128:(c + 1) * 128], ones_b[0:1, :], False, True, ["cbrow", "ones_b"], pk(c // 4))
                    act(xs_fm[:], pconv[:, 0:8, :], AF.Silu, pk(0, 2), [Fk[9]])
                    act(BCf[:], pconv[:, 8:12, :], AF.Silu, pk(2), ["BCf"])
                    if CUT < 1:
                        continue
                    dtv = sm[:, 0:16]; av = sm[:, 16:32]; eac = sm[:, 32:48]; cdec = sm[:, 48:64]
                    tt("dve", dtv, dtt[:], vec16[:, 0:16], ALU.add, ["dtt", "vec16"], ["sm_dt"])
                    act(dtv, dtv, AF.Exp, ["sm_dt"], ["sm_dt"])
                    act(dtv, dtv, AF.Ln, ["sm_dt"], ["sm_dt"], bias=1.0)
                    tt("dve", av, dtv, vec16[:, 16:32], ALU.mult, ["sm_dt", "vec16"], ["sm_av"])
                    if CUT < 2:
                        continue
                    pxs = psv(3, 2, 128)
                    for c in range(8):
                        if os.environ.get("MK_V") == "A":
                            break
                        mm(pxs[:, c, :], xs_fm[:, c, :], ident_f[:], True, True, [Fk[9], "ident_f"], pk(3 + c // 4))
                    pxs_f = psflat(3, 2)
                    if os.environ.get("MK_V") != "B":
                        cp("act", XS[:], pxs_f, pk(3, 2), [kXS])
                    if os.environ.get("MK_V") != "C":
                        tt("dve", v3(Ht["XDT"][:]), v3(pxs_f), bc16(dtv), ALU.mult, pk(3, 2) + ["sm_dt"], ["kC"])
                    if CUT < 3:
                        continue
                    pbt = psbf(5, 1)[:, 0:256].rearrange("p (c t) -> p c t", t=128)
                    for c in range(2):
                        tp(pbt[:, c, :], BCf[:, c, :], ident_b[:], ["BCf", "ident_b"], pk(5))
                    cp("dve", Btm[:].rearrange("p (c t) -> p c t", t=128), pbt, pk(5), ["Btm"])
                    if CUT < 4:
                        continue
                    mm(PS[:, 6, 0:16], le_f[:], av, True, True, ["le_f", "sm_av"], pk(6))
                    mm(PS[:, 6, 16:32], ones_f[:], av, True, True, ["ones_f", "sm_av"], pk(6))
                    act(sm[:, 32:64], PS[:, 6, 0:32], AF.Exp, pk(6), ["sm_e"])
                    if CUT < 5:
                        continue
                    tt("dve", arhs[:], mat_bc(le_f[:], 16), av.unsqueeze(2).to_broadcast([128, 16, 128]), ALU.mult, ["le_f", "sm_av"], [("arhs", 0), ("arhs", 1)])
                    pseg = psv(0, 4, 128)
                    for b in range(4):
                        mm(PS[:, b, :], gt_f[:], arhs[:, 4 * b:4 * b + 4, :].rearrange("p h l -> p (h l)"), True, True, ["gt_f", ("arhs", 0), ("arhs", 1)], pk(b))
                    act(LTe[:], pseg, AF.Exp, pk(0, 4), ["LTe"])
                    if CUT < 6:
                        continue
                    pcb = PS[:, 7, 0:256].rearrange("p (g l) -> p g l", l=128)
                    for g2 in range(2):
                        mm(pcb[:, g2, :], BCf[:, g2, :], BCf[:, 2 + g2, :], True, True, ["BCf"], pk(7))
                    tt("dve", CBm[:], pcb, mat_bc(le_f[:], 2), ALU.mult, pk(7) + ["le_f"], ["CBm"])
                    tt("pool", MT[:].rearrange("p (g e) l -> p g e l", g=2), LTe[:].rearrange("p (g e) l -> p g e l", g=2),
                       CBm[:].unsqueeze(2).to_broadcast([128, 2, 8, 128]), ALU.mult, ["LTe", "CBm"], ["MT"])
                    if CUT < 7:
                        continue
                    pyd = psflat(4, 2)
                    for h in range(16):
                        mm(pyd[:, h * 64:(h + 1) * 64], MT[:, h, :], Ht["XDT"][:, h * 64:(h + 1) * 64], True, True, ["MT", "kC"], pk(4 + h // 8))
                    pyo = psflat(6, 2)
                    for g2 in range(2):
                        mm(pyo[:, g2 * 512:(g2 + 1) * 512], BCf[:, 2 + g2, :], SPb[:, g2 * 512:(g2 + 1) * 512], True, True, ["BCf", "SPb"], pk(6 + g2))
                    tt("dve", v3(TY[:]), v3(pyo), bc16(eac), ALU.mult, pk(6, 2) + ["sm_e"], [kTY])
                    tt("dve", TY[:], TY[:], pyd, ALU.add, [kTY] + pk(4, 2), [kTY])
                    tt("pool", v3(T3[:]), v3(XS[:]), bc16(vec16[:, 32:48]), ALU.mult, [kXS, "vec16"], [kT3])
                    tt("pool", TY[:], TY[:], T3[:], ALU.add, [kTY, kT3], [kTY])
                    if CUT < 8:
                        continue
                    tt("dve", v3(Ht["XD"][:]), v3(Ht["XDT"][:]), LTe[:, :, 127:128].to_broadcast([128, 16, 64]), ALU.mult, ["kC", "LTe"], ["bC"])
                    pst_ = psflat(0, 2)
                    for g2 in range(2):
                        mm(pst_[:, g2 * 512:(g2 + 1) * 512], Btm[:, g2 * 128:(g2 + 1) * 128], Ht["XD"][:, g2 * 512:(g2 + 1) * 512], True, True,
                           ["Btm", "bC"], pk(g2))
                    tt("dve", v3(SPf[:]), v3(SPf[:]), bc16(cdec), ALU.mult, ["SPf", "sm_e"], ["SPf"])
                    tt("dve", SPf[:], SPf[:], pst_, ALU.add, ["SPf"] + pk(0, 2), ["SPf"])
                    cp("act", SPb[:], SPf[:], ["SPf"], ["SPb"])
                    if CUT < 9:
                        continue
                    tt("dve", TY[:], TY[:], Ft[ZS_][:], ALU.mult, [kTY, Fk[ZS_]], [kTY])
                    for g2 in range(2):
                        act(T3[:, g2 * 512:(g2 + 1) * 512], TY[:, g2 * 512:(g2 + 1) * 512], AF.Square, [kTY], [kT3, "sm_q"], accum_out=sm[:, 64 + g2:65 + g2])
                    ts("dve", sm[:, 64:66], sm[:, 64:66], 1.0 / 512.0, 1e-5, ALU.mult, ALU.add, ["sm_q"], ["sm_q"])
                    act(sm[:, 64:66], sm[:, 64:66], AF.Sqrt, ["sm_q"], ["sm_q"])
                    op("dve", lambda e: e.reciprocal(out=sm[:, 64:66], in_=sm[:, 64:66]), ["sm_q"], ["sm_q"])
                    tt("dve", TY[:].rearrange("p (g e) -> p g e", g=2), TY[:].rearrange("p (g e) -> p g e", g=2),
                       sm[:, 64:66].unsqueeze(2).to_broadcast([128, 2, 512]), ALU.mult, [kTY, "sm_q"], [kTY])
                    tt("pool", Ht["Yo"][:], TY[:], ng_bc[:], ALU.mult, [kTY, "bcs"], ["Yo"])
                    pty = psbf(6, 1).rearrange("p (c t) -> p c t", t=128)
                    yt_ = yTo[0]
                    transpose_tile(Ht["Yo"], 8, pty, ("ps", 6), yt_[:], "yTo0", "Yo", "act")
                    dma(Dm["yT"][0:1024, r0:r0 + 128].rearrange("(c p) t -> p c t", p=128), yt_[:], r=["yTo0"], w=[])

                    if 'rwkv' not in scan_parts:
                        continue
                    KK, T1, E1, E2, E3, E4 = Ft[7], Ft[8], Ft[9], Ft[10], Ft[11], Ft[12]
                    kKK, kT1, kE1, kE2, kE3, kE4 = Fk[7], Fk[8], Fk[9], Fk[10], Fk[11], Fk[12]
                    R, Kt, V, LW, A, G = Ft[R_], Ft[K_], Ft[V_], Ft[LW_], Ft[A_], Ft[G_]
                    kR, kK, kV, kLW, kA, kG = Fk[R_], Fk[K_], Fk[V_], Fk[LW_], Fk[A_], Fk[G_]
                    tt("dve", KK[:], Kt[:], kk_bc[:], ALU.mult, [kK, "bcs"], [kKK])
                    tt("pool", T1[:], KK[:], KK[:], ALU.mult, [kKK], [kT1])
                    red(sm[:, 80:96], v3(T1[:]), [kT1], ["sm_r"])
                    ts("dve", sm[:, 80:96], sm[:, 80:96], 1e-24, None, ALU.max, None, ["sm_r"], ["sm_r"])
                    act(sm[:, 80:96], sm[:, 80:96], AF.Sqrt, ["sm_r"], ["sm_r"])
                    op("dve", lambda e: e.reciprocal(out=sm[:, 80:96], in_=sm[:, 80:96]), ["sm_r"], ["sm_r"])
                    tt("dve", v3(KK[:]), v3(KK[:]), bc16(sm[:, 80:96]), ALU.mult, [kKK, "sm_r"], [kKK])
                    stt(T1[:], A[:], -1.0, ka_bc[:], ALU.add, ALU.mult, [kA, "bcs"], [kT1])
                    stt(Kt[:], T1[:], 1.0, Kt[:], ALU.add, ALU.mult, [kT1, kK], [kK])
                    tt("pool", T1[:], KK[:], A[:], ALU.mult, [kKK, kA], [kT1])
                    for (b0, msk, mk) in ((0, le_f, "le_f"), (2, lt_f, "lt_f"), (4, gt_f, "gt_f")):
                        for n in range(2):
                            mm(PS[:, b0 + n, :], msk[:], LW[:, n * 512:(n + 1) * 512], True, True, [mk, kLW], pk(b0 + n))
                    act(E1[:], psflat(0, 2), AF.Exp, pk(0, 2), [kE1], scale=SC_W)
                    act(E2[:], psflat(0, 2), AF.Exp, pk(0, 2), [kE2], scale=-SC_W)
                    act(E3[:], psflat(2, 2), AF.Exp, pk(2, 2), [kE3], scale=SC_W)
                    act(E4[:], psflat(4, 2), AF.Exp, pk(4, 2), [kE4], scale=SC_W)
                    for h in range(16):
                        mm(PS[0:64, 6, h:h + 1], LW[:, h * 64:(h + 1) * 64], ones_f[:, 0:1], True, True, [kLW, "ones_f"], pk(6))
                    pC = sm[0:64, 96:112]
                    act(pC, PS[0:64, 6, 0:16], AF.Exp, pk(6), ["sm_pc"], scale=SC_W)
                    tt("dve", Ht["rH"][:], R[:], E1[:], ALU.mult, [kR, kE1], ["rH"])
                    tt("pool", Ht["kH"][:], Kt[:], E2[:], ALU.mult, [kK, kE2], ["kH"])
                    tt("dve", Ht["bH"][:], T1[:], E2[:], ALU.mult, [kT1, kE2], ["bH"])
                    stt(Ht["aH"][:], KK[:], -1.0, E3[:], ALU.mult, ALU.mult, [kKK, kE3], ["aH"])
                    tt("pool", Ht["kC"][:], Kt[:], E4[:], ALU.mult, [kK, kE4], ["kC"])
                    tt("pool", Ht["bC"][:], T1[:], E4[:], ALU.mult, [kT1, kE4], ["bC"])
                    cp("act", Ht["vH"][:], V[:], [kV], ["vH"])
                    for j, (src, dstv, dk, b0) in enumerate((("aH", fmA[:, :, 0, :], "fmA", 0), ("rH", fmA[:, :, 1, :], "fmA", 2),
                                                             ("bH", fmB[:], "fmB", 4), ("kH", fmK[:], "fmK", 6))):
                        pt_ = psbf(b0, 2)[0:64, :].rearrange("p (h t) -> p h t", t=128)
                        for h in range(16):
                            tp(pt_[:, h, :], Ht[src][:, h * 64:(h + 1) * 64], ident_b[:], [src, "ident_b"], pk(b0 + h // 8))
                        cp("act" if j % 2 else "dve", dstv, pt_, pk(b0, 2), [dk])
                    for g4 in range(4):
                        bb = 0 if g4 % 2 == 0 else 4
                        pB = psv(bb, 2, 256)
                        pK = psv(bb + 2, 2, 256)
                        for hl in range(4):
                            h = g4 * 4 + hl
                            rhsA = fmA[:, h, :, :].rearrange("p a t -> p (a t)")
                            mm(pB[:, hl, :], fmB[:, h, :], rhsA, True, True, ["fmA", "fmB"], pk(bb + hl // 2))
                            mm(pK[:, hl, :], fmK[:, h, :], rhsA, True, True, ["fmA", "fmK"], pk(bb + 2 + hl // 2))
                        hs = slice(g4 * 4, g4 * 4 + 4)
                        q0 = NQT[g4 % 2][0]
                        tt("dve", q0[:], pB[:, :, 0:128], mat_bc(lt_f[:], 4), ALU.mult, pk(bb, 2) + ["lt_f"], ["NQT%d_0" % (g4 % 2), ("LT", g4)])
                        tt("dve", MrbT[:, hs, :], pB[:, :, 128:256], mat_bc(le_f[:], 4), ALU.mult, pk(bb, 2) + ["le_f"], [("MrbT", g4)])
                        tt("dve", MakT[:, hs, :], pK[:, :, 0:128], mat_bc(lt_f[:], 4), ALU.mult, pk(bb + 2, 2) + ["lt_f"], [("MakT", g4)])
                        tt("dve", MrkT[:, hs, :], pK[:, :, 128:256], mat_bc(le_f[:], 4), ALU.mult, pk(bb + 2, 2) + ["le_f"], [("MrkT", g4)])
                        cp("pool", TT[:, hs, :], q0[:], ["NQT%d_0" % (g4 % 2)], [("TT", g4)])
                    for pair in range(2):
                        cur = [0, 0]
                        for gi in range(2):
                            g4 = pair * 2 + gi
                            hs = slice(g4 * 4, g4 * 4 + 4)
                            bq = gi * 3
                            pL = psv(bq, 1, 128)
                            for hl in range(4):
                                h = g4 * 4 + hl
                                mm(pL[:, hl, :], fmA[:, h, 0, :], fmB[:, h, :], True, True, ["fmA", "fmB"], pk(bq))
                            tt("dve", NQ[gi][0][:], pL, mat_bc(gt_f[:], 4), ALU.mult, pk(bq) + ["gt_f"], ["NQ%d_0" % gi])
                            cp("pool", NQT[gi][0][:], TT[:, hs, :], [("TT", g4)], ["NQT%d_0" % gi])
                            tt("pool", NP[gi][0][:], NQT[gi][0][:], mat_bc(ident_f[:], 4), ALU.add, ["NQT%d_0" % gi, "ident_f"], ["NP%d_0" % gi])
                        for lvl in range(1, 7):
                            for gi in range(2):
                                g4 = pair * 2 + gi
                                bq = gi * 3
                                c = cur[gi]
                                n_ = 1 - c
                                Qc, QTc, Pc = NQ[gi][c], NQT[gi][c], NP[gi][c]
                                Qn, QTn, Pn = NQ[gi][n_], NQT[gi][n_], NP[gi][n_]
                                kQc, kQTc, kPc = "NQ%d_%d" % (gi, c), "NQT%d_%d" % (gi, c), "NP%d_%d" % (gi, c)
                                kQn, kQTn, kPn = "NQ%d_%d" % (gi, n_), "NQT%d_%d" % (gi, n_), "NP%d_%d" % (gi, n_)
                                pQ = psv(bq, 1, 128); pQT = psv(bq + 1, 1, 128); pP = psv(bq + 2, 1, 128)
                                for hl in range(4):
                                    mm(pQ[:, hl, :], QTc[:, hl, :], Qc[:, hl, :], True, True, [kQc, kQTc], pk(bq))
                                cp("act", Qn[:], pQ, pk(bq), [kQn])
                                if lvl < 6:
                                    for hl in range(4):
                                        mm(pQT[:, hl, :], Qc[:, hl, :], QTc[:, hl, :], True, True, [kQc, kQTc], pk(bq + 1))
                                    cp("dve", QTn[:], pQT, pk(bq + 1), [kQTn])
                                for hl in range(4):
                                    mm(pP[:, hl, :], ident_b[:], Pc[:, hl, :], True, False, ["ident_b", kPc], pk(bq + 2))
                                    mm(pP[:, hl, :], Qn[:, hl, :], Pc[:, hl, :], False, True, [kQn, kPc], pk(bq + 2))
                                if lvl < 6:
                                    cp("dve", Pn[:], pP, pk(bq + 2), [kPn])
                                else:
                                    cp("dve", TT[:, g4 * 4:g4 * 4 + 4, :], pP, pk(bq + 2), [("TT", g4)])
                                cur[gi] = n_
                    TTk = [("TT", g4) for g4 in range(4)]
                    MK = lambda nm: [(nm, g4) for g4 in range(4)]
                    pX = psflat(6, 2)
                    for h in range(16):
                        hc = slice(h * 64, (h + 1) * 64)
                        mm(pX[:, hc], fmA[:, h, 0, :], S0b[:, hc], True, False, ["fmA", "S0b"], pk(6 + h // 8))
                        mm(pX[:, hc], MakT[:, h, :], Ht["vH"][:, hc], False, True, MK("MakT") + ["vH"], pk(6 + h // 8))
                    cp("act", Ht["Xb"][:], pX, pk(6, 2), ["Xb"])
                    pU = psflat(0, 2)
                    for h in range(16):
                        hc = slice(h * 64, (h + 1) * 64)
                        mm(pU[:, hc], TT[:, h, :], Ht["Xb"][:, hc], True, True, TTk + ["Xb"], pk(h // 8))
                    cp("act", Ht["Ub"][:], pU, pk(0, 2), ["Ub"])
                    pY = psflat(2, 2)
                    for h in range(16):
                        hc = slice(h * 64, (h + 1) * 64)
                        mm(pY[:, hc], fmA[:, h, 1, :], S0b[:, hc], True, False, ["fmA", "S0b"], pk(2 + h // 8))
                        mm(pY[:, hc], MrbT[:, h, :], Ht["Ub"][:, hc], False, False, MK("MrbT") + ["Ub"], pk(2 + h // 8))
                        mm(pY[:, hc], MrkT[:, h, :], Ht["vH"][:, hc], False, True, MK("MrkT") + ["vH"], pk(2 + h // 8))
                    pS = psflat(4, 2)
                    for h in range(16):
                        hc = slice(h * 64, (h + 1) * 64)
                        mm(pS[0:64, hc], Ht["bC"][:, hc], Ht["Ub"][:, hc], True, False, ["bC", "Ub"], pk(4 + h // 8))
                        mm(pS[0:64, hc], Ht["kC"][:, hc], Ht["vH"][:, hc], False, True, ["kC", "vH"], pk(4 + h // 8))
                    S3 = S0f[:].rearrange("p (h d) -> p h d", d=64)
                    tt("dve", S3, S3, pC.unsqueeze(2).to_broadcast([64, 16, 64]), ALU.mult, ["S0f", "sm_pc"], ["S0f"])
                    tt("dve", S0f[:], S0f[:], pS[0:64, :], ALU.add, ["S0f"] + pk(4, 2), ["S0f"])
                    cp("act", S0b[:], S0f[:], ["S0f"], ["S0b"])
                    Yc, T2 = E1, E2
                    kYc, kT2 = kE1, kE2
                    red(sm[:, 112:128], v3(pY), pk(2, 2), ["sm_m"])
                    ts("dve", sm[:, 112:128], sm[:, 112:128], 1.0 / 64.0, None, ALU.mult, None, ["sm_m"], ["sm_m"])
                    tt("dve", v3(Yc[:]), v3(pY), bc16(sm[:, 112:128]), ALU.subtract, pk(2, 2) + ["sm_m"], [kYc])
                    act(T2[:], Yc[:], AF.Square, [kYc], [kT2])
                    red(sm[:, 128:144], v3(T2[:]), [kT2], ["sm_v"])
                    ts("dve", sm[:, 128:144], sm[:, 128:144], 1.0 / 64.0, 64e-5, ALU.mult, ALU.add, ["sm_v"], ["sm_v"])
                    act(sm[:, 128:144], sm[:, 128:144], AF.Sqrt, ["sm_v"], ["sm_v"])
                    op("dve", lambda e: e.reciprocal(out=sm[:, 128:144], in_=sm[:, 128:144]), ["sm_v"], ["sm_v"])
                    tt("dve", v3(Yc[:]), v3(Yc[:]), bc16(sm[:, 128:144]), ALU.mult, [kYc, "sm_v"], [kYc])
                    tt("pool", Yc[:], Yc[:], lg_bc[:], ALU.mult, [kYc, "bcs"], [kYc])
                    tt("pool", Yc[:], Yc[:], lb_bc[:], ALU.add, [kYc, "bcs"], [kYc])
                    tt("pool", T2[:], R[:], Kt[:], ALU.mult, [kR, kK], [kT2])
                    tt("pool", T2[:], T2[:], rk_bc[:], ALU.mult, [kT2, "bcs"], [kT2])
                    red(sm[:, 144:160], v3(T2[:]), [kT2], ["sm_b"])
                    tt("dve", v3(T2[:]), v3(V[:]), bc16(sm[:, 144:160]), ALU.mult, [kV, "sm_b", kT2], [kT2])
                    tt("pool", Yc[:], Yc[:], T2[:], ALU.add, [kYc, kT2], [kYc])
                    tt("dve", Ht["Yo"][:], Yc[:], G[:], ALU.mult, [kYc, kG], ["Yo"])
                    pty = psbf(6, 1).rearrange("p (c t) -> p c t", t=128)
                    yt_ = yTo[1]
                    transpose_tile(Ht["Yo"], 8, pty, ("ps", 6), yt_[:], "yTo1", "Yo", "act")
                    dma(Dm["yT"][1024:2048, r0:r0 + 128].rearrange("(c p) t -> p c t", p=128), yt_[:], r=["yTo1"], w=[])
                P.barrier()


        def stage_tail(layer):
            KC = 16 if layer == 0 else 8
            Wo = Dm["wb_eo"] if layer == 0 else Dm["wb_oo"]
            Wo_name = "wb_eo" if layer == 0 else "wb_oo"
            xsrc = Dm["x"] if layer == 0 else Dm["xcur"]
            xdst = Dm["xcur"] if layer == 0 else Dm["out"]
            with contextlib.ExitStack() as st:
                wb = WB(st, 3)
                XR = [sb(st, "XR%d" % i, [128, D]) for i in range(4)]
                xb = [sb(st, "xb%d" % i, [128, D], BF16) for i in range(2)]
                yTg = sb(st, "yTg", [128, KC, 512], BF16)
                x1T = sb(st, "x1T", [128, 8, 512], BF16)
                qT = sb(st, "qTx", [128, 8, 512], BF16)
                oT = sb(st, "oTx", [128, 8, 512], BF16)
                hT = sb(st, "hT", [128, 22, 512], BF16)
                PT = [sb(st, "PTx%d" % i, [128, 512], BF16) for i in range(2)]
                rden = sb(st, "rden", [128, 512])
                sg = [sb(st, "sg%d" % i, [128, 512]) for i in range(2)]
                memT = sb(st, "memT", [128, 8, 256], BF16)
                KmT = sb(st, "KmT", [128, 8, 256], BF16)
                Vm = sb(st, "Vm", [128, 2, D], BF16)
                sm = sb(st, "smx", [128, 16])
                lnp = {}
                for nm in ("ln_mix_g", "ln_mix_b", "ln_xa_g", "ln_xa_b", "ln_ffn_g", "ln_ffn_b"):
                    lnp[nm] = sb(st, "bc_" + nm, [128, D])
                    bc_load(lnp[nm][:], Dm[nm][layer], ["lnp"])
                PS = psum(st, "PS3", [128, 8, 512], F32)

                def pk(b0, nb=1):
                    return [("ps", b) for b in range(b0, b0 + nb)]
                pst = PS[:, 6, :].bitcast(BF16).rearrange("p (c t) -> p c t", t=128)

                for mt in range(2):
                    dma(XR[mt][:], Dm["mem"][mt * 128:(mt + 1) * 128, :], r=["mem"], w=["XR%d" % mt])
                    cp("act", xb[mt][:], XR[mt][:], ["XR%d" % mt], ["xb%d" % mt])
                    transpose_tile(xb[mt], 8, pst, ("ps", 6), memT[:, :, mt * 128:(mt + 1) * 128], "memT", "xb%d" % mt, "dve")
                for fc in range(8):
                    if fc % 4 == 0:
                        wt, wk = wb.load(Dm["wb_xkv"][layer], "wb_xkv", 0, 8, fc * 128, 512)
                    for kc in range(8):
                        mm(PS[:, 4 + fc % 2, 0:256], wt[:, kc, (fc % 4) * 128:(fc % 4 + 1) * 128], memT[:, kc, :], kc == 0, kc == 7, ["memT", wk], pk(4 + fc % 2))
                    cp("dve", KmT[:, fc, :], PS[:, 4 + fc % 2, 0:256], pk(4 + fc % 2), ["KmT"])
                for n in range(2):
                    wt, wk = wb.load(Dm["wb_xkv"][layer], "wb_xkv", 0, 8, 1024 + n * 512, 512)
                    for mt in range(2):
                        for kc in range(8):
                            mm(PS[:, mt, :], memT[:, kc, mt * 128:(mt + 1) * 128], wt[:, kc, :], kc == 0, kc == 7, ["memT", wk], pk(mt))
                        cp("act", Vm[:, mt, n * 512:(n + 1) * 512], PS[:, mt, :], pk(mt), ["Vm"])

                def gemm_res_ln(lhs_fn, lhs_keys, nk, Wd, wname, gname, bname):
                    kgs = [(k0, min(8, nk - k0)) for k0 in range(0, nk, 8)]
                    for n in range(2):
                        for gi, (k0, nkc) in enumerate(kgs):
                            wt, wk = wb.load(Wd, wname, k0, nkc, n * 512, 512)
                            for t in range(4):
                                for kc in range(nkc):
                                    mm(PS[:, t, :], lhs_fn(k0 + kc, t), wt[:, kc, :], gi == 0 and kc == 0, gi == len(kgs) - 1 and kc == nkc - 1,
                                       lhs_keys + [wk], pk(t))
                        for t in range(4):
                            ns = slice(n * 512, (n + 1) * 512)
                            stt(XR[t][:, ns], XR[t][:, ns], ALPHA, PS[:, t, :], ALU.mult, ALU.add, ["XR%d" % t] + pk(t), ["XR%d" % t])
                    for t in range(4):
                        layer_norm(XR[t][:], "XR%d" % t, XR[t][:], "XR%d" % t, lnp[gname][:], lnp[bname][:], ["lnp"], sm, "smx")

                def to_fm(dst, dkey):
                    for t in range(4):
                        i = t % 2
                        cp("act", xb[i][:], XR[t][:], ["XR%d" % t], ["xb%d" % i])
                        transpose_tile(xb[i], 8, pst, ("ps", 6), dst[:, :, t * 128:(t + 1) * 128], dkey, "xb%d" % i, "dve")

                for g in range(NG):
                    c0 = g * TG
                    dma(yTg[:], Dm["yT"][0:KC * 128, c0:c0 + TG].rearrange("(c p) t -> p c t", p=128), r=["yT"], w=["yTg"])
                    for t in range(4):
                        dma(XR[t][:], xsrc[c0 + t * 128:c0 + (t + 1) * 128, :], r=["xsrc"], w=["XR%d" % t])
                    gemm_res_ln(lambda kc, t: yTg[:, kc, t * 128:(t + 1) * 128], ["yTg"], KC, Wo, Wo_name, "ln_mix_g", "ln_mix_b")
                    to_fm(x1T, "x1T")
                    for fc in range(8):
                        if fc % 4 == 0:
                            wt, wk = wb.load(Dm["wb_xq"][layer], "wb_xq", 0, 8, fc * 128, 512)
                        for kc in range(8):
                            mm(PS[:, 4 + fc % 2, :], wt[:, kc, (fc % 4) * 128:(fc % 4 + 1) * 128], x1T[:, kc, :], kc == 0, kc == 7, ["x1T", wk], pk(4 + fc % 2))
                        cp("act" if fc % 2 else "dve", qT[:, fc, :], PS[:, 4 + fc % 2, :], pk(4 + fc % 2), ["qTx"])
                    for hh in range(4):
                        for mt in range(2):
                            for dc in range(2):
                                mm(PS[:, mt, :], KmT[:, 2 * hh + dc, mt * 128:(mt + 1) * 128], qT[:, 2 * hh + dc, :], dc == 0, dc == 1, ["KmT", "qTx"], pk(mt))
                            act(PT[mt][:], PS[:, mt, :], AF.Exp, pk(mt), ["PTx%d" % mt], scale=1.0 / 16.0)
                        for mt in range(2):
                            mm(PS[:, 2, :], ones_b[:], PT[mt][:], mt == 0, mt == 1, ["ones_b", "PTx%d" % mt], pk(2))
                        for dc in range(2):
                            bo = 3 if dc == 0 else 7
                            for mt in range(2):
                                mm(PS[:, bo, :], Vm[:, mt, (2 * hh + dc) * 128:(2 * hh + dc + 1) * 128], PT[mt][:], mt == 0, mt == 1, ["Vm", "PTx%d" % mt], pk(bo))
                        op("dve", lambda e: e.reciprocal(out=rden[:], in_=PS[:, 2, :]), pk(2), ["rden"])
                        for dc in range(2):
                            bo = 3 if dc == 0 else 7
                            tt("dve", oT[:, 2 * hh + dc, :], PS[:, bo, :], rden[:], ALU.mult, pk(bo) + ["rden"], ["oTx"])
                    gemm_res_ln(lambda kc, t: oT[:, kc, t * 128:(t + 1) * 128], ["oTx"], 8, Dm["wb_xo"][layer], "wb_xo", "ln_xa_g", "ln_xa_b")
                    to_fm(x1T, "x1T")
                    for j in range(22):
                        if j % 2 == 0:
                            i = wb.i
                            wb.i = (wb.i + 1) % len(wb.t)
                            wt = wb.t[i]
                            wk = "wbuf%d" % i
                            for half, cbase in ((0, 0), (1, FFN_H)):
                                dma(wt[:, :, half * 256:half * 256 + 256],
                                    Dm["wb_13"][layer][:, cbase + j * 128:cbase + j * 128 + 256].rearrange("(c p) f -> p c f", p=128),
                                    r=["wb_13"], w=[wk])
                        jl = j % 2
                        bg = 4 + 2 * jl
                        for kc in range(8):
                            mm(PS[:, bg, :], wt[:, kc, jl * 128:(jl + 1) * 128], x1T[:, kc, :], kc == 0, kc == 7, ["x1T", wk], pk(bg))
                        for kc in range(8):
                            mm(PS[:, bg + 1, :], wt[:, kc, 256 + jl * 128:256 + (jl + 1) * 128], x1T[:, kc, :], kc == 0, kc == 7, ["x1T", wk], pk(bg + 1))
                        act(sg[jl][:], PS[:, bg, :], AF.Silu, pk(bg), ["sg%d" % jl])
                        tt("dve", hT[:, j, :], sg[jl][:], PS[:, bg + 1, :], ALU.mult, ["sg%d" % jl] + pk(bg + 1), ["hT"])
                    gemm_res_ln(lambda kc, t: hT[:, kc, t * 128:(t + 1) * 128], ["hT"], 22, Dm["wb_2"][layer], "wb_2", "ln_ffn_g", "ln_ffn_b")
                    for t in range(4):
                        dma(xdst[c0 + t * 128:c0 + (t + 1) * 128, :], XR[t][:], r=["XR%d" % t], w=[])
                P.barrier()

        def stage_l1_proj():
            with contextlib.ExitStack() as st:
                wb = WB(st, 3)
                xt = [sb(st, "xt%d" % i, [128, D]) for i in range(2)]
                xb = [sb(st, "xb%d" % i, [128, D], BF16) for i in range(2)]
                xT = [sb(st, "xT%d" % i, [128, 8, 512], BF16) for i in range(2)]
                evb = [sb(st, "evb%d" % i, [128, 512], BF16) for i in range(4)]
                kms = sb(st, "kms", [128, 8, 16])
                pst = psum(st, "pst1", [128, 8, 128], BF16)
                ptm = psum(st, "ptm1", [128, 4, 512], F32)
                pfm = psum(st, "pfm1", [128, 2, 512], F32)
                evi = [0]

                def next_ev():
                    i = evi[0]
                    evi[0] = (i + 1) % 4
                    return evb[i], "evb%d" % i
                for g in range(NG):
                    xTg = xT[g % 2]
                    xk = "xT%d" % (g % 2)
                    for t in range(4):
                        i = t % 2
                        row0 = g * TG + t * 128
                        dma(xt[i][:], Dm["xcur"][row0:row0 + 128, :], r=["xcur"], w=["xt%d" % i])
                        cp("act", xb[i][:], xt[i][:], ["xt%d" % i], ["xb%d" % i])
                        transpose_tile(xb[i], 8, pst, "pst", xTg[:, :, t * 128:(t + 1) * 128], xk, "xb%d" % i, "dve")
                    for fc in range(16):
                        if fc % 4 == 0:
                            wt, wk = wb.load(Dm["wb_qkv"], "wb_qkv", 0, 8, fc * 128, 512)
                        pb = fc % 2
                        for kc in range(8):
                            mm(pfm[:, pb, :], wt[:, kc, (fc % 4) * 128:(fc % 4 + 1) * 128], xTg[:, kc, :], kc == 0, kc == 7, [xk, wk], [("pfm", pb)])
                        e_t, e_k = next_ev()
                        cp("act" if fc % 2 else "dve", e_t[:], pfm[:, pb, :], [("pfm", pb)], [e_k])
                        if fc < 8:
                            dma(Dm["qT"][fc * 128:(fc + 1) * 128, g * TG:(g + 1) * TG], e_t[:], r=[e_k], w=[])
                        else:
                            dma(Dm["kT"][(fc - 8) * 128:(fc - 7) * 128, g * TG:(g + 1) * TG], e_t[:], r=[e_k], w=[])
                            red(kms[:, fc - 8, 2 * g:2 * g + 2], pfm[:, pb, :].rearrange("p (b t) -> p b t", t=256), [("pfm", pb)], ["kms"])
                    for n in range(2):
                        wt, wk = wb.load(Dm["wb_qkv"], "wb_qkv", 0, 8, 2048 + n * 512, 512)
                        for t in range(4):
                            for kc in range(8):
                                mm(ptm[:, t, :], xTg[:, kc, t * 128:(t + 1) * 128], wt[:, kc, :], kc == 0, kc == 7, [xk, wk], [("ptm", t)])
                            e_t, e_k = next_ev()
                            cp("act" if t % 2 else "dve", e_t[:], ptm[:, t, :], [("ptm", t)], [e_k])
                            row0 = g * TG + t * 128
                            dma(Dm["vv"][row0:row0 + 128, n * 512:(n + 1) * 512], e_t[:], r=[e_k], w=[])
                ts("dve", kms[:], kms[:], 1.0 / 256.0, None, ALU.mult, None, ["kms"], ["kms"])
                dma(Dm["kmT"], kms[:], r=["kms"], w=[])
                P.barrier()

        def stage_l1_moba():
            with contextlib.ExitStack() as st:
                KT = sb(st, "KT", [128, 8, S], BF16)
                VV = sb(st, "VV", [128, NT, D], BF16)
                qraw = sb(st, "qraw", [128, 8, 512], BF16)
                qm = sb(st, "qm", [128, 16, 512], BF16)
                kmf = sb(st, "kmf", [128, 8, 16])
                kmb = sb(st, "kmb", [128, 8, 16], BF16)
                Esel = sb(st, "Esel", [128, 16, 128], BF16)
                gsb = sb(st, "gsb", [128, 16, 16])
                t8 = sb(st, "t8", [128, 16, 8])
                thr = sb(st, "thr", [128, 16])
                sel = sb(st, "sel", [128, 16, 16])
                bpad = sb(st, "bpad", [128, 16, 128], BF16)
                biasT = sb(st, "biasT", [128, 16, 512], BF16)
                PTm = [sb(st, "PTm%d" % i, [128, 512], BF16) for i in range(3)]
                rden = [sb(st, "rdenm%d" % i, [64, 512]) for i in range(2)]
                oTs = [sb(st, "oTs%d" % i, [64, 512], BF16) for i in range(2)]
                PS = psum(st, "PS5", [128, 8, 512], F32)

                def pk(b0, nb=1):
                    return [("ps", b) for b in range(b0, b0 + nb)]
                dma(kmf[:], Dm["kmT"], r=["kmT"], w=["kmf"])
                cp("dve", kmb[:], kmf[:], ["kmf"], ["kmb"])
                op("pool", lambda e: e.memset(Esel[:], 0.0), [], ["Esel"])
                op("pool", lambda e: e.memset(bpad[:], 0.0), [], ["bpad"])
                op("pool", lambda e: e.memset(qm[:], 0.0), [], ["qm"])
                for n in range(16):
                    ts("dve", Esel[0:16, n, :], ones_f[0:16, :], ident_f[0:16, n:n + 1], None, ALU.mult, None, ["ones_f", "ident_f", "Esel"], ["Esel"])
                MG = int(os.environ.get("MK_MG", NG))
                for G in range(MG):
                    c0 = G * TG
                    dma(qraw[:], Dm["qT"][:, c0:c0 + TG].rearrange("(c p) t -> p c t", p=128), r=["qT"], w=["qraw"])
                    dma(KT[:, :, c0:c0 + TG], Dm["kT"][:, c0:c0 + TG].rearrange("(c p) t -> p c t", p=128), r=["kT"], w=[("KT", G)])
                    dma(VV[:, 4 * G:4 * G + 4, :], Dm["vv"][c0:c0 + TG, :].rearrange("(t p) f -> p t f", p=128), r=["vv"], w=[("VV", G)])
                    KTk = [("KT", gg_) for gg_ in range(G + 1)]
                    VVk = [("VV", gg_) for gg_ in range(G + 1)]
                    qm4 = qm[:].rearrange("p (a b) t -> p a b t", b=2)
                    cp("dve", qm4[0:64, :, 0, :], qraw[0:64, :, :], ["qraw"], ["qm"])
                    cp("pool", qm4[64:128, :, 1, :], qraw[64:128, :, :], ["qraw"], ["qm"])
                    for c in range(4):
                        own = 2 * G + c // 2
                        pg = PS[:, 7, 0:256].rearrange("p (h n) -> p h n", n=16)
                        for h in range(16):
                            mm(pg[:, h, :], qm[:, h, c * 128:(c + 1) * 128], kmb[:, h // 2, :], True, True, ["qm", "kmb"], pk(7))
                        op("pool", lambda e: e.memset(gsb[:], -1e30), [], ["gsb"])
                        if own > 0:
                            cp("dve", gsb[:, :, 0:own], pg[:, :, 0:own], pk(7), ["gsb"])
                        for h in range(16):
                            op("dve", lambda e, h=h: e.max(out=t8[:, h, :], in_=gsb[:, h, :]), ["gsb"], ["t8"])
                        ts("dve", thr[:], t8[:, :, 2], -1e29, None, ALU.max, None, ["t8"], ["thr"])
                        tt("dve", sel[:], gsb[:], thr[:].unsqueeze(2).to_broadcast([128, 16, 16]), ALU.is_ge, ["gsb", "thr"], ["sel"])
                        op("dve", lambda e, own=own: e.memset(sel[:, :, own:own + 1], 1.0), [], ["sel"])
                        ts("dve", bpad[:, :, 0:16], sel[:], 30000.0, -30000.0, ALU.mult, ALU.add, ["sel"], ["bpad"])
                        for half in range(2):
                            pbt = PS[:, 5 + half, :].bitcast(BF16).rearrange("p (h t) -> p h t", t=128)
                            for hl in range(8):
                                h = half * 8 + hl
                                tp(pbt[:, hl, :], bpad[:, h, :], ident_b[:], ["bpad", "ident_b"], pk(5 + half))
                            cp("act", biasT[:, half * 8:half * 8 + 8, c * 128:(c + 1) * 128], pbt, pk(5 + half), ["biasT"])
                    nkt = 4 * G + 4
                    steps = [(h, kt) for h in range(16) for kt in range(nkt)]

                    def issue_scores(i):
                        h, kt = steps[i]
                        pr = h // 2
                        col0 = max(0, kt - 4 * G) * 128
                        bs = i % 2
                        mm(PS[:, bs, col0:512], KT[:, pr, kt * 128:(kt + 1) * 128], qm[:, h, col0:512], True, False, KTk + ["qm"], pk(bs))
                        mm(PS[:, bs, col0:512], Esel[:, kt // 2, :], biasT[:, h, col0:512], False, True, ["Esel", "biasT"], pk(bs))

                    issue_scores(0)
                    for i, (h, kt) in enumerate(steps):
                        j = kt - 4 * G
                        col0 = max(0, j) * 128
                        bs = i % 2
                        pt = PTm[i % 3]
                        ptk = "PTm%d" % (i % 3)
                        if i + 1 < len(steps):
                            issue_scores(i + 1)
                        act(pt[:, col0:512], PS[:, bs, col0:512], AF.Exp, pk(bs), [ptk], scale=0.125)
                        if j >= 0:
                            tt("pool", pt[:, col0:col0 + 128], pt[:, col0:col0 + 128], le_f[:], ALU.mult, [ptk, "le_f"], [ptk])
                        ob = 2 + 2 * (h % 2)
                        mm(PS[0:64, ob, col0:512], VV[:, kt, h * 64:(h + 1) * 64], pt[:, col0:512], kt == 0, kt == nkt - 1, VVk + [ptk], pk(ob))
                        mm(PS[0:64, ob + 1, col0:512], ones_b[:, 0:64], pt[:, col0:512], kt == 0, kt == nkt - 1, ["ones_b", ptk], pk(ob + 1))
                        if kt == nkt - 1:
                            rd = rden[h % 2]
                            rdk = "rdenm%d" % (h % 2)
                            op("dve", lambda e, rd=rd, ob=ob: e.reciprocal(out=rd[:], in_=PS[0:64, ob + 1, :]), pk(ob + 1), [rdk])
                            ot = oTs[h % 2]
                            otk = "oTs%d" % (h % 2)
                            tt("dve", ot[:], PS[0:64, ob, :], rd[:], ALU.mult, pk(ob) + [rdk], [otk])
                            dma(Dm["yT"][h * 64:(h + 1) * 64, c0:c0 + TG], ot[:], r=[otk], w=[])
                P.barrier()

        ONLY5 = os.environ.get("MK_ONLY5") == "1"
        if not ONLY5:
            stage_w()
        if upto >= 1 and not ONLY5:
            stage_l0_proj()
        if upto >= 2 and not ONLY5:
            stage_l0_scan()
        if upto >= 3 and not ONLY5:
            stage_tail(0)
        if upto >= 4 and not ONLY5:
            stage_l1_proj()
        if upto >= 5:
            stage_l1_moba()
        if upto >= 6:
            stage_tail(1)
        for nm in dumps:
            dma(Dm["dump_" + nm], Dm[nm], r=[], w=[])
        P.barrier()
        P.emit()
    return nc


def make_in_maps(inputs, n_cores=8):
    def a(x):
        return np.ascontiguousarray(np.asarray(x, dtype=np.float32))
    i = inputs
    shared = {
        "w_in": a(i["even_w_in"][0]), "conv_w": a(i["ssd_conv_w"][0]), "conv_b": a(i["ssd_conv_b"]).reshape(1, 1536),
        "dt_bias": a(i["ssd_dt_bias"]).reshape(1, 16), "a_log": a(i["ssd_a_log"]).reshape(1, 16), "ssd_d": a(i["ssd_d"]).reshape(1, 16),
        "ssd_ng": a(i["ssd_norm_g"]).reshape(1, D), "mu": a(i["rwkv_mu"]).reshape(1, RW_IN), "w0": a(i["rwkv_w0"]).reshape(1, D),
        "w2l": a(i["rwkv_w2"][0]), "a0": a(i["rwkv_a0"]).reshape(1, D), "a2l": a(i["rwkv_a2"][0]), "g2l": a(i["rwkv_g2"][0]),
        "k_k": a(i["rwkv_k_k"]).reshape(1, D), "k_a": a(i["rwkv_k_a"]).reshape(1, D), "r_k": a(i["rwkv_r_k"]).reshape(1, D),
        "lnx_g": a(i["rwkv_lnx_g"]).reshape(1, D), "lnx_b": a(i["rwkv_lnx_b"]).reshape(1, D), "w_eo": a(i["even_w_out"][0]),
        "w_qkv": a(i["odd_w_qkv"][0]), "w_oo": a(i["odd_w_out"][0]), "ln_mix_g": a(i["ln_mix_g"]), "ln_mix_b": a(i["ln_mix_b"]),
        "xa_wq": a(i["xa_wq"]), "xa_wkv": a(i["xa_wkv"]), "xa_wo": a(i["xa_wo"]), "ln_xa_g": a(i["ln_xa_g"]), "ln_xa_b": a(i["ln_xa_b"]),
        "w13": a(i["ffn_w13"]), "w2": a(i["ffn_w2"]), "ln_ffn_g": a(i["ln_ffn_g"]), "ln_ffn_b": a(i["ln_ffn_b"]),
    }
    maps = []
    for c in range(n_cores):
        b = c % 4
        m = dict(shared)
        m["x"] = a(i["x"][b])
        m["mem"] = a(i["mem"][b])
        maps.append(m)
    return maps


def kernel(**inputs):
    nc = build_program()
    in_maps = make_in_maps(inputs, 8)
    res = run_bass_kernel_spmd(nc, in_maps, core_ids=list(range(8)))
    out = np.stack([np.asarray(res.results[b]["out"], dtype=np.float32) for b in range(4)], axis=0)
    return out
```

```python
import contextlib
import numpy as np
import concourse.bass as bass
import concourse.mybir as mybir
from concourse.bass_utils import run_bass_kernel_spmd

F32 = mybir.dt.float32
BF16 = mybir.dt.bfloat16
AF = mybir.ActivationFunctionType
ALU = mybir.AluOpType
AX = mybir.AxisListType

ENGS = ["pe", "act", "dve", "pool", "sp"]
N_DMA_SEMS = 24


class Prog:
    def __init__(self, nc, stack):
        self.nc = nc
        self.stack = stack
        self.q = {e: [] for e in ENGS}
        self.cnt = {e: 0 for e in ENGS}
        self.known = {e: {} for e in ENGS}
        self.res = {}
        self.sems = {}
        for e in ENGS:
            self.sems[e] = stack.enter_context(nc.semaphore("s_" + e))
        self.dma_val = [0] * N_DMA_SEMS
        self.dma_pool = {"sp": list(range(0, 14)), "pool": list(range(14, 20)), "act": list(range(20, 24))}
        self.dma_rr = {"sp": 0, "pool": 0, "act": 0}
        for i in range(N_DMA_SEMS):
            self.sems[("d", i)] = stack.enter_context(nc.semaphore("s_d%d" % i))

    def _deps(self, reads, writes):
        need = {}

        def add(sv):
            if sv is None:
                return
            k, v = sv
            if need.get(k, 0) < v:
                need[k] = v
        for r in reads:
            st = self.res.get(r)
            if st is not None:
                add(st[0])
        for w in writes:
            st = self.res.get(w)
            if st is not None:
                add(st[0])
                for k, v in st[1].items():
                    add((k, v))
        return need

    def _mark(self, reads, writes, done):
        for r in reads:
            st = self.res.setdefault(r, [None, {}])
            if st[1].get(done[0], 0) < done[1]:
                st[1][done[0]] = done[1]
        for w in writes:
            self.res[w] = [done, {}]

    def _waits(self, eng, need, pe_chain=False):
        out = []
        kn = self.known[eng]
        for k, v in need.items():
            if k == eng and pe_chain:
                continue
            if kn.get(k, 0) >= v:
                continue
            kn[k] = v
            out.append((k, v))
        return out

    @staticmethod
    def _is_psum(k):
        return k == "pst" or (isinstance(k, tuple) and k[0] in ("ps", "ptm", "pfm"))

    def op(self, eng, fn, reads=(), writes=(), pe_chain=False):
        extra = [k for k in reads if self._is_psum(k) and k not in writes]
        if extra:
            writes = list(writes) + extra
        need = self._deps(reads, writes)
        waits = self._waits(eng, need, pe_chain)
        self.cnt[eng] += 1
        done = (eng, self.cnt[eng])
        self.q[eng].append((waits, fn, (eng, 1)))
        self._mark(reads, writes, done)
        return done

    def dma(self, fn, reads=(), writes=(), eng="sp"):
        need = self._deps(reads, writes)
        pool = self.dma_pool[eng]
        i = pool[self.dma_rr[eng]]
        self.dma_rr[eng] = (self.dma_rr[eng] + 1) % len(pool)
        key = ("d", i)
        if self.dma_val[i] > 0:
            if need.get(key, 0) < self.dma_val[i]:
                need[key] = self.dma_val[i]
        waits = self._waits(eng, need)
        self.dma_val[i] += 16
        done = (key, self.dma_val[i])
        self.q[eng].append((waits, fn, (key, 16)))
        self._mark(reads, writes, done)
        return done

    def barrier(self):
        state = {e: self.cnt[e] for e in ENGS if self.cnt[e] > 0}
        for i in range(N_DMA_SEMS):
            if self.dma_val[i] > 0:
                state[("d", i)] = self.dma_val[i]
        for e in ENGS:
            waits = self._waits(e, {k: v for k, v in state.items() if k != e})
            if waits:
                self.q[e].append((waits, None, None))
        self.res = {}

    def emit_flush(self):
        self.emit()
        self.q = {e: [] for e in ENGS}

    def emit(self):
        nc = self.nc
        sems = self.sems
        q = self.q
        with nc.Block() as block:
            def run(eng_name):
                def body(eng):
                    for waits, fn, inc in q[eng_name]:
                        for k, v in waits:
                            eng.wait_ge(sems[k], v)
                        if fn is not None:
                            ins = fn(eng)
                            ins.then_inc(sems[inc[0]], inc[1])
                return body
            block.tensor(run("pe"))
            block.scalar(run("act"))
            block.vector(run("dve"))
            block.gpsimd(run("pool"))
            block.sync(run("sp"))


D = 1024
S = 4096
NT = 32
TG = 512
NG = 8
HN = 8
HW = 512
Z0, XS0, B0, C0, DT0, RW0 = 0, 512, 1024, 1152, 1280, 1288
RW_L = 1792
EVEN_L = RW0 + RW_L
FFN_H = 2816
ALPHA = 4.0 ** 0.25
SC_W = -0.6065306597126334
SH = S // 2

IN_SPECS = [
    ("x", [S, D]), ("x_tail", [SH, D]), ("mem", [256, D]), ("w_in", [D, EVEN_L]), ("conv_w", [4, 768]), ("conv_b", [1, 768]),
    ("dt_bias", [1, HN]), ("a_log", [1, HN]), ("ssd_d", [1, HN]), ("ssd_ng", [1, HW]), ("mu", [1, RW_L]),
    ("w0", [1, HW]), ("w2l", [64, HW]), ("a0", [1, HW]), ("a2l", [64, HW]), ("g2l", [128, HW]), ("k_k", [1, HW]),
    ("k_a", [1, HW]), ("r_k", [1, HW]), ("lnx_g", [1, HW]), ("lnx_b", [1, HW]), ("w_eo", [2048, D]),
    ("w_qkv", [D, 3 * HW]), ("w_oo", [D, D]), ("ln_mix_g", [2, D]), ("ln_mix_b", [2, D]),
    ("xa_wq", [2, D, D]), ("xa_wkv", [2, D, 2048]), ("xa_wo", [2, D, D]), ("ln_xa_g", [2, D]), ("ln_xa_b", [2, D]),
    ("w13", [2, D, 2 * FFN_H]), ("w2", [2, FFN_H, D]), ("ln_ffn_g", [2, D]), ("ln_ffn_b", [2, D]),
]

import os
CUT = int(os.environ.get('MK_CUT', '99'))


def build_program(upto=99, dumps=()):
    nc = bass.Bass("TRN2", target_bir_lowering=False, num_devices=8)
    Dm = {}
    for name, shape in IN_SPECS:
        Dm[name] = nc.dram_tensor(name, shape, F32, kind="ExternalInput").ap()
    Dm["out"] = nc.dram_tensor("out", [SH, D], F32, kind="ExternalOutput").ap()

    def internal(name, shape, dt, shared=False):
        Dm[name] = nc.dram_tensor(name, shape, dt, kind="Internal", addr_space=("Shared" if shared else "Local")).ap()

    internal("wb_in_a", [D, EVEN_L], BF16)
    internal("wb_in_b", [D, RW_L], BF16)
    internal("wb_eo", [2048, D], BF16)
    internal("wb_qkv", [D, 3 * HW], BF16)
    internal("wb_oo", [D, D], BF16)
    internal("wb_xq", [2, D, D], BF16)
    internal("wb_xkv", [2, D, 2048], BF16)
    internal("wb_xo", [2, D, D], BF16)
    internal("wb_13", [2, D, 2 * FFN_H], BF16)
    internal("wb_2", [2, FFN_H, D], BF16)
    internal("wb_w2l", [64, HW], BF16)
    internal("wb_a2l", [64, HW], BF16)
    internal("wb_g2l", [128, HW], BF16)
    internal("zs", [S, HW], F32)
    internal("xbc", [768, 3 + S], F32)
    internal("dts", [S, HN], F32)
    internal("rkv", [S, 3 * HW], F32)
    internal("lw", [S, HW], F32)
    internal("aa", [S, HW], F32)
    internal("gg", [S, HW], F32)
    internal("xloc", [SH, D], F32)
    internal("qT", [HW, S], BF16)
    internal("kT", [HW, S], BF16)
    internal("vv", [S, HW], BF16)
    internal("kmT", [128, 4, 16], F32)
    internal("yTs", [2, 2, 1024, SH], BF16, shared=True)
    internal("oTs", [2, 2, HW, SH], BF16, shared=True)
    internal("xs_sh", [2, SH, D], F32, shared=True)
    internal("yT_loc", [2, 1024, SH], BF16)
    internal("oT_loc", [2, HW, SH], BF16)
    internal("yT_in", [2, 1024, SH], BF16)
    internal("oT_in", [2, HW, SH], BF16)
    for nm in dumps:
        shp, dt = {"xloc": ([SH, D], F32), "lw": ([S, HW], F32), "rkv": ([S, 3 * HW], F32), "zs": ([S, HW], F32)}[nm]
        Dm["dump_" + nm] = nc.dram_tensor("dump_" + nm, shp, dt, kind="ExternalOutput").ap()

    with contextlib.ExitStack() as top:
        P = Prog(nc, top)

        uid = [0]

        def sb(st, name, shape, dt=F32):
            uid[0] += 1
            return st.enter_context(nc.sbuf_tensor("%s_%d" % (name, uid[0]), shape, dt))

        def psum(st, name, shape, dt=F32):
            uid[0] += 1
            return st.enter_context(nc.psum_tensor("%s_%d" % (name, uid[0]), shape, dt))

        def op(eng, fn, r=(), w=()):
            return P.op(eng, fn, reads=r, writes=w, pe_chain=(eng == "pe"))

        def dma(out, in_, r=(), w=(), eng="sp"):
            return P.dma(lambda e: e.dma_start(out=out, in_=in_), reads=r, writes=w, eng=eng)

        def dma_slow(out, in_, r=(), w=(), eng="sp"):
            return P.dma(lambda e: e.dma_start(out=out, in_=in_, allow_slow_non_contiguous=True), reads=r, writes=w, eng=eng)

        def dma_dyn(fn, r=(), w=()):
            def go(e):
                me = e.partition_id() % 2
                o, i = fn(me)
                return e.dma_start(out=o, in_=i)
            return P.dma(go, reads=r, writes=w, eng="pool")

        def pair_barrier():
            P.barrier()
            P.emit_flush()
            nc.all_core_barrier()

        def mm(out, lhsT, rhs, start, stop, r, w):
            return op("pe", lambda e: e.matmul(out, lhsT=lhsT, rhs=rhs, start=start, stop=stop), r, w)

        def tp(out, in_, ident, r, w):
            return op("pe", lambda e: e.transpose(out=out, in_=in_, identity=ident), r, w)

        def act(out, in_, func, r, w, bias=None, scale=None, accum_out=None):
            kw = {}
            if bias is not None:
                kw["bias"] = bias
            if scale is not None:
                kw["scale"] = scale
            if accum_out is not None:
                kw["accum_out"] = accum_out
            return op("act", lambda e: e.activation(out=out, in_=in_, func=func, **kw), r, w)

        def tt(eng, out, in0, in1, alu, r, w):
            return op(eng, lambda e: e.tensor_tensor(out=out, in0=in0, in1=in1, op=alu), r, w)

        def ts(eng, out, in0, s1, s2, op0, op1, r, w):
            if op1 is None:
                return op(eng, lambda e: e.tensor_scalar(out=out, in0=in0, scalar1=s1, scalar2=None, op0=op0), r, w)
            return op(eng, lambda e: e.tensor_scalar(out=out, in0=in0, scalar1=s1, scalar2=s2, op0=op0, op1=op1), r, w)

        def stt(out, in0, scalar, in1, op0, op1, r, w):
            return op("dve", lambda e: e.scalar_tensor_tensor(out=out, in0=in0, scalar=scalar, in1=in1, op0=op0, op1=op1), r, w)

        def cp(eng, out, in_, r, w):
            if eng == "act":
                return act(out, in_, AF.Copy, r, w)
            return op(eng, lambda e: e.tensor_copy(out=out, in_=in_), r, w)

        def red(out, in_, r, w, alu=ALU.add):
            return op("dve", lambda e: e.tensor_reduce(out=out, in_=in_, axis=AX.X, op=alu), r, w)

        def bc_load(tile, vec_row_ap, w):
            return dma(tile, vec_row_ap.partition_broadcast(128), r=(), w=w)

        ident_b = sb(top, "ident_b", [128, 128], BF16)
        ident_f = sb(top, "ident_f", [128, 128], F32)
        ones_b = sb(top, "ones_b", [128, 128], BF16)
        ones_f = sb(top, "ones_f", [128, 128], F32)
        le_f = sb(top, "le_f", [128, 128], F32)
        lt_f = sb(top, "lt_f", [128, 128], F32)
        gt_f = sb(top, "gt_f", [128, 128], F32)
        op("pool", lambda e: e.memset(ones_f[:], 1.0), w=["ones_f"])
        op("pool", lambda e: e.memset(ones_b[:], 1.0), w=["ones_b"])
        op("pool", lambda e: e.memset(ident_f[:], 0.0), w=["ident_f"])
        op("pool", lambda e: e.affine_select(out=ident_f[:], in_=ident_f[:], pattern=[[-1, 128]], compare_op=ALU.not_equal,
                                             fill=1.0, base=0, channel_multiplier=1), r=["ident_f"], w=["ident_f"])
        op("pool", lambda e: e.tensor_copy(out=ident_b[:], in_=ident_f[:]), r=["ident_f"], w=["ident_b"])
        op("pool", lambda e: e.affine_select(out=le_f[:], in_=ones_f[:], pattern=[[1, 128]], compare_op=ALU.is_ge,
                                             fill=0.0, base=0, channel_multiplier=-1), r=["ones_f"], w=["le_f"])
        op("pool", lambda e: e.affine_select(out=lt_f[:], in_=ones_f[:], pattern=[[1, 128]], compare_op=ALU.is_gt,
                                             fill=0.0, base=0, channel_multiplier=-1), r=["ones_f"], w=["lt_f"])
        op("pool", lambda e: e.affine_select(out=gt_f[:], in_=ones_f[:], pattern=[[-1, 128]], compare_op=ALU.is_gt,
                                             fill=0.0, base=0, channel_multiplier=1), r=["ones_f"], w=["gt_f"])
        CONST_R = ["ident_b", "ident_f", "ones_b", "ones_f", "le_f", "lt_f", "gt_f"]

        def cast_w(dst, src, rows, name):
            for r0 in range(0, rows, 128):
                r1 = min(rows, r0 + 128)
                dma(dst[r0:r1], src[r0:r1], w=[], eng="pool")

        def stage_w():
            with contextlib.ExitStack() as st:
                cast_w(Dm["wb_in_a"][:, 0:RW0], Dm["w_in"][:, 0:RW0], D, "wb_in_a")
                cast_w(Dm["wb_eo"], Dm["w_eo"], 2048, "wb_eo")
                cast_w(Dm["wb_qkv"], Dm["w_qkv"], D, "wb_qkv")
                cast_w(Dm["wb_oo"], Dm["w_oo"], D, "wb_oo")
                for l in range(2):
                    cast_w(Dm["wb_xq"][l], Dm["xa_wq"][l], D, "wb_xq")
                    cast_w(Dm["wb_xkv"][l], Dm["xa_wkv"][l], D, "wb_xkv")
                    cast_w(Dm["wb_xo"][l], Dm["xa_wo"][l], D, "wb_xo")
                    cast_w(Dm["wb_13"][l], Dm["w13"][l], D, "wb_13")
                    cast_w(Dm["wb_2"][l], Dm["w2"][l], FFN_H, "wb_2")
                cast_w(Dm["wb_w2l"], Dm["w2l"], 64, "wb_w2l")
                cast_w(Dm["wb_a2l"], Dm["a2l"], 64, "wb_a2l")
                cast_w(Dm["wb_g2l"], Dm["g2l"], 128, "wb_g2l")
                mu_bc = sb(st, "mu_bc", [128, RW_L])
                omu_bc = sb(st, "omu_bc", [128, RW_L])
                bc_load(mu_bc[:], Dm["mu"][0], ["mu_bc"])
                ts("dve", omu_bc[:], mu_bc[:], -1.0, 1.0, ALU.mult, ALU.add, ["mu_bc"], ["omu_bc"])
                wf = [sb(st, "wf%d" % i, [128, RW_L]) for i in range(2)]
                wa = [sb(st, "wa%d" % i, [128, RW_L], BF16) for i in range(2)]
                wbt = [sb(st, "wbt%d" % i, [128, RW_L], BF16) for i in range(2)]
                for kc in range(8):
                    i = kc % 2
                    dma(wf[i][:], Dm["w_in"][kc * 128:(kc + 1) * 128, RW0:EVEN_L], w=["wf%d" % i])
                    tt("dve", wbt[i][:], wf[i][:], mu_bc[:], ALU.mult, ["wf%d" % i, "mu_bc"], ["wbt%d" % i])
                    tt("pool", wa[i][:], wf[i][:], omu_bc[:], ALU.mult, ["wf%d" % i, "omu_bc"], ["wa%d" % i])
                    dma(Dm["wb_in_b"][kc * 128:(kc + 1) * 128, :], wbt[i][:], r=["wbt%d" % i], w=[])
                    dma(Dm["wb_in_a"][kc * 128:(kc + 1) * 128, RW0:EVEN_L], wa[i][:], r=["wa%d" % i], w=[])
                zt = sb(st, "zt", [128, 6, 3])
                op("dve", lambda e: e.memset(zt[:], 0.0), w=["zt"])
                dma(Dm["xbc"][:, 0:3].rearrange("(c p) t -> p c t", p=128), zt[:], r=["zt"], w=[])
                P.barrier()

        class WB:
            def __init__(self, st, n=3):
                self.t = [sb(st, "wbuf%d" % i, [128, 8, 512], BF16) for i in range(n)]
                self.i = 0

            def load(self, Wd, wname, r0, nkc, c0, ncols):
                i = self.i
                self.i = (self.i + 1) % len(self.t)
                t = self.t[i]
                dma(t[:, 0:nkc, 0:ncols],
                    Wd[r0 * 128:(r0 + nkc) * 128, c0:c0 + ncols].rearrange("(c p) f -> p c f", p=128),
                    r=[wname], w=["wbuf%d" % i])
                return t, "wbuf%d" % i

        def transpose_tile(src_bf, nblk, pst, pst_key, dst_view, dst_key, src_key, evac_eng):
            for c in range(nblk):
                tp(pst[:, c, :], src_bf[:, c * 128:(c + 1) * 128], ident_b[:], [src_key, "ident_b"], [pst_key])
            cp(evac_eng, dst_view, pst[:, 0:nblk, :], [pst_key], [dst_key])

        def layer_norm(s_ap, skey, out_ap, okey, g_bc, b_bc, gkeys, sm, smkey):
            op("dve", lambda e: e.bn_stats(out=sm[:, 0:6], in_=s_ap[:, 0:512]), [skey], [smkey])
            op("dve", lambda e: e.bn_stats(out=sm[:, 6:12], in_=s_ap[:, 512:1024]), [skey], [smkey])
            op("dve", lambda e: e.bn_aggr(out=sm[:, 12:14], in_=sm[:, 0:12]), [smkey], [smkey])
            ts("dve", sm[:, 14:15], sm[:, 13:14], 1e-5, None, ALU.add, None, [smkey], [smkey])
            act(sm[:, 14:15], sm[:, 14:15], AF.Sqrt, [smkey], [smkey])
            op("dve", lambda e: e.reciprocal(out=sm[:, 14:15], in_=sm[:, 14:15]), [smkey], [smkey])
            ts("dve", out_ap, s_ap, sm[:, 12:13], sm[:, 14:15], ALU.subtract, ALU.mult, [skey, smkey], [okey])
            tt("pool", out_ap, out_ap, g_bc, ALU.mult, [okey] + gkeys, [okey])
            tt("pool", out_ap, out_ap, b_bc, ALU.add, [okey] + gkeys, [okey])

        def stage_l0_proj():
            with contextlib.ExitStack() as st:
                wb = WB(st, 3)
                xt = [sb(st, "xt%d" % i, [128, D]) for i in range(2)]
                xb = [sb(st, "xb%d" % i, [128, D], BF16) for i in range(2)]
                xT = [sb(st, "xT%d" % i, [128, 8, 513], BF16) for i in range(2)]
                ev = [sb(st, "ev%d" % i, [128, 512]) for i in range(4)]
                lo1 = sb(st, "lo1", [128, 512], BF16)
                lo2 = sb(st, "lo2", [128, 512], BF16)
                wa2 = sb(st, "wa2", [128, HW], BF16)
                g2s = sb(st, "g2s", [128, HW], BF16)
                w0_bc = sb(st, "w0_bc", [128, HW])
                a0_bc = sb(st, "a0_bc", [128, HW])
                pst = psum(st, "pst", [128, 8, 128], BF16)
                ptm = psum(st, "ptm", [128, 4, 512], F32)
                pfm = psum(st, "pfm", [128, 2, 512], F32)
                dma(wa2[0:64, :], Dm["wb_w2l"], r=["wb_w2l"], w=["wa2"])
                dma(wa2[64:128, :], Dm["wb_a2l"], r=["wb_a2l"], w=["wa2"])
                dma(g2s[:], Dm["wb_g2l"], r=["wb_g2l"], w=["g2s"])
                bc_load(w0_bc[:], Dm["w0"][0], ["w0_bc"])
                bc_load(a0_bc[:], Dm["a0"][0], ["a0_bc"])
                evi = [0]

                def next_ev():
                    i = evi[0]
                    evi[0] = (i + 1) % 4
                    return ev[i], "ev%d" % i

                for g in range(NG):
                    xTg = xT[g % 2]
                    xk = "xT%d" % (g % 2)
                    if g == 0:
                        op("dve", lambda e, xTg=xTg: e.memset(xTg[:, :, 0:1], 0.0), w=[xk])
                    else:
                        cp("dve", xTg[:, :, 0:1], xT[(g - 1) % 2][:, :, 512:513], ["xT%d" % ((g - 1) % 2)], [xk])
                    for t in range(4):
                        i = t % 2
                        row0 = g * TG + t * 128
                        dma(xt[i][:], Dm["x"][row0:row0 + 128, :], r=["x"], w=["xt%d" % i])
                        cp("act", xb[i][:], xt[i][:], ["xt%d" % i], ["xb%d" % i])
                        transpose_tile(xb[i], 8, pst, "pst", xTg[:, :, 1 + t * 128:1 + (t + 1) * 128], xk, "xb%d" % i, "dve")

                    def lhs_cur(kc, t):
                        return xTg[:, kc, 1 + t * 128:1 + (t + 1) * 128]

                    def lhs_prev(kc, t):
                        return xTg[:, kc, t * 128:(t + 1) * 128]

                    wt, wk = wb.load(Dm["wb_in_a"], "wb_in_a", 0, 8, Z0, 512)
                    for t in range(4):
                        for kc in range(8):
                            mm(ptm[:, t, :], lhs_cur(kc, t), wt[:, kc, :], kc == 0, kc == 7, [xk, wk], [("ptm", t)])
                        e_t, e_k = next_ev()
                        act(e_t[:], ptm[:, t, :], AF.Silu, [("ptm", t)], [e_k])
                        row0 = g * TG + t * 128
                        dma(Dm["zs"][row0:row0 + 128, :], e_t[:], r=[e_k], w=[])
                    wt, wk = wb.load(Dm["wb_in_a"], "wb_in_a", 0, 8, DT0, HN)
                    for t in range(4):
                        for kc in range(8):
                            mm(ptm[:, t, 0:HN], lhs_cur(kc, t), wt[:, kc, 0:HN], kc == 0, kc == 7, [xk, wk], [("ptm", t)])
                        e_t, e_k = next_ev()
                        cp("dve", e_t[:, 0:HN], ptm[:, t, 0:HN], [("ptm", t)], [e_k])
                        row0 = g * TG + t * 128
                        dma(Dm["dts"][row0:row0 + 128, :], e_t[:, 0:HN], r=[e_k], w=[])
                    for fc in range(6):
                        if fc == 0:
                            wt, wk = wb.load(Dm["wb_in_a"], "wb_in_a", 0, 8, XS0, 512)
                        elif fc == 4:
                            wt, wk = wb.load(Dm["wb_in_a"], "wb_in_a", 0, 8, B0, 256)
                        pb = fc % 2
                        for kc in range(8):
                            mm(pfm[:, pb, :], wt[:, kc, (fc % 4) * 128:(fc % 4 + 1) * 128], xTg[:, kc, 1:513], kc == 0, kc == 7,
                               [xk, wk], [("pfm", pb)])
                        e_t, e_k = next_ev()
                        cp("dve", e_t[:], pfm[:, pb, :], [("pfm", pb)], [e_k])
                        dma(Dm["xbc"][fc * 128:(fc + 1) * 128, 3 + g * TG:3 + (g + 1) * TG], e_t[:], r=[e_k], w=[])
                    for n in range(3):
                        wt_a, wk_a = wb.load(Dm["wb_in_a"], "wb_in_a", 0, 8, RW0 + n * 512, 512)
                        wt_b, wk_b = wb.load(Dm["wb_in_b"], "wb_in_b", 0, 8, n * 512, 512)
                        for t in range(4):
                            for kc in range(8):
                                mm(ptm[:, t, :], lhs_cur(kc, t), wt_a[:, kc, :], kc == 0, False, [xk, wk_a], [("ptm", t)])
                            for kc in range(8):
                                mm(ptm[:, t, :], lhs_prev(kc, t), wt_b[:, kc, :], False, kc == 7, [xk, wk_b], [("ptm", t)])
                            e_t, e_k = next_ev()
                            cp("act" if t % 2 else "dve", e_t[:], ptm[:, t, :], [("ptm", t)], [e_k])
                            row0 = g * TG + t * 128
                            dma(Dm["rkv"][row0:row0 + 128, n * 512:(n + 1) * 512], e_t[:], r=[e_k], w=[])
                    wt_a, wk_a = wb.load(Dm["wb_in_a"], "wb_in_a", 0, 8, RW0 + 1536, 256)
                    wt_b, wk_b = wb.load(Dm["wb_in_b"], "wb_in_b", 0, 8, 1536, 256)
                    for c in range(2):
                        for kc in range(8):
                            mm(pfm[:, c, :], wt_a[:, kc, c * 128:(c + 1) * 128], xTg[:, kc, 1:513], kc == 0, False, [xk, wk_a], [("pfm", c)])
                        for kc in range(8):
                            mm(pfm[:, c, :], wt_b[:, kc, c * 128:(c + 1) * 128], xTg[:, kc, 0:512], False, kc == 7, [xk, wk_b], [("pfm", c)])
                    act(lo1[0:64, :], pfm[0:64, 0, :], AF.Tanh, [("pfm", 0)], ["lo1"])
                    act(lo1[64:128, :], pfm[64:128, 0, :], AF.Copy, [("pfm", 0)], ["lo1"])
                    act(lo2[:], pfm[:, 1, :], AF.Sigmoid, [("pfm", 1)], ["lo2"])
                    for kind in range(3):
                        for t in range(4):
                            cs = slice(t * 128, (t + 1) * 128)
                            row0 = g * TG + t * 128
                            e_t, e_k = next_ev()
                            if kind == 0:
                                mm(ptm[:, t, :], lo1[0:64, cs], wa2[0:64, :], True, True, ["lo1", "wa2"], [("ptm", t)])
                                tt("dve", e_t[:], ptm[:, t, :], w0_bc[:], ALU.add, [("ptm", t), "w0_bc"], [e_k])
                                act(e_t[:], e_t[:], AF.Sigmoid, [e_k], [e_k])
                                dma(Dm["lw"][row0:row0 + 128, :], e_t[:], r=[e_k], w=[])
                            elif kind == 1:
                                mm(ptm[:, t, :], lo1[64:128, cs], wa2[64:128, :], True, True, ["lo1", "wa2"], [("ptm", t)])
                                tt("dve", e_t[:], ptm[:, t, :], a0_bc[:], ALU.add, [("ptm", t), "a0_bc"], [e_k])
                                act(e_t[:], e_t[:], AF.Sigmoid, [e_k], [e_k])
                                dma(Dm["aa"][row0:row0 + 128, :], e_t[:], r=[e_k], w=[])
                            else:
                                mm(ptm[:, t, :], lo2[:, cs], g2s[:], True, True, ["lo2", "g2s"], [("ptm", t)])
                                cp("dve", e_t[:], ptm[:, t, :], [("ptm", t)], [e_k])
                                dma(Dm["gg"][row0:row0 + 128, :], e_t[:], r=[e_k], w=[])
                P.barrier()

        def stage_l0_scan():
            with contextlib.ExitStack() as st:
                W = HW
                Ft = [sb(st, "F%d" % i, [128, W]) for i in range(13)]
                Fk = ["F%d" % i for i in range(13)]
                Hn = ["rH", "kH", "bH", "aH", "kC", "bC", "vH", "Xb", "Ub", "Yo", "XDT", "XD"]
                Ht = {n: sb(st, n, [128, W], BF16) for n in Hn}
                fmA = sb(st, "fmA", [64, HN, 2, 128], BF16)
                fmB = sb(st, "fmB", [64, HN, 128], BF16)
                fmK = sb(st, "fmK", [64, HN, 128], BF16)
                MrbT = sb(st, "MrbT", [128, HN, 128], BF16)
                MakT = sb(st, "MakT", [128, HN, 128], BF16)
                MrkT = sb(st, "MrkT", [128, HN, 128], BF16)
                TT = sb(st, "TT", [128, HN, 128], BF16)
                NQ = [[sb(st, "NQ%d_%d" % (a_, b_), [128, 4, 128], BF16) for b_ in range(2)] for a_ in range(2)]
                NQT = [[sb(st, "NQT%d_%d" % (a_, b_), [128, 4, 128], BF16) for b_ in range(2)] for a_ in range(2)]
                NP = [[sb(st, "NP%d_%d" % (a_, b_), [128, 4, 128], BF16) for b_ in range(2)] for a_ in range(2)]
                S0f = sb(st, "S0f", [64, W])
                S0b = sb(st, "S0b", [64, W], BF16)
                SPf = sb(st, "SPf", [128, W])
                SPb = sb(st, "SPb", [128, W], BF16)
                xin = sb(st, "xin", [128, 6, 131])
                xinb = sb(st, "xinb", [128, 6, 131], BF16)
                diagW = sb(st, "diagW", [128, 4, 6, 128], BF16)
                cwt = sb(st, "cwt", [128, 4, 6])
                cbrow = sb(st, "cbrow", [1, 768], BF16)
                cbrow_f = sb(st, "cbrow_f", [1, 768])
                xs_fm = sb(st, "xs_fm", [128, 4, 128])
                BCf = sb(st, "BCf", [128, 2, 128], BF16)
                arhs = sb(st, "arhs", [128, HN, 128])
                LTe = sb(st, "LTe", [128, HN, 128], BF16)
                MT = sb(st, "MT", [128, HN, 128], BF16)
                CBm = sb(st, "CBm", [128, 128], BF16)
                Btm = sb(st, "Btm", [128, 128], BF16)
                yTo = [sb(st, "yTo%d" % i, [128, 4, 128], BF16) for i in range(2)]
                sm = sb(st, "sm", [128, 256])
                dtt = sb(st, "dtt", [128, HN])
                kk_bc = sb(st, "kk_bc", [128, W]); ka_bc = sb(st, "ka_bc", [128, W]); rk_bc = sb(st, "rk_bc", [128, W])
                lg_bc = sb(st, "lg_bc", [128, W]); lb_bc = sb(st, "lb_bc", [128, W]); ng_bc = sb(st, "ng_bc", [128, W])
                vec16 = sb(st, "vec16", [128, 64])
                PS = psum(st, "PS", [128, 8, 512], F32)

                def pk(b0, nb=1):
                    return [("ps", b) for b in range(b0, b0 + nb)]

                def psv(b0, nb, inner):
                    return PS[:, b0:b0 + nb, :].rearrange("p b (c t) -> p (b c) t", t=inner)

                def psbf(b0):
                    return PS[:, b0, :].bitcast(BF16)

                for t_, nm in ((kk_bc, "k_k"), (ka_bc, "k_a"), (rk_bc, "r_k"), (lg_bc, "lnx_g"), (lb_bc, "lnx_b"), (ng_bc, "ssd_ng")):
                    bc_load(t_[:], Dm[nm][0], ["bcs"])
                bc_load(vec16[:, 0:HN], Dm["dt_bias"][0], ["vec16"])
                bc_load(vec16[:, 16:16 + HN], Dm["a_log"][0], ["vec16"])
                bc_load(vec16[:, 32:32 + HN], Dm["ssd_d"][0], ["vec16"])
                act(vec16[:, 16:16 + HN], vec16[:, 16:16 + HN], AF.Exp, ["vec16"], ["vec16"])
                ts("dve", vec16[:, 16:16 + HN], vec16[:, 16:16 + HN], -1.0, None, ALU.mult, None, ["vec16"], ["vec16"])
                dma_slow(cwt[:], Dm["conv_w"].rearrange("k (c p) -> p k c", p=128), w=["cwt"])
                dma(cbrow_f[:], Dm["conv_b"], w=["cbrow_f"])
                cp("dve", cbrow[:], cbrow_f[:], ["cbrow_f"], ["cbrow"])
                for k in range(4):
                    for c in range(6):
                        ts("dve" if (k * 6 + c) % 2 else "pool", diagW[:, k, c, :], ident_f[:], cwt[:, k, c:c + 1], None, ALU.mult, None,
                           ["cwt", "ident_f"], ["diagW"])
                op("dve", lambda e: e.memset(S0f[:], 0.0), w=["S0f"])
                op("dve", lambda e: e.memset(S0b[:], 0.0), w=["S0b"])
                op("pool", lambda e: e.memset(SPf[:], 0.0), w=["SPf"])
                op("pool", lambda e: e.memset(SPb[:], 0.0), w=["SPb"])
                R_, K_, V_, LW_, A_, G_, ZS_ = range(7)

                def v3(ap):
                    return ap.rearrange("p (h d) -> p h d", d=64)

                def bc16(ap8):
                    return ap8.unsqueeze(2).to_broadcast([128, HN, 64])

                def mat_bc(m, n):
                    return m.unsqueeze(1).to_broadcast([128, n, 128])

                for it in range(NT):
                    r0 = it * 128
                    ch, ccol = it // 16, (it % 16) * 128
                    dma(Ft[R_][:], Dm["rkv"][r0:r0 + 128, 0:W], r=["rkv"], w=[Fk[R_]])
                    dma(Ft[K_][:], Dm["rkv"][r0:r0 + 128, W:2 * W], r=["rkv"], w=[Fk[K_]])
                    dma(Ft[V_][:], Dm["rkv"][r0:r0 + 128, 2 * W:3 * W], r=["rkv"], w=[Fk[V_]])
                    dma(Ft[LW_][:], Dm["lw"][r0:r0 + 128, :], r=["lw"], w=[Fk[LW_]])
                    dma(Ft[A_][:], Dm["aa"][r0:r0 + 128, :], r=["aa"], w=[Fk[A_]])
                    dma(Ft[G_][:], Dm["gg"][r0:r0 + 128, :], r=["gg"], w=[Fk[G_]])
                    dma(Ft[ZS_][:], Dm["zs"][r0:r0 + 128, :], r=["zs"], w=[Fk[ZS_]])
                    dma(xin[:], Dm["xbc"][:, r0:r0 + 131].rearrange("(c p) t -> p c t", p=128), r=["xbc"], w=["xin"])
                    dma(dtt[:], Dm["dts"][r0:r0 + 128, :], r=["dts"], w=["dtt"])

                    XS, TY, T3 = Ft[7], Ft[8], Ft[9]
                    kXS, kTY, kT3 = Fk[7], Fk[8], Fk[9]
                    cp("pool", xinb[:], xin[:], ["xin"], ["xinb"])
                    pconv = psv(0, 2, 128)
                    for c in range(6):
                        for k in range(4):
                            mm(pconv[:, c, :], diagW[:, k, c, :], xinb[:, c, k:k + 128], k == 0, False, ["diagW", "xinb"], pk(c // 4))
                        mm(pconv[:, c, :], cbrow[0:1, c * 128:(c + 1) * 128], ones_b[0:1, :], False, True, ["cbrow", "ones_b"], pk(c // 4))
                    act(xs_fm[:], pconv[:, 0:4, :], AF.Silu, pk(0), ["xs_fm"])
                    act(BCf[:], pconv[:, 4:6, :], AF.Silu, pk(1), ["BCf"])
                    dtv = sm[:, 0:HN]; av = sm[:, 16:16 + HN]; eac = sm[:, 32:32 + HN]; cdec = sm[:, 48:48 + HN]
                    tt("dve", dtv, dtt[:], vec16[:, 0:HN], ALU.add, ["dtt", "vec16"], ["sm_dt"])
                    act(dtv, dtv, AF.Exp, ["sm_dt"], ["sm_dt"])
                    act(dtv, dtv, AF.Ln, ["sm_dt"], ["sm_dt"], bias=1.0)
                    tt("dve", av, dtv, vec16[:, 16:16 + HN], ALU.mult, ["sm_dt", "vec16"], ["sm_av"])
                    pxs = psv(2, 1, 128)
                    for c in range(4):
                        mm(pxs[:, c, :], xs_fm[:, c, :], ident_f[:], True, True, ["xs_fm", "ident_f"], pk(2))
                    pxs_f = PS[:, 2, :]
                    cp("act", XS[:], pxs_f, pk(2), [kXS])
                    tt("dve", v3(Ht["XDT"][:]), v3(pxs_f), bc16(dtv), ALU.mult, pk(2) + ["sm_dt"], ["XDT"])
                    pbt = psbf(3)[:, 0:128]
                    tp(pbt, BCf[:, 0, :], ident_b[:], ["BCf", "ident_b"], pk(3))
                    cp("dve", Btm[:], pbt, pk(3), ["Btm"])
                    mm(PS[:, 4, 0:HN], le_f[:], av, True, True, ["le_f", "sm_av"], pk(4))
                    mm(PS[:, 4, 16:16 + HN], ones_f[:], av, True, True, ["ones_f", "sm_av"], pk(4))
                    act(sm[:, 32:32 + HN], PS[:, 4, 0:HN], AF.Exp, pk(4), ["sm_e"])
                    act(sm[:, 48:48 + HN], PS[:, 4, 16:16 + HN], AF.Exp, pk(4), ["sm_e"])
                    tt("dve", arhs[:], mat_bc(le_f[:], HN), av.unsqueeze(2).to_broadcast([128, HN, 128]), ALU.mult, ["le_f", "sm_av"], ["arhs"])
                    pseg = psv(5, 2, 128)
                    for b_ in range(2):
                        mm(PS[:, 5 + b_, :], gt_f[:], arhs[:, 4 * b_:4 * b_ + 4, :].rearrange("p h l -> p (h l)"), True, True, ["gt_f", "arhs"], pk(5 + b_))
                    act(LTe[:], pseg, AF.Exp, pk(5, 2), ["LTe"])
                    pcb = PS[:, 7, 0:128]
                    mm(pcb, BCf[:, 0, :], BCf[:, 1, :], True, True, ["BCf"], pk(7))
                    tt("dve", CBm[:], pcb, le_f[:], ALU.mult, pk(7) + ["le_f"], ["CBm"])
                    tt("pool", MT[:], LTe[:], mat_bc(CBm[:], HN), ALU.mult, ["LTe", "CBm"], ["MT"])
                    pyd = PS[:, 0, :]
                    for h in range(HN):
                        mm(pyd[:, h * 64:(h + 1) * 64], MT[:, h, :], Ht["XDT"][:, h * 64:(h + 1) * 64], True, True, ["MT", "XDT"], pk(0))
                    pyo = PS[:, 1, :]
                    mm(pyo, BCf[:, 1, :], SPb[:], True, True, ["BCf", "SPb"], pk(1))
                    tt("dve", v3(TY[:]), v3(pyo), bc16(eac), ALU.mult, pk(1) + ["sm_e"], [kTY])
                    tt("dve", TY[:], TY[:], pyd, ALU.add, [kTY] + pk(0), [kTY])
                    tt("pool", v3(T3[:]), v3(XS[:]), bc16(vec16[:, 32:32 + HN]), ALU.mult, [kXS, "vec16"], [kT3])
                    tt("pool", TY[:], TY[:], T3[:], ALU.add, [kTY, kT3], [kTY])
                    tt("dve", v3(Ht["XD"][:]), v3(Ht["XDT"][:]), LTe[:, :, 127:128].to_broadcast([128, HN, 64]), ALU.mult, ["XDT", "LTe"], ["XD"])
                    pst_ = PS[:, 2, :]
                    mm(pst_, Btm[:], Ht["XD"][:], True, True, ["Btm", "XD"], pk(2))
                    tt("dve", v3(SPf[:]), v3(SPf[:]), bc16(cdec), ALU.mult, ["SPf", "sm_e"], ["SPf"])
                    tt("dve", SPf[:], SPf[:], pst_, ALU.add, ["SPf"] + pk(2), ["SPf"])
                    cp("act", SPb[:], SPf[:], ["SPf"], ["SPb"])
                    tt("dve", TY[:], TY[:], Ft[ZS_][:], ALU.mult, [kTY, Fk[ZS_]], [kTY])
                    act(T3[:], TY[:], AF.Square, [kTY], [kT3, "sm_q"], accum_out=sm[:, 64:65])
                    ts("dve", sm[:, 64:65], sm[:, 64:65], 1.0 / 512.0, 1e-5, ALU.mult, ALU.add, ["sm_q"], ["sm_q"])
                    act(sm[:, 64:65], sm[:, 64:65], AF.Sqrt, ["sm_q"], ["sm_q"])
                    op("dve", lambda e: e.reciprocal(out=sm[:, 64:65], in_=sm[:, 64:65]), ["sm_q"], ["sm_q"])
                    ts("dve", TY[:], TY[:], sm[:, 64:65], None, ALU.mult, None, [kTY, "sm_q"], [kTY])
                    tt("pool", Ht["Yo"][:], TY[:], ng_bc[:], ALU.mult, [kTY, "bcs"], ["Yo"])
                    pty = psbf(3)[:, 0:512].rearrange("p (c t) -> p c t", t=128)
                    yt_ = yTo[0]
                    transpose_tile(Ht["Yo"], 4, pty, ("ps", 3), yt_[:], "yTo0", "Yo", "act")
                    dma(Dm["yT_loc"][ch, 0:512, ccol:ccol + 128].rearrange("(c p) t -> p c t", p=128), yt_[:], r=["yTo0"], w=[])

                    KK, T1, E1, E2, E3, E4 = Ft[7], Ft[8], Ft[9], Ft[10], Ft[11], Ft[12]
                    kKK, kT1, kE1, kE2, kE3, kE4 = Fk[7], Fk[8], Fk[9], Fk[10], Fk[11], Fk[12]
                    R, Kt, V, LW, A, G = Ft[R_], Ft[K_], Ft[V_], Ft[LW_], Ft[A_], Ft[G_]
                    kR, kK, kV, kLW, kA, kG = Fk[R_], Fk[K_], Fk[V_], Fk[LW_], Fk[A_], Fk[G_]
                    tt("dve", KK[:], Kt[:], kk_bc[:], ALU.mult, [kK, "bcs"], [kKK])
                    tt("pool", T1[:], KK[:], KK[:], ALU.mult, [kKK], [kT1])
                    red(sm[:, 80:80 + HN], v3(T1[:]), [kT1], ["sm_r"])
                    ts("dve", sm[:, 80:80 + HN], sm[:, 80:80 + HN], 1e-24, None, ALU.max, None, ["sm_r"], ["sm_r"])
                    act(sm[:, 80:80 + HN], sm[:, 80:80 + HN], AF.Sqrt, ["sm_r"], ["sm_r"])
                    op("dve", lambda e: e.reciprocal(out=sm[:, 80:80 + HN], in_=sm[:, 80:80 + HN]), ["sm_r"], ["sm_r"])
                    tt("dve", v3(KK[:]), v3(KK[:]), bc16(sm[:, 80:80 + HN]), ALU.mult, [kKK, "sm_r"], [kKK])
                    stt(T1[:], A[:], -1.0, ka_bc[:], ALU.add, ALU.mult, [kA, "bcs"], [kT1])
                    stt(Kt[:], T1[:], 1.0, Kt[:], ALU.add, ALU.mult, [kT1, kK], [kK])
                    tt("pool", T1[:], KK[:], A[:], ALU.mult, [kKK, kA], [kT1])
                    for (b0, msk, mk) in ((0, le_f, "le_f"), (1, lt_f, "lt_f"), (2, gt_f, "gt_f")):
                        mm(PS[:, b0, :], msk[:], LW[:], True, True, [mk, kLW], pk(b0))
                    act(E1[:], PS[:, 0, :], AF.Exp, pk(0), [kE1], scale=SC_W)
                    act(E2[:], PS[:, 0, :], AF.Exp, pk(0), [kE2], scale=-SC_W)
                    act(E3[:], PS[:, 1, :], AF.Exp, pk(1), [kE3], scale=SC_W)
                    act(E4[:], PS[:, 2, :], AF.Exp, pk(2), [kE4], scale=SC_W)
                    for h in range(HN):
                        mm(PS[0:64, 3, h:h + 1], LW[:, h * 64:(h + 1) * 64], ones_f[:, 0:1], True, True, [kLW, "ones_f"], pk(3))
                    pC = sm[0:64, 96:96 + HN]
                    act(pC, PS[0:64, 3, 0:HN], AF.Exp, pk(3), ["sm_pc"], scale=SC_W)
                    tt("dve", Ht["rH"][:], R[:], E1[:], ALU.mult, [kR, kE1], ["rH"])
                    tt("pool", Ht["kH"][:], Kt[:], E2[:], ALU.mult, [kK, kE2], ["kH"])
                    tt("dve", Ht["bH"][:], T1[:], E2[:], ALU.mult, [kT1, kE2], ["bH"])
                    stt(Ht["aH"][:], KK[:], -1.0, E3[:], ALU.mult, ALU.mult, [kKK, kE3], ["aH"])
                    tt("pool", Ht["kC"][:], Kt[:], E4[:], ALU.mult, [kK, kE4], ["kC"])
                    tt("pool", Ht["bC"][:], T1[:], E4[:], ALU.mult, [kT1, kE4], ["bC"])
                    cp("act", Ht["vH"][:], V[:], [kV], ["vH"])
                    for j, (src, dstv, dk, b0) in enumerate((("aH", fmA[:, :, 0, :], "fmA", 4), ("rH", fmA[:, :, 1, :], "fmA", 5),
                                                             ("bH", fmB[:], "fmB", 6), ("kH", fmK[:], "fmK", 7))):
                        pt_ = psbf(b0)[0:64, :].rearrange("p (h t) -> p h t", t=128)
                        for h in range(HN):
                            tp(pt_[:, h, :], Ht[src][:, h * 64:(h + 1) * 64], ident_b[:], [src, "ident_b"], pk(b0))
                        cp("act" if j % 2 else "dve", dstv, pt_, pk(b0), [dk])
                    for g4 in range(2):
                        bb = 4 * g4
                        pB = psv(bb, 2, 256)
                        pK = psv(bb + 2, 2, 256)
                        for hl in range(4):
                            h = g4 * 4 + hl
                            rhsA = fmA[:, h, :, :].rearrange("p a t -> p (a t)")
                            mm(pB[:, hl, :], fmB[:, h, :], rhsA, True, True, ["fmA", "fmB"], pk(bb + hl // 2))
                            mm(pK[:, hl, :], fmK[:, h, :], rhsA, True, True, ["fmA", "fmK"], pk(bb + 2 + hl // 2))
                        hs = slice(g4 * 4, g4 * 4 + 4)
                        tt("dve", NQT[g4][0][:], pB[:, :, 0:128], mat_bc(lt_f[:], 4), ALU.mult, pk(bb, 2) + ["lt_f"], ["NQT%d_0" % g4])
                        tt("dve", MrbT[:, hs, :], pB[:, :, 128:256], mat_bc(le_f[:], 4), ALU.mult, pk(bb, 2) + ["le_f"], [("MrbT", g4)])
                        tt("dve", MakT[:, hs, :], pK[:, :, 0:128], mat_bc(lt_f[:], 4), ALU.mult, pk(bb + 2, 2) + ["lt_f"], [("MakT", g4)])
                        tt("dve", MrkT[:, hs, :], pK[:, :, 128:256], mat_bc(le_f[:], 4), ALU.mult, pk(bb + 2, 2) + ["le_f"], [("MrkT", g4)])
                    cur = [0, 0]
                    for gi in range(2):
                        bq = gi * 3
                        pL = psv(bq, 1, 128)
                        for hl in range(4):
                            h = gi * 4 + hl
                            mm(pL[:, hl, :], fmA[:, h, 0, :], fmB[:, h, :], True, True, ["fmA", "fmB"], pk(bq))
                        tt("dve", NQ[gi][0][:], pL, mat_bc(gt_f[:], 4), ALU.mult, pk(bq) + ["gt_f"], ["NQ%d_0" % gi])
                        tt("pool", NP[gi][0][:], NQT[gi][0][:], mat_bc(ident_f[:], 4), ALU.add, ["NQT%d_0" % gi, "ident_f"], ["NP%d_0" % gi])
                    for lvl in range(1, 7):
                        for gi in range(2):
                            bq = gi * 3
                            c = cur[gi]
                            n_ = 1 - c
                            Qc, QTc, Pc = NQ[gi][c], NQT[gi][c], NP[gi][c]
                            Qn, QTn, Pn = NQ[gi][n_], NQT[gi][n_], NP[gi][n_]
                            kQc, kQTc, kPc = "NQ%d_%d" % (gi, c), "NQT%d_%d" % (gi, c), "NP%d_%d" % (gi, c)
                            kQn, kQTn, kPn = "NQ%d_%d" % (gi, n_), "NQT%d_%d" % (gi, n_), "NP%d_%d" % (gi, n_)
                            pQ = psv(bq, 1, 128); pQT = psv(bq + 1, 1, 128); pP = psv(bq + 2, 1, 128)
                            for hl in range(4):
                                mm(pQ[:, hl, :], QTc[:, hl, :], Qc[:, hl, :], True, True, [kQc, kQTc], pk(bq))
                            cp("act", Qn[:], pQ, pk(bq), [kQn])
                            if lvl < 6:
                                for hl in range(4):
                                    mm(pQT[:, hl, :], Qc[:, hl, :], QTc[:, hl, :], True, True, [kQc, kQTc], pk(bq + 1))
                                cp("dve", QTn[:], pQT, pk(bq + 1), [kQTn])
                            for hl in range(4):
                                mm(pP[:, hl, :], ident_b[:], Pc[:, hl, :], True, False, ["ident_b", kPc], pk(bq + 2))
                                mm(pP[:, hl, :], Qn[:, hl, :], Pc[:, hl, :], False, True, [kQn, kPc], pk(bq + 2))
                            if lvl < 6:
                                cp("dve", Pn[:], pP, pk(bq + 2), [kPn])
                            else:
                                cp("dve", TT[:, gi * 4:gi * 4 + 4, :], pP, pk(bq + 2), [("TT", gi)])
                            cur[gi] = n_
                    TTk = [("TT", g4) for g4 in range(2)]
                    MK = lambda nm: [(nm, g4) for g4 in range(2)]
                    pX = PS[:, 6, :]
                    for h in range(HN):
                        hc = slice(h * 64, (h + 1) * 64)
                        mm(pX[:, hc], fmA[:, h, 0, :], S0b[:, hc], True, False, ["fmA", "S0b"], pk(6))
                        mm(pX[:, hc], MakT[:, h, :], Ht["vH"][:, hc], False, True, MK("MakT") + ["vH"], pk(6))
                    cp("act", Ht["Xb"][:], pX, pk(6), ["Xb"])
                    pU = PS[:, 7, :]
                    for h in range(HN):
                        hc = slice(h * 64, (h + 1) * 64)
                        mm(pU[:, hc], TT[:, h, :], Ht["Xb"][:, hc], True, True, TTk + ["Xb"], pk(7))
                    cp("act", Ht["Ub"][:], pU, pk(7), ["Ub"])
                    pY = PS[:, 0, :]
                    for h in range(HN):
                        hc = slice(h * 64, (h + 1) * 64)
                        mm(pY[:, hc], fmA[:, h, 1, :], S0b[:, hc], True, False, ["fmA", "S0b"], pk(0))
                        mm(pY[:, hc], MrbT[:, h, :], Ht["Ub"][:, hc], False, False, MK("MrbT") + ["Ub"], pk(0))
                        mm(pY[:, hc], MrkT[:, h, :], Ht["vH"][:, hc], False, True, MK("MrkT") + ["vH"], pk(0))
                    pS = PS[:, 1, :]
                    for h in range(HN):
                        hc = slice(h * 64, (h + 1) * 64)
                        mm(pS[0:64, hc], Ht["bC"][:, hc], Ht["Ub"][:, hc], True, False, ["bC", "Ub"], pk(1))
                        mm(pS[0:64, hc], Ht["kC"][:, hc], Ht["vH"][:, hc], False, True, ["kC", "vH"], pk(1))
                    S3 = S0f[:].rearrange("p (h d) -> p h d", d=64)
                    tt("dve", S3, S3, pC.unsqueeze(2).to_broadcast([64, HN, 64]), ALU.mult, ["S0f", "sm_pc"], ["S0f"])
                    tt("dve", S0f[:], S0f[:], pS[0:64, :], ALU.add, ["S0f"] + pk(1), ["S0f"])
                    cp("act", S0b[:], S0f[:], ["S0f"], ["S0b"])
                    Yc, T2 = E1, E2
                    kYc, kT2 = kE1, kE2
                    red(sm[:, 112:112 + HN], v3(pY), pk(0), ["sm_m"])
                    ts("dve", sm[:, 112:112 + HN], sm[:, 112:112 + HN], 1.0 / 64.0, None, ALU.mult, None, ["sm_m"], ["sm_m"])
                    tt("dve", v3(Yc[:]), v3(pY), bc16(sm[:, 112:112 + HN]), ALU.subtract, pk(0) + ["sm_m"], [kYc])
                    act(T2[:], Yc[:], AF.Square, [kYc], [kT2])
                    red(sm[:, 128:128 + HN], v3(T2[:]), [kT2], ["sm_v"])
                    ts("dve", sm[:, 128:128 + HN], sm[:, 128:128 + HN], 1.0 / 64.0, 64e-5, ALU.mult, ALU.add, ["sm_v"], ["sm_v"])
                    act(sm[:, 128:128 + HN], sm[:, 128:128 + HN], AF.Sqrt, ["sm_v"], ["sm_v"])
                    op("dve", lambda e: e.reciprocal(out=sm[:, 128:128 + HN], in_=sm[:, 128:128 + HN]), ["sm_v"], ["sm_v"])
                    tt("dve", v3(Yc[:]), v3(Yc[:]), bc16(sm[:, 128:128 + HN]), ALU.mult, [kYc, "sm_v"], [kYc])
                    tt("pool", Yc[:], Yc[:], lg_bc[:], ALU.mult, [kYc, "bcs"], [kYc])
                    tt("pool", Yc[:], Yc[:], lb_bc[:], ALU.add, [kYc, "bcs"], [kYc])
                    tt("pool", T2[:], R[:], Kt[:], ALU.mult, [kR, kK], [kT2])
                    tt("pool", T2[:], T2[:], rk_bc[:], ALU.mult, [kT2, "bcs"], [kT2])
                    red(sm[:, 144:144 + HN], v3(T2[:]), [kT2], ["sm_b"])
                    tt("dve", v3(T2[:]), v3(V[:]), bc16(sm[:, 144:144 + HN]), ALU.mult, [kV, "sm_b", kT2], [kT2])
                    tt("pool", Yc[:], Yc[:], T2[:], ALU.add, [kYc, kT2], [kYc])
                    tt("dve", Ht["Yo"][:], Yc[:], G[:], ALU.mult, [kYc, kG], ["Yo"])
                    pty = psbf(3)[:, 0:512].rearrange("p (c t) -> p c t", t=128)
                    yt_ = yTo[1]
                    transpose_tile(Ht["Yo"], 4, pty, ("ps", 3), yt_[:], "yTo1", "Yo", "act")
                    dma(Dm["yT_loc"][ch, 512:1024, ccol:ccol + 128].rearrange("(c p) t -> p c t", p=128), yt_[:], r=["yTo1"], w=[])
                P.barrier()
                for ch in range(2):
                    dma_dyn(lambda me, ch=ch: (Dm["yTs"][ch, bass.ds(me, 1), :, :][0], Dm["yT_loc"][ch]), r=[], w=[])
                pair_barrier()

        def stage_tail(layer):
            KC = 16 if layer == 0 else 8
            Wo = Dm["wb_eo"] if layer == 0 else Dm["wb_oo"]
            Wo_name = "wb_eo" if layer == 0 else "wb_oo"
            xsrc = Dm["x_tail"] if layer == 0 else Dm["xloc"]
            xdst = Dm["xloc"] if layer == 0 else Dm["out"]
            ysh = Dm["yTs"] if layer == 0 else Dm["oTs"]
            KCP = KC // 2
            with contextlib.ExitStack() as st:
                wb = WB(st, 3)
                XR = [sb(st, "XR%d" % i, [128, D]) for i in range(4)]
                xb = [sb(st, "xb%d" % i, [128, D], BF16) for i in range(2)]
                yTg = sb(st, "yTg", [128, KC, 512], BF16)
                x1T = sb(st, "x1T", [128, 8, 512], BF16)
                qT = sb(st, "qTx", [128, 8, 512], BF16)
                oT = sb(st, "oTx", [128, 8, 512], BF16)
                hT = sb(st, "hT", [128, 22, 512], BF16)
                PT = [sb(st, "PTx%d" % i, [128, 512], BF16) for i in range(2)]
                rden = sb(st, "rden", [128, 512])
                sg = [sb(st, "sg%d" % i, [128, 512]) for i in range(2)]
                memT = sb(st, "memT", [128, 8, 256], BF16)
                KmT = sb(st, "KmT", [128, 8, 256], BF16)
                Vm = sb(st, "Vm", [128, 2, D], BF16)
                sm = sb(st, "smx", [128, 16])
                lnp = {}
                for nm in ("ln_mix_g", "ln_mix_b", "ln_xa_g", "ln_xa_b", "ln_ffn_g", "ln_ffn_b"):
                    lnp[nm] = sb(st, "bc_" + nm, [128, D])
                    bc_load(lnp[nm][:], Dm[nm][layer], ["lnp"])
                PS = psum(st, "PS3", [128, 8, 512], F32)

                def pk(b0, nb=1):
                    return [("ps", b) for b in range(b0, b0 + nb)]
                pst = PS[:, 6, :].bitcast(BF16).rearrange("p (c t) -> p c t", t=128)

                for mt in range(2):
                    dma(XR[mt][:], Dm["mem"][mt * 128:(mt + 1) * 128, :], r=["mem"], w=["XR%d" % mt])
                    cp("act", xb[mt][:], XR[mt][:], ["XR%d" % mt], ["xb%d" % mt])
                    transpose_tile(xb[mt], 8, pst, ("ps", 6), memT[:, :, mt * 128:(mt + 1) * 128], "memT", "xb%d" % mt, "dve")
                for fc in range(8):
                    if fc % 4 == 0:
                        wt, wk = wb.load(Dm["wb_xkv"][layer], "wb_xkv", 0, 8, fc * 128, 512)
                    for kc in range(8):
                        mm(PS[:, 4 + fc % 2, 0:256], wt[:, kc, (fc % 4) * 128:(fc % 4 + 1) * 128], memT[:, kc, :], kc == 0, kc == 7, ["memT", wk], pk(4 + fc % 2))
                    cp("dve", KmT[:, fc, :], PS[:, 4 + fc % 2, 0:256], pk(4 + fc % 2), ["KmT"])
                for n in range(2):
                    wt, wk = wb.load(Dm["wb_xkv"][layer], "wb_xkv", 0, 8, 1024 + n * 512, 512)
                    for mt in range(2):
                        for kc in range(8):
                            mm(PS[:, mt, :], memT[:, kc, mt * 128:(mt + 1) * 128], wt[:, kc, :], kc == 0, kc == 7, ["memT", wk], pk(mt))
                        cp("act", Vm[:, mt, n * 512:(n + 1) * 512], PS[:, mt, :], pk(mt), ["Vm"])

                def gemm_res_ln(lhs_fn, lhs_keys, nk, Wd, wname, gname, bname):
                    kgs = [(k0, min(8, nk - k0)) for k0 in range(0, nk, 8)]
                    for n in range(2):
                        for gi, (k0, nkc) in enumerate(kgs):
                            wt, wk = wb.load(Wd, wname, k0, nkc, n * 512, 512)
                            for t in range(4):
                                for kc in range(nkc):
                                    mm(PS[:, t, :], lhs_fn(k0 + kc, t), wt[:, kc, :], gi == 0 and kc == 0, gi == len(kgs) - 1 and kc == nkc - 1,
                                       lhs_keys + [wk], pk(t))
                        for t in range(4):
                            ns = slice(n * 512, (n + 1) * 512)
                            stt(XR[t][:, ns], XR[t][:, ns], ALPHA, PS[:, t, :], ALU.mult, ALU.add, ["XR%d" % t] + pk(t), ["XR%d" % t])
                    for t in range(4):
                        layer_norm(XR[t][:], "XR%d" % t, XR[t][:], "XR%d" % t, lnp[gname][:], lnp[bname][:], ["lnp"], sm, "smx")

                def to_fm(dst, dkey):
                    for t in range(4):
                        i = t % 2
                        cp("act", xb[i][:], XR[t][:], ["XR%d" % t], ["xb%d" % i])
                        transpose_tile(xb[i], 8, pst, ("ps", 6), dst[:, :, t * 128:(t + 1) * 128], dkey, "xb%d" % i, "dve")

                yin = Dm["yT_in"] if layer == 0 else Dm["oT_in"]
                dma_dyn(lambda me: (yin, ysh[bass.ds(me, 1), :, :, :][0]), r=[], w=["yin"])
                for g in range(NG // 2):
                    c0 = g * TG
                    for pp in range(2):
                        dma(yTg[:, pp * KCP:(pp + 1) * KCP, :], yin[pp, :, c0:c0 + TG].rearrange("(c p) t -> p c t", p=128), r=["yin"], w=["yTg"])
                    for t in range(4):
                        dma(XR[t][:], xsrc[c0 + t * 128:c0 + (t + 1) * 128, :], r=["xsrc"], w=["XR%d" % t])
                    gemm_res_ln(lambda kc, t: yTg[:, kc, t * 128:(t + 1) * 128], ["yTg"], KC, Wo, Wo_name, "ln_mix_g", "ln_mix_b")
                    to_fm(x1T, "x1T")
                    for fc in range(8):
                        if fc % 4 == 0:
                            wt, wk = wb.load(Dm["wb_xq"][layer], "wb_xq", 0, 8, fc * 128, 512)
                        for kc in range(8):
                            mm(PS[:, 4 + fc % 2, :], wt[:, kc, (fc % 4) * 128:(fc % 4 + 1) * 128], x1T[:, kc, :], kc == 0, kc == 7, ["x1T", wk], pk(4 + fc % 2))
                        cp("act" if fc % 2 else "dve", qT[:, fc, :], PS[:, 4 + fc % 2, :], pk(4 + fc % 2), ["qTx"])
                    for hh in range(4):
                        for mt in range(2):
                            for dc in range(2):
                                mm(PS[:, mt, :], KmT[:, 2 * hh + dc, mt * 128:(mt + 1) * 128], qT[:, 2 * hh + dc, :], dc == 0, dc == 1, ["KmT", "qTx"], pk(mt))
                            act(PT[mt][:], PS[:, mt, :], AF.Exp, pk(mt), ["PTx%d" % mt], scale=1.0 / 16.0)
                        for mt in range(2):
                            mm(PS[:, 2, :], ones_b[:], PT[mt][:], mt == 0, mt == 1, ["ones_b", "PTx%d" % mt], pk(2))
                        for dc in range(2):
                            bo = 3 if dc == 0 else 7
                            for mt in range(2):
                                mm(PS[:, bo, :], Vm[:, mt, (2 * hh + dc) * 128:(2 * hh + dc + 1) * 128], PT[mt][:], mt == 0, mt == 1, ["Vm", "PTx%d" % mt], pk(bo))
                        op("dve", lambda e: e.reciprocal(out=rden[:], in_=PS[:, 2, :]), pk(2), ["rden"])
                        for dc in range(2):
                            bo = 3 if dc == 0 else 7
                            tt("dve", oT[:, 2 * hh + dc, :], PS[:, bo, :], rden[:], ALU.mult, pk(bo) + ["rden"], ["oTx"])
                    gemm_res_ln(lambda kc, t: oT[:, kc, t * 128:(t + 1) * 128], ["oTx"], 8, Dm["wb_xo"][layer], "wb_xo", "ln_xa_g", "ln_xa_b")
                    to_fm(x1T, "x1T")
                    for j in range(22):
                        if j % 2 == 0:
                            i = wb.i
                            wb.i = (wb.i + 1) % len(wb.t)
                            wt = wb.t[i]
                            wk = "wbuf%d" % i
                            for half, cbase in ((0, 0), (1, FFN_H)):
                                dma(wt[:, :, half * 256:half * 256 + 256],
                                    Dm["wb_13"][layer][:, cbase + j * 128:cbase + j * 128 + 256].rearrange("(c p) f -> p c f", p=128),
                                    r=["wb_13"], w=[wk])
                        jl = j % 2
                        bg = 4 + 2 * jl
                        for kc in range(8):
                            mm(PS[:, bg, :], wt[:, kc, jl * 128:(jl + 1) * 128], x1T[:, kc, :], kc == 0, kc == 7, ["x1T", wk], pk(bg))
                        for kc in range(8):
                            mm(PS[:, bg + 1, :], wt[:, kc, 256 + jl * 128:256 + (jl + 1) * 128], x1T[:, kc, :], kc == 0, kc == 7, ["x1T", wk], pk(bg + 1))
                        act(sg[jl][:], PS[:, bg, :], AF.Silu, pk(bg), ["sg%d" % jl])
                        tt("dve", hT[:, j, :], sg[jl][:], PS[:, bg + 1, :], ALU.mult, ["sg%d" % jl] + pk(bg + 1), ["hT"])
                    gemm_res_ln(lambda kc, t: hT[:, kc, t * 128:(t + 1) * 128], ["hT"], 22, Dm["wb_2"][layer], "wb_2", "ln_ffn_g", "ln_ffn_b")
                    for t in range(4):
                        dma(xdst[c0 + t * 128:c0 + (t + 1) * 128, :], XR[t][:], r=["XR%d" % t], w=[])
                if layer == 0:
                    P.barrier()
                    dma_dyn(lambda me: (Dm["xs_sh"][bass.ds(me, 1), :, :][0], Dm["xloc"]), r=[], w=[])
                    pair_barrier()
                else:
                    P.barrier()

        def stage_l1_proj():
            xall = Dm["xs_sh"].rearrange("h t d -> (h t) d")
            with contextlib.ExitStack() as st:
                wb = WB(st, 3)
                xt = [sb(st, "xt%d" % i, [128, D]) for i in range(2)]
                xb = [sb(st, "xb%d" % i, [128, D], BF16) for i in range(2)]
                xT = [sb(st, "xT%d" % i, [128, 8, 512], BF16) for i in range(2)]
                evb = [sb(st, "evb%d" % i, [128, 512], BF16) for i in range(4)]
                kms = sb(st, "kms", [128, 4, 16])
                pst = psum(st, "pst1", [128, 8, 128], BF16)
                ptm = psum(st, "ptm1", [128, 4, 512], F32)
                pfm = psum(st, "pfm1", [128, 2, 512], F32)
                evi = [0]

                def next_ev():
                    i = evi[0]
                    evi[0] = (i + 1) % 4
                    return evb[i], "evb%d" % i
                for g in range(NG):
                    xTg = xT[g % 2]
                    xk = "xT%d" % (g % 2)
                    for t in range(4):
                        i = t % 2
                        row0 = g * TG + t * 128
                        dma(xt[i][:], xall[row0:row0 + 128, :], r=["xall"], w=["xt%d" % i])
                        cp("act", xb[i][:], xt[i][:], ["xt%d" % i], ["xb%d" % i])
                        transpose_tile(xb[i], 8, pst, "pst", xTg[:, :, t * 128:(t + 1) * 128], xk, "xb%d" % i, "dve")
                    for fc in range(8):
                        if fc % 4 == 0:
                            wt, wk = wb.load(Dm["wb_qkv"], "wb_qkv", 0, 8, fc * 128, 512)
                        pb = fc % 2
                        for kc in range(8):
                            mm(pfm[:, pb, :], wt[:, kc, (fc % 4) * 128:(fc % 4 + 1) * 128], xTg[:, kc, :], kc == 0, kc == 7, [xk, wk], [("pfm", pb)])
                        e_t, e_k = next_ev()
                        cp("act" if fc % 2 else "dve", e_t[:], pfm[:, pb, :], [("pfm", pb)], [e_k])
                        if fc < 4:
                            dma(Dm["qT"][fc * 128:(fc + 1) * 128, g * TG:(g + 1) * TG], e_t[:], r=[e_k], w=[])
                        else:
                            dma(Dm["kT"][(fc - 4) * 128:(fc - 3) * 128, g * TG:(g + 1) * TG], e_t[:], r=[e_k], w=[])
                            red(kms[:, fc - 4, 2 * g:2 * g + 2], pfm[:, pb, :].rearrange("p (b t) -> p b t", t=256), [("pfm", pb)], ["kms"])
                    wt, wk = wb.load(Dm["wb_qkv"], "wb_qkv", 0, 8, 2 * HW, 512)
                    for t in range(4):
                        for kc in range(8):
                            mm(ptm[:, t, :], xTg[:, kc, t * 128:(t + 1) * 128], wt[:, kc, :], kc == 0, kc == 7, [xk, wk], [("ptm", t)])
                        e_t, e_k = next_ev()
                        cp("act" if t % 2 else "dve", e_t[:], ptm[:, t, :], [("ptm", t)], [e_k])
                        row0 = g * TG + t * 128
                        dma(Dm["vv"][row0:row0 + 128, :], e_t[:], r=[e_k], w=[])
                ts("dve", kms[:], kms[:], 1.0 / 256.0, None, ALU.mult, None, ["kms"], ["kms"])
                dma(Dm["kmT"], kms[:], r=["kms"], w=[])
                P.barrier()

        def stage_l1_moba():
            with contextlib.ExitStack() as st:
                NP_ = HN // 2
                KT = sb(st, "KT", [128, NP_, S], BF16)
                VV = sb(st, "VV", [128, NT, HW], BF16)
                qraw = sb(st, "qraw", [128, NP_, 512], BF16)
                qm = sb(st, "qm", [128, HN, 512], BF16)
                kmf = sb(st, "kmf", [128, NP_, 16])
                kmb = sb(st, "kmb", [128, NP_, 16], BF16)
                Esel = sb(st, "Esel", [128, 16, 128], BF16)
                gsb = sb(st, "gsb", [128, HN, 16])
                t8 = sb(st, "t8", [128, HN, 8])
                thr = sb(st, "thr", [128, HN])
                sel = sb(st, "sel", [128, HN, 16])
                bpad = sb(st, "bpad", [128, HN, 128], BF16)
                biasT = sb(st, "biasT", [128, HN, 512], BF16)
                PTm = [sb(st, "PTm%d" % i, [128, 512], BF16) for i in range(3)]
                rden = [sb(st, "rdenm%d" % i, [64, 512]) for i in range(2)]
                oTs = [sb(st, "oTs%d" % i, [64, 512], BF16) for i in range(2)]
                PS = psum(st, "PS5", [128, 8, 512], F32)

                def pk(b0, nb=1):
                    return [("ps", b) for b in range(b0, b0 + nb)]
                dma(kmf[:], Dm["kmT"], r=["kmT"], w=["kmf"])
                cp("dve", kmb[:], kmf[:], ["kmf"], ["kmb"])
                op("pool", lambda e: e.memset(Esel[:], 0.0), [], ["Esel"])
                op("pool", lambda e: e.memset(bpad[:], 0.0), [], ["bpad"])
                op("pool", lambda e: e.memset(qm[:], 0.0), [], ["qm"])
                for n in range(16):
                    ts("dve", Esel[0:16, n, :], ones_f[0:16, :], ident_f[0:16, n:n + 1], None, ALU.mult, None, ["ones_f", "ident_f", "Esel"], ["Esel"])
                for G in range(NG):
                    c0 = G * TG
                    ch, ccol = G // 4, (G % 4) * TG
                    dma(qraw[:], Dm["qT"][:, c0:c0 + TG].rearrange("(c p) t -> p c t", p=128), r=["qT"], w=["qraw"])
                    dma(KT[:, :, c0:c0 + TG], Dm["kT"][:, c0:c0 + TG].rearrange("(c p) t -> p c t", p=128), r=["kT"], w=[("KT", G)])
                    dma(VV[:, 4 * G:4 * G + 4, :], Dm["vv"][c0:c0 + TG, :].rearrange("(t p) f -> p t f", p=128), r=["vv"], w=[("VV", G)])
                    KTk = [("KT", gg_) for gg_ in range(G + 1)]
                    VVk = [("VV", gg_) for gg_ in range(G + 1)]
                    qm4 = qm[:].rearrange("p (a b) t -> p a b t", b=2)
                    cp("dve", qm4[0:64, :, 0, :], qraw[0:64, :, :], ["qraw"], ["qm"])
                    cp("pool", qm4[64:128, :, 1, :], qraw[64:128, :, :], ["qraw"], ["qm"])
                    for c in range(4):
                        own = 2 * G + c // 2
                        pg = PS[:, 7, 0:HN * 16].rearrange("p (h n) -> p h n", n=16)
                        for h in range(HN):
                            mm(pg[:, h, :], qm[:, h, c * 128:(c + 1) * 128], kmb[:, h // 2, :], True, True, ["qm", "kmb"], pk(7))
                        op("pool", lambda e: e.memset(gsb[:], -1e30), [], ["gsb"])
                        if own > 0:
                            cp("dve", gsb[:, :, 0:own], pg[:, :, 0:own], pk(7), ["gsb"])
                        for h in range(HN):
                            op("dve", lambda e, h=h: e.max(out=t8[:, h, :], in_=gsb[:, h, :]), ["gsb"], ["t8"])
                        ts("dve", thr[:], t8[:, :, 2], -1e29, None, ALU.max, None, ["t8"], ["thr"])
                        tt("dve", sel[:], gsb[:], thr[:].unsqueeze(2).to_broadcast([128, HN, 16]), ALU.is_ge, ["gsb", "thr"], ["sel"])
                        op("dve", lambda e, own=own: e.memset(sel[:, :, own:own + 1], 1.0), [], ["sel"])
                        ts("dve", bpad[:, :, 0:16], sel[:], 30000.0, -30000.0, ALU.mult, ALU.add, ["sel"], ["bpad"])
                        pbt = PS[:, 6, :].bitcast(BF16).rearrange("p (h t) -> p h t", t=128)
                        for h in range(HN):
                            tp(pbt[:, h, :], bpad[:, h, :], ident_b[:], ["bpad", "ident_b"], pk(6))
                        cp("act", biasT[:, :, c * 128:(c + 1) * 128], pbt, pk(6), ["biasT"])
                    nkt = 4 * G + 4
                    steps = [(h, kt) for h in range(HN) for kt in range(nkt)]

                    def issue_scores(i):
                        h, kt = steps[i]
                        pr = h // 2
                        col0 = max(0, kt - 4 * G) * 128
                        bs = i % 2
                        mm(PS[:, bs, col0:512], KT[:, pr, kt * 128:(kt + 1) * 128], qm[:, h, col0:512], True, False, KTk + ["qm"], pk(bs))
                        mm(PS[:, bs, col0:512], Esel[:, kt // 2, :], biasT[:, h, col0:512], False, True, ["Esel", "biasT"], pk(bs))

                    issue_scores(0)
                    for i, (h, kt) in enumerate(steps):
                        j = kt - 4 * G
                        col0 = max(0, j) * 128
                        bs = i % 2
                        pt = PTm[i % 3]
                        ptk = "PTm%d" % (i % 3)
                        if i + 1 < len(steps):
                            issue_scores(i + 1)
                        act(pt[:, col0:512], PS[:, bs, col0:512], AF.Exp, pk(bs), [ptk], scale=0.125)
                        if j >= 0:
                            tt("pool", pt[:, col0:col0 + 128], pt[:, col0:col0 + 128], le_f[:], ALU.mult, [ptk, "le_f"], [ptk])
                        ob = 2 + 2 * (h % 2)
                        mm(PS[0:64, ob, col0:512], VV[:, kt, h * 64:(h + 1) * 64], pt[:, col0:512], kt == 0, kt == nkt - 1, VVk + [ptk], pk(ob))
                        mm(PS[0:64, ob + 1, col0:512], ones_b[:, 0:64], pt[:, col0:512], kt == 0, kt == nkt - 1, ["ones_b", ptk], pk(ob + 1))
                        if kt == nkt - 1:
                            rd = rden[h % 2]
                            rdk = "rdenm%d" % (h % 2)
                            op("dve", lambda e, rd=rd, ob=ob: e.reciprocal(out=rd[:], in_=PS[0:64, ob + 1, :]), pk(ob + 1), [rdk])
                            ot = oTs[h % 2]
                            otk = "oTs%d" % (h % 2)
                            tt("dve", ot[:], PS[0:64, ob, :], rd[:], ALU.mult, pk(ob) + [rdk], [otk])
                            dma(Dm["oT_loc"][ch, h * 64:(h + 1) * 64, ccol:ccol + TG], ot[:], r=[otk], w=[])
                P.barrier()
                for ch in range(2):
                    dma_dyn(lambda me, ch=ch: (Dm["oTs"][ch, bass.ds(me, 1), :, :][0], Dm["oT_loc"][ch]), r=[], w=[])
                pair_barrier()

        stage_w()
        if upto >= 1:
            stage_l0_proj()
        if upto >= 2:
            stage_l0_scan()
        if upto >= 3:
            stage_tail(0)
        if upto >= 4:
            stage_l1_proj()
        if upto >= 5:
            stage_l1_moba()
        if upto >= 6:
            stage_tail(1)
        for nm in dumps:
            dma(Dm["dump_" + nm], Dm[nm], r=[], w=[])
        P.barrier()
        P.emit_flush()
    return nc


def make_in_maps(inputs, n_cores=8):
    def a(x):
        return np.ascontiguousarray(np.asarray(x, dtype=np.float32))
    i = inputs
    Win = np.asarray(i["even_w_in"][0], dtype=np.float32)
    SSD_IN = 2576
    shared = {
        "w_qkv_full": None,
        "w_oo": a(i["odd_w_out"][0]), "ln_mix_g": a(i["ln_mix_g"]), "ln_mix_b": a(i["ln_mix_b"]),
        "xa_wq": a(i["xa_wq"]), "xa_wkv": a(i["xa_wkv"]), "xa_wo": a(i["xa_wo"]), "ln_xa_g": a(i["ln_xa_g"]), "ln_xa_b": a(i["ln_xa_b"]),
        "w13": a(i["ffn_w13"]), "w2": a(i["ffn_w2"]), "ln_ffn_g": a(i["ln_ffn_g"]), "ln_ffn_b": a(i["ln_ffn_b"]),
    }
    del shared["w_qkv_full"]
    Weo = np.asarray(i["even_w_out"][0], dtype=np.float32)
    shared["w_eo"] = a(np.concatenate([Weo[0:512], Weo[1024:1536], Weo[512:1024], Weo[1536:2048]], axis=0))
    per_half = []
    for hh in range(2):
        hs = slice(hh * HW, (hh + 1) * HW)
        h8 = slice(hh * HN, (hh + 1) * HN)
        rw = SSD_IN
        cols = np.concatenate([
            np.arange(0, 1024)[hs],
            1024 + np.arange(0, 1024)[hs],
            2048 + hh * 128 + np.arange(128),
            2304 + hh * 128 + np.arange(128),
            2560 + np.arange(16)[h8],
            rw + np.arange(0, 1024)[hs],
            rw + 1024 + np.arange(0, 1024)[hs],
            rw + 2048 + np.arange(0, 1024)[hs],
            rw + 3072 + np.arange(256),
        ])
        assert cols.shape[0] == EVEN_L
        rwc = cols[RW0:] - rw
        xbc_cols = cols[XS0:DT0] - 1024
        Wqkv = np.asarray(i["odd_w_qkv"][0], dtype=np.float32)
        m = {
            "w_in": a(Win[:, cols]),
            "conv_w": a(np.asarray(i["ssd_conv_w"][0])[:, xbc_cols]),
            "conv_b": a(np.asarray(i["ssd_conv_b"][0])[xbc_cols]).reshape(1, 768),
            "dt_bias": a(np.asarray(i["ssd_dt_bias"][0])[h8]).reshape(1, HN),
            "a_log": a(np.asarray(i["ssd_a_log"][0])[h8]).reshape(1, HN),
            "ssd_d": a(np.asarray(i["ssd_d"][0])[h8]).reshape(1, HN),
            "ssd_ng": a(np.asarray(i["ssd_norm_g"][0])[hs]).reshape(1, HW),
            "mu": a(np.asarray(i["rwkv_mu"][0])[rwc]).reshape(1, RW_L),
            "w0": a(np.asarray(i["rwkv_w0"][0])[hs]).reshape(1, HW),
            "w2l": a(np.asarray(i["rwkv_w2"][0])[:, hs]),
            "a0": a(np.asarray(i["rwkv_a0"][0])[hs]).reshape(1, HW),
            "a2l": a(np.asarray(i["rwkv_a2"][0])[:, hs]),
            "g2l": a(np.asarray(i["rwkv_g2"][0])[:, hs]),
            "k_k": a(np.asarray(i["rwkv_k_k"][0])[hs]).reshape(1, HW),
            "k_a": a(np.asarray(i["rwkv_k_a"][0])[hs]).reshape(1, HW),
            "r_k": a(np.asarray(i["rwkv_r_k"][0]).reshape(-1)[hs]).reshape(1, HW),
            "lnx_g": a(np.asarray(i["rwkv_lnx_g"][0])[hs]).reshape(1, HW),
            "lnx_b": a(np.asarray(i["rwkv_lnx_b"][0])[hs]).reshape(1, HW),
            "w_qkv": a(np.concatenate([Wqkv[:, 0:1024][:, hs], Wqkv[:, 1024:2048][:, hs], Wqkv[:, 2048:3072][:, hs]], axis=1)),
        }
        per_half.append(m)
    maps = []
    for c in range(n_cores):
        b, hh = c // 2, c % 2
        m = dict(shared)
        m.update(per_half[hh])
        xb_ = np.asarray(i["x"][b], dtype=np.float32)
        m["x"] = a(xb_)
        m["x_tail"] = a(xb_[hh * SH:(hh + 1) * SH])
        m["mem"] = a(i["mem"][b])
        maps.append(m)
    return maps


def kernel(**inputs):
    nc = build_program()
    in_maps = make_in_maps(inputs, 8)
    res = run_bass_kernel_spmd(nc, in_maps, core_ids=list(range(8)))
    out = np.empty((4, S, D), dtype=np.float32)
    for c in range(8):
        b, hh = c // 2, c % 2
        out[b, hh * SH:(hh + 1) * SH] = np.asarray(res.results[c]["out"], dtype=np.float32)
    return out
```
